# Optimizing a Trainium2 kernel written in Bass

```python
import math
import jax
import jax.numpy as jnp
from jax import lax
import numpy as np

D_MODEL = 1024
BATCH = 4
SEQ = 4096
DEPTH = 2

RET_HEADS = 8
RET_DK = 64
RET_DV = 128
RET_CHUNK = 128
ROPE_BASE = 10000.0
DIL_HEADS = 8
DIL_DH = 64
DIL_BRANCHES = ((128, 1), (512, 4), (2048, 16))
DIL_BLOCK = 128
HGRN_HEADS = 8
HGRN_DK = 128
HGRN_DV = 128
HGRN_CHUNK = 32
REL_BUCKETS = 32
REL_MAX_DIST = 2048
D_FF = 2816
CONV_WIDTH = 3
EPS = 1e-6

A_QK = RET_HEADS * RET_DK
A_V = RET_HEADS * RET_DV
B_W = DIL_HEADS * DIL_DH
EVEN_IN = 2 * A_QK + 2 * A_V + 3 * B_W
EVEN_OUT = A_V + B_W
EVEN_SPLITS = (A_QK, 2 * A_QK, 2 * A_QK + A_V, 2 * A_QK + 2 * A_V,
               2 * A_QK + 2 * A_V + B_W, 2 * A_QK + 2 * A_V + 2 * B_W)
C_K = HGRN_HEADS * HGRN_DK
C_V = HGRN_HEADS * HGRN_DV
ODD_IN = 2 * C_K + 2 * C_V
ODD_SPLITS = (C_K, 2 * C_K, 2 * C_K + C_V)

kernel_name = "hybrid_retention_dilated_hgrn2_trunk"


def rmsnorm(x, g):
    xf = x.astype(jnp.float32)
    y = xf * lax.rsqrt(jnp.mean(xf * xf, axis=-1, keepdims=True) + EPS)
    return (y * g.astype(jnp.float32)).astype(x.dtype)


def head_layernorm(y):
    mu = jnp.mean(y, axis=-1, keepdims=True)
    yc = y - mu
    return yc * lax.rsqrt(jnp.mean(yc * yc, axis=-1, keepdims=True) + EPS)


def head_rmsnorm(y):
    return y * lax.rsqrt(jnp.mean(y * y, axis=-1, keepdims=True) + EPS)


def rotary(x):
    S, d = x.shape[1], x.shape[-1]
    inv = ROPE_BASE ** (-jnp.arange(0, d, 2, dtype=jnp.float32) / d)
    ang = jnp.arange(S, dtype=jnp.float32)[:, None] * inv[None, :]
    cos = jnp.cos(ang)[None, :, None, :]
    sin = jnp.sin(ang)[None, :, None, :]
    x1, x2 = x[..., : d // 2], x[..., d // 2:]
    return jnp.concatenate([x1 * cos - x2 * sin, x1 * sin + x2 * cos], axis=-1)


def rel_bucket(dist):
    max_exact = REL_BUCKETS // 2
    d = dist.astype(jnp.float32)
    large = max_exact + (jnp.log(jnp.maximum(d, 1.0) / max_exact)
                         / math.log(REL_MAX_DIST / max_exact)
                         * (REL_BUCKETS - max_exact)).astype(jnp.int32)
    large = jnp.minimum(large, REL_BUCKETS - 1)
    return jnp.where(dist < max_exact, dist, large)


def retention(q, k, v):
    Bn, S, H, dk = q.shape
    dv = v.shape[-1]
    C = RET_CHUNK
    N = S // C
    log_g = jnp.log1p(-jnp.exp2(-5.0 - jnp.arange(H, dtype=jnp.float32)))
    pos = jnp.arange(C, dtype=jnp.float32)
    diff = pos[:, None] - pos[None, :]
    decay = jnp.where(diff >= 0, jnp.exp(jnp.maximum(diff, 0.0)[None] * log_g[:, None, None]), 0.0)
    qc = q.reshape(Bn, N, C, H, dk)
    kc = k.reshape(Bn, N, C, H, dk)
    vc = v.reshape(Bn, N, C, H, dv)
    scores = jnp.einsum('bnihd,bnjhd->bnhij', qc, kc) * decay
    intra = jnp.einsum('bnhij,bnjhe->bnihe', scores, vc)
    q_in = qc * jnp.exp((pos + 1.0)[:, None] * log_g[None, :])[:, :, None]
    k_out = kc * jnp.exp((C - 1.0 - pos)[:, None] * log_g[None, :])[:, :, None]
    chunk_decay = jnp.exp(C * log_g)[None, :, None, None]

    def step(state, inp):
        q_i, k_i, v_i = inp
        out = jnp.einsum('bihd,bhde->bihe', q_i, state)
        state = state * chunk_decay + jnp.einsum('bjhd,bjhe->bhde', k_i, v_i)
        return state, out

    xs = (jnp.moveaxis(q_in, 1, 0), jnp.moveaxis(k_out, 1, 0), jnp.moveaxis(vc, 1, 0))
    _, inter = lax.scan(step, jnp.zeros((Bn, H, dk, dv), jnp.float32), xs)
    return (intra + jnp.moveaxis(inter, 0, 1)).reshape(Bn, S, H, dv)


def dilated_branch(q, k, v, rel_bias, window, dilation):
    Bn, S, H, dh = q.shape
    W = window // dilation
    Lb = DIL_BLOCK
    n_prev = -(-W // Lb)
    seg = dilation * Lb
    Sp = -(-S // seg) * seg
    L = Sp // dilation
    nb = L // Lb

    def to_blocks(a):
        a = jnp.pad(a, ((0, 0), (0, Sp - S), (0, 0), (0, 0)))
        a = a.reshape(Bn, L, dilation, H, dh).transpose(0, 2, 1, 3, 4)
        return a.reshape(Bn, dilation, nb, Lb, H, dh)

    def with_history(a):
        ap = jnp.pad(a, ((0, 0), (0, 0), (n_prev, 0), (0, 0), (0, 0), (0, 0)))
        return jnp.concatenate([ap[:, :, j:j + nb] for j in range(n_prev + 1)], axis=3)

    qb = to_blocks(q)
    kh = with_history(to_blocks(k))
    vh = with_history(to_blocks(v))
    Kc = (n_prev + 1) * Lb
    a_idx = jnp.arange(Lb)[:, None]
    c_idx = jnp.arange(Kc)[None, :]
    dist = n_prev * Lb + a_idx - c_idx
    key_pos = (jnp.arange(nb)[:, None, None] - n_prev) * Lb + c_idx[None]
    valid = (dist >= 0)[None] & (dist <= W)[None] & (key_pos >= 0)
    bias = rel_bias[rel_bucket(jnp.maximum(dist, 0) * dilation)].transpose(2, 0, 1)
    s = jnp.einsum('brnahd,brnchd->brnhac', qb, kh) + bias.astype(jnp.float32)
    s = jnp.where(valid[None, None, :, None], s, -jnp.inf)
    m = jnp.max(s, axis=-1, keepdims=True)
    p = jnp.exp(s - m)
    den = jnp.sum(p, axis=-1)
    o = jnp.einsum('brnhac,brnchd->brnahd', p, vh) / jnp.swapaxes(den, 3, 4)[..., None]
    lse = jnp.swapaxes(m[..., 0] + jnp.log(den), 3, 4)

    def from_blocks(a):
        a = a.reshape(Bn, dilation, L, *a.shape[4:])
        a = jnp.moveaxis(a, 1, 2).reshape(Bn, Sp, *a.shape[3:])
        return a[:, :S]

    return from_blocks(o), from_blocks(lse)


def dilated_attention(q, k, v, rel_bias):
    outs, lses = [], []
    for window, dilation in DIL_BRANCHES:
        o, lse = dilated_branch(q, k, v, rel_bias, window, dilation)
        outs.append(o)
        lses.append(lse)
    w = jax.nn.softmax(jnp.stack(lses, axis=0), axis=0)
    return jnp.sum(w[..., None] * jnp.stack(outs, axis=0), axis=0)


def even_mixer(hn, w_in, w_out, ret_norm, rel_bias):
    Bn, S, _ = hn.shape
    proj = (hn @ w_in).astype(jnp.float32)
    qa, ka, va, ga, qb, kb, vb = jnp.split(proj, EVEN_SPLITS, axis=-1)
    qa = rotary(qa.reshape(Bn, S, RET_HEADS, RET_DK))
    ka = rotary(ka.reshape(Bn, S, RET_HEADS, RET_DK)) * (RET_DK ** -0.5)
    ya = retention(qa, ka, va.reshape(Bn, S, RET_HEADS, RET_DV))
    ya = head_layernorm(ya).reshape(Bn, S, A_V) * ret_norm.astype(jnp.float32) * jax.nn.silu(ga)
    yb = dilated_attention(qb.reshape(Bn, S, DIL_HEADS, DIL_DH) * (DIL_DH ** -0.5),
                           kb.reshape(Bn, S, DIL_HEADS, DIL_DH),
                           vb.reshape(Bn, S, DIL_HEADS, DIL_DH), rel_bias).reshape(Bn, S, B_W)
    y = jnp.concatenate([ya, yb], axis=-1).astype(hn.dtype)
    return y @ w_out


def hgrn2(q, f_gate, i, Bn, S):
    H, dk, dv = HGRN_HEADS, HGRN_DK, HGRN_DV
    C = HGRN_CHUNK
    N = S // C
    k = 1.0 - f_gate
    log_f = jnp.log(f_gate)
    qc = q.reshape(Bn, N, C, H, dk)
    kc = k.reshape(Bn, N, C, H, dk)
    vc = i.reshape(Bn, N, C, H, dv)
    b = jnp.cumsum(log_f.reshape(Bn, N, C, H, dk), axis=2)
    b_last = b[:, :, -1:]
    q_t = qc * jnp.exp(b)
    k_t = kc * jnp.exp(-b)
    mask = jnp.tril(jnp.ones((C, C), dtype=bool))
    scores = jnp.where(mask, jnp.einsum('bnihd,bnjhd->bnhij', q_t, k_t), 0.0)
    intra = jnp.einsum('bnhij,bnjhe->bnihe', scores, vc)
    k_end = kc * jnp.exp(b_last - b)
    chunk_decay = jnp.exp(b_last[:, :, 0])

    def step(state, inp):
        q_i, k_i, v_i, d_i = inp
        out = jnp.einsum('bihd,bhde->bihe', q_i, state)
        state = state * d_i[..., None] + jnp.einsum('bjhd,bjhe->bhde', k_i, v_i)
        return state, out

    xs = (jnp.moveaxis(q_t, 1, 0), jnp.moveaxis(k_end, 1, 0), jnp.moveaxis(vc, 1, 0),
          jnp.moveaxis(chunk_decay, 1, 0))
    _, inter = lax.scan(step, jnp.zeros((Bn, H, dk, dv), jnp.float32), xs)
    return (intra + jnp.moveaxis(inter, 0, 1)).reshape(Bn, S, H, dv)


def odd_mixer(hn, w_in, w_out, out_norm, lb):
    Bn, S, _ = hn.shape
    proj = (hn @ w_in).astype(jnp.float32)
    q, f, i, g = jnp.split(proj, ODD_SPLITS, axis=-1)
    q = jax.nn.silu(q)
    f_gate = lb + (1.0 - lb) * jax.nn.sigmoid(f)
    shp_k = (Bn, S, HGRN_HEADS, HGRN_DK)
    o = hgrn2(q.reshape(shp_k), f_gate.reshape(shp_k), i.reshape(Bn, S, HGRN_HEADS, HGRN_DV), Bn, S)
    o = head_rmsnorm(o).reshape(Bn, S, C_V) * out_norm.astype(jnp.float32) * jax.nn.silu(g)
    return o.astype(hn.dtype) @ w_out


def conv_ffn(hn, w_up, conv_w, conv_b, w_down):
    u = hn @ w_up
    ch = u.shape[-1]
    u = lax.conv_general_dilated(u, conv_w[:, None, :].astype(u.dtype), window_strides=(1,),
                                 padding=[(CONV_WIDTH - 1, 0)],
                                 dimension_numbers=('NWC', 'WIO', 'NWC'),
                                 feature_group_count=ch) + conv_b
    a, b = jnp.split(u, 2, axis=-1)
    return (jax.nn.silu(a) * b) @ w_down


def setup_inputs(seed: int = 0) -> dict:
    key = jax.random.key(seed)
    ks = jax.random.split(key, 17)
    f32 = jnp.float32
    n_even = (DEPTH + 1) // 2
    n_odd = DEPTH // 2

    def nrm(k, shape, scale):
        return jax.random.normal(k, shape, f32) * scale

    return {
        "x": nrm(ks[0], (BATCH, SEQ, D_MODEL), 1.0),
        "even_w_in": nrm(ks[1], (n_even, D_MODEL, EVEN_IN), D_MODEL ** -0.5),
        "even_w_out": nrm(ks[2], (n_even, EVEN_OUT, D_MODEL), EVEN_OUT ** -0.5),
        "ret_norm": 1.0 + nrm(ks[3], (n_even, A_V), 0.02),
        "rel_bias": nrm(ks[4], (REL_BUCKETS, DIL_HEADS), 0.5),
        "odd_w_in": nrm(ks[5], (n_odd, D_MODEL, ODD_IN), D_MODEL ** -0.5),
        "odd_w_out": nrm(ks[6], (n_odd, C_V, D_MODEL), C_V ** -0.5),
        "hgrn_lb": nrm(ks[7], (DEPTH, C_K), 0.1),
        "hgrn_norm": 1.0 + nrm(ks[8], (n_odd, C_V), 0.02),
        "mix_norm": 1.0 + nrm(ks[9], (DEPTH, D_MODEL), 0.02),
        "ffn_norm": 1.0 + nrm(ks[10], (DEPTH, D_MODEL), 0.02),
        "ffn_w_up": nrm(ks[11], (DEPTH, D_MODEL, 2 * D_FF), D_MODEL ** -0.5),
        "ffn_conv_w": nrm(ks[12], (DEPTH, CONV_WIDTH, 2 * D_FF), CONV_WIDTH ** -0.5),
        "ffn_conv_b": nrm(ks[13], (DEPTH, 2 * D_FF), 0.01),
        "ffn_w_down": nrm(ks[14], (DEPTH, D_FF, D_MODEL), D_FF ** -0.5),
        "final_norm": 1.0 + nrm(ks[15], (D_MODEL,), 0.02),
    }


def reference(x, even_w_in, even_w_out, ret_norm, rel_bias, odd_w_in, odd_w_out, hgrn_lb,
              hgrn_norm, mix_norm, ffn_norm, ffn_w_up, ffn_conv_w, ffn_conv_b, ffn_w_down,
              final_norm):
    sm = jax.nn.softmax(hgrn_lb.astype(jnp.float32), axis=0)
    lower_bounds = jnp.cumsum(sm, axis=0) - sm[0]
    h = x
    for l in range(DEPTH):
        hn = rmsnorm(h, mix_norm[l])
        if l % 2 == 0:
            e = l // 2
            h = h + even_mixer(hn, even_w_in[e], even_w_out[e], ret_norm[e], rel_bias)
        else:
            o = l // 2
            h = h + odd_mixer(hn, odd_w_in[o], odd_w_out[o], hgrn_norm[o], lower_bounds[l])
        hn = rmsnorm(h, ffn_norm[l])
        h = h + conv_ffn(hn, ffn_w_up[l], ffn_conv_w[l], ffn_conv_b[l], ffn_w_down[l])
    return rmsnorm(h, final_norm)
```

```python
import math
import numpy as np
import ml_dtypes
from contextlib import ExitStack
import concourse.bass as bass
import concourse.mybir as mybir
from concourse.bass_utils import run_bass_kernel_spmd

F32 = mybir.dt.float32
BF16 = mybir.dt.bfloat16
AF = mybir.ActivationFunctionType
ALU = mybir.AluOpType
AX = mybir.AxisListType

SEQ = 4096
DM = 1024
DFF = 2816
EPS = 1e-6
STOP_AFTER = None


class Sched:
    CE = ('pe', 'act', 'dve', 'pool')
    ALLQ = ('pe', 'act', 'dve', 'pool', 'sp')

    def __init__(self, nc, stack, ndma=24):
        self.nc = nc
        self.sem = {e: stack.enter_context(nc.semaphore('s_' + e)) for e in self.CE}
        self.cnt = {e: 0 for e in self.CE}
        self.dsem = [stack.enter_context(nc.semaphore('d%d' % i)) for i in range(ndma)]
        self.dcnt = [0] * ndma
        self.dnext = 0
        self.waited = {}
        self.lastw = {}
        self.readers = {}
        self.q = {e: [] for e in self.ALLQ}
        self.ninst = {e: 0 for e in self.ALLQ}

    def _semof(self, key):
        return self.sem[key[1]] if key[0] == 'c' else self.dsem[key[1]]

    def _deps(self, eng, reads, writes):
        raw = set()
        toks = set()
        for k in reads:
            t = self.lastw.get(k)
            if t is not None:
                raw.add(t)
        for k in writes:
            t = self.lastw.get(k)
            if t is not None:
                toks.add(t)
            r = self.readers.get(k)
            if r:
                toks.update(r.values())
        waits = {}
        for t in raw | toks:
            kind, who, val = t
            if kind == 'c' and who == eng and t not in raw:
                continue
            key = (kind, who)
            if self.waited.get((eng,) + key, 0) >= val:
                continue
            if waits.get(key, 0) < val:
                waits[key] = val
        for key, val in waits.items():
            self.waited[(eng,) + key] = val
        return waits

    def _record(self, tok, reads, writes):
        for k in writes:
            self.lastw[k] = tok
            self.readers[k] = {}
        for k in reads:
            d = self.readers.setdefault(k, {})
            if tok[0] == 'c':
                d[('c', tok[1])] = tok
            else:
                d[tok] = tok

    def op(self, eng, fn, reads=(), writes=()):
        return self.ops(eng, [fn], reads, writes)

    def ops(self, eng, fns, reads=(), writes=()):
        waits = self._deps(eng, reads, writes)
        self.cnt[eng] += 1
        tok = ('c', eng, self.cnt[eng])
        sem = self.sem[eng]
        wl = [(self._semof(k), v) for k, v in waits.items()]

        def run(e, fns=fns, wl=wl, sem=sem):
            for s, v in wl:
                e.wait_ge(s, v)
            for f in fns[:-1]:
                f(e)
            fns[-1](e).then_inc(sem, 1)
        self.q[eng].append(run)
        self.ninst[eng] += len(fns) + len(wl)
        self._record(tok, reads, writes)
        return tok

    def dma(self, qeng, out, in_, reads=(), writes=()):
        i = self.dnext
        self.dnext = (i + 1) % len(self.dsem)
        waits = self._deps(qeng, reads, writes)
        prev = self.dcnt[i]
        if prev and self.waited.get((qeng, 'd', i), 0) < prev:
            waits[('d', i)] = prev
            self.waited[(qeng, 'd', i)] = prev
        self.dcnt[i] += 16
        tok = ('d', i, self.dcnt[i])
        sem = self.dsem[i]
        wl = [(self._semof(k), v) for k, v in waits.items()]

        def run(e, wl=wl, sem=sem, out=out, in_=in_):
            for s, v in wl:
                e.wait_ge(s, v)
            e.dma_start(out=out, in_=in_).then_inc(sem, 16)
        self.q[qeng].append(run)
        self.ninst[qeng] += 1 + len(wl)
        self._record(tok, reads, writes)
        return tok

    def barrier(self):
        for eng in self.ALLQ:
            wl = []
            for e in self.CE:
                if e != eng and self.cnt[e] > self.waited.get((eng, 'c', e), 0):
                    wl.append((self.sem[e], self.cnt[e]))
                    self.waited[(eng, 'c', e)] = self.cnt[e]
            for i, c in enumerate(self.dcnt):
                if c > self.waited.get((eng, 'd', i), 0):
                    wl.append((self.dsem[i], c))
                    self.waited[(eng, 'd', i)] = c

            def run(e, wl=wl):
                for s, v in wl:
                    e.wait_ge(s, v)
            self.q[eng].append(run)
        self.lastw.clear()
        self.readers.clear()

    def flush(self):
        nc = self.nc
        q = self.q
        with nc.Block() as block:
            @block.tensor
            def _(e):
                for f in q['pe']:
                    f(e)

            @block.scalar
            def _(e):
                for f in q['act']:
                    f(e)

            @block.vector
            def _(e):
                for f in q['dve']:
                    f(e)

            @block.gpsimd
            def _(e):
                for f in q['pool']:
                    f(e)

            @block.sync
            def _(e):
                for f in q['sp']:
                    f(e)
        self.q = {e: [] for e in self.ALLQ}

    def sync(self):
        self.barrier()
        self.flush()


class Ctx:
    debug = False

    def tap(self, name, ap, shape, dt, keys):
        if not self.debug:
            return
        d = self.nc.dram_tensor("tap_" + name, list(shape), dt, kind="ExternalOutput").ap()
        self.S.dma('sp', d, ap, reads=list(keys))


_UID = [0]


def uniq(n):
    _UID[0] += 1
    return '%s_%d' % (n, _UID[0])


def I(name, *args, **kw):
    return lambda e: getattr(e, name)(*args, **kw)


def mm_group(S, ps_ap, pskey, pairs, reads):
    n = len(pairs)
    fns = [(I('matmul', ps_ap, lhsT=l, rhs=r, start=(i == 0), stop=(i == n - 1)))
           for i, (l, r) in enumerate(pairs)]
    S.ops('pe', fns, reads=reads, writes=[pskey])


def norm_pass(C, src, gidx, hnT, tok0, ntile):
    nc, S = C.nc, C.S
    srcv = src.rearrange("(c p) t -> p c t", p=128)
    with ExitStack() as st:
        T = lambda n, s, d=F32: st.enter_context(nc.sbuf_tensor(uniq(n), s, d))
        ht = [T("n_ht%d" % i, [128, 8, 512]) for i in range(2)]
        sq = [T("n_sq%d" % i, [128, 8, 512], BF16) for i in range(2)]
        rs = [T("n_rs%d" % i, [128, 512]) for i in range(2)]
        pss = [st.enter_context(nc.psum_tensor(uniq("n_ps%d" % i), [128, 512], F32)) for i in range(2)]
        for i in range(ntile):
            b = i % 2
            t0 = tok0 + i * 512
            S.dma('sp', ht[b][:], srcv[:, :, t0:t0 + 512], writes=[('ht', b)])
            S.op('act', I('activation', sq[b][:], ht[b][:], AF.Square), reads=[('ht', b)], writes=[('sq', b)])
            mm_group(S, pss[b][:], ('nps', b), [(C.ones[:], sq[b][:, c, :]) for c in range(8)], [('sq', b)])
            S.op('act', I('activation', rs[b][:], pss[b][:], AF.Sqrt, scale=1.0 / DM, bias=C.epsc[:, 0:1]),
                 reads=[('nps', b)], writes=[('rs', b)])
            S.op('dve', I('reciprocal', rs[b][:], rs[b][:]), reads=[('rs', b)], writes=[('rs', b)])
            for c in range(8):
                eng = 'dve'
                S.op(eng, I('scalar_tensor_tensor',
                    hnT[:, c, i * 512:(i + 1) * 512], in0=ht[b][:, c, :], scalar=C.gains[:, gidx, c:c + 1],
                    in1=rs[b][:], op0=ALU.mult, op1=ALU.mult), reads=[('ht', b), ('rs', b)], writes=[('hn', i, c)])
        S.sync()


def load_consts(C, st):
    nc, S = C.nc, C.S
    T = lambda n, s, d=F32: st.enter_context(nc.sbuf_tensor(uniq(n), s, d))
    C.ones = T("c_ones", [128, 128], BF16)
    C.ident = T("c_ident", [128, 128], BF16)
    C.gains = T("c_gains", [128, 5, 8])
    C.epsc = T("c_eps", [128, 1])
    S.dma('sp', C.ones[:], C.din['ones'], writes=['c1'])
    S.dma('sp', C.ident[:], C.din['ident'], writes=['c2'])
    S.dma('sp', C.gains[:], C.din['gains'], writes=['c3'])
    S.op('dve', I('memset', C.epsc[:], EPS), writes=['c4'])
    S.sync()


def proj_phase(C, hnT, w_ap, jobs, ntok=SEQ):
    nc, S = C.nc, C.S
    wv = w_ap.rearrange("(c p) n -> p c n", p=128)
    with ExitStack() as st:
        T = lambda n, s, d=F32: st.enter_context(nc.sbuf_tensor(uniq(n), s, d))
        wf = [T("p_wf%d" % i, [128, 8, 512]) for i in range(2)]
        wb = [T("p_wb%d" % i, [128, 8, 512], BF16) for i in range(2)]
        wsw = [T("p_ws%d" % i, [128, 8, 512], BF16) for i in range(2)]
        psum = [st.enter_context(nc.psum_tensor(uniq("p_ps%d" % i), [128, 512], F32)) for i in range(6)]
        C.pp = 0

        def next_ps():
            i = C.pp % 6
            C.pp += 1
            return psum[i], ('pps', i)
        ntt = ntok // 512
        for ji, job in enumerate(jobs):
            b = ji % 2
            c0 = job['c0']
            S.dma('sp', wf[b][:], wv[:, :, c0:c0 + 512], writes=[('wf', b)])
            S.op('pool', I('tensor_copy', wb[b][:], wf[b][:]), reads=[('wf', b)], writes=[('wb', b)])
            role = job['role']
            if role == 'rot':
                src = wf[b][:].rearrange("p c (h two j) -> p c h two j", two=2, j=32)
                dst = wsw[b][:].rearrange("p c (h two j) -> p c h two j", two=2, j=32)
                for c in range(8):
                    S.op('act', I('copy', dst[:, c, :, 0, :], src[:, c, :, 1, :]),
                         reads=[('wf', b)], writes=[('wsa', b, c)])
                    S.op('act', I('copy', dst[:, c, :, 1, :], src[:, c, :, 0, :]),
                         reads=[('wf', b)], writes=[('wsb', b, c)])
            if role in ('fm', 'both', 'rot'):
                for sub in range(4):
                    for tt in range(ntt):
                        ps, pk = next_ps()
                        mm_group(S, ps[:], pk, [(wb[b][:, c, sub * 128:(sub + 1) * 128], hnT[:, c, tt * 512:(tt + 1) * 512])
                                                for c in range(8)], [('wb', b)])
                        if role == 'rot':
                            ps2, pk2 = next_ps()
                            mm_group(S, ps2[:], pk2, [(wsw[b][:, c, sub * 128:(sub + 1) * 128], hnT[:, c, tt * 512:(tt + 1) * 512])
                                                      for c in range(8)],
                                     [('wsa', b, c) for c in range(8)] + [('wsb', b, c) for c in range(8)])
                            job['epi_fm'](sub, tt, ps, pk, ps2, pk2)
                        else:
                            job['epi_fm'](sub, tt, ps, pk)
            if role in ('tm', 'both'):
                for t in range(ntok // 128):
                    ps, pk = next_ps()
                    mm_group(S, ps[:], pk, [(hnT[:, c, t * 128:(t + 1) * 128], wb[b][:, c, :]) for c in range(8)], [('wb', b)])
                    job['epi_tm'](t, ps, pk)
        S.sync()


class Stager:
    def __init__(self, C, st, name, shape, dt, n=4):
        self.C = C
        self.bufs = [st.enter_context(C.nc.sbuf_tensor(uniq("%s%d" % (name, i)), shape, dt)) for i in range(n)]
        self.name = name
        self.i = 0

    def next(self):
        i = self.i % len(self.bufs)
        self.i += 1
        return self.bufs[i], (self.name, i)

    def store(self, dst_ap, buf_ap, key, wkeys=()):
        self.C.S.dma('pool', dst_ap, buf_ap, reads=[key], writes=list(wkeys))


def phase_A0(C):
    nc, S, D = C.nc, C.S, C.dsc
    with ExitStack() as st:
        T = lambda n, s, d=F32: st.enter_context(nc.sbuf_tensor(uniq(n), s, d))
        hnT = T("a0_hn", [128, 8, SEQ], BF16)
        norm_pass(C, C.din['xT'], 0, hnT, 0, 8)
        rot = T("a0_rot", [128, 2, SEQ])
        retn = T("a0_retn", [128, 1024])
        S.dma('sp', rot[:], C.din['rot'], writes=['rot'])
        S.dma('sp', retn[:], C.din['retn'], writes=['retn'])
        sb = Stager(C, st, "a0_sb", [128, 512], BF16, 4)
        t1 = [T("a0_t1%d" % i, [128, 512]) for i in range(2)]
        t2 = [T("a0_t2%d" % i, [128, 512]) for i in range(2)]
        cnt = [0]

        def epi_rot(dst, kscale):
            def f(sub, tt, ps, pk, ps2, pk2, dst=dst, kscale=kscale):
                i = cnt[0] % 2
                cnt[0] += 1
                tsl = slice(tt * 512, (tt + 1) * 512)
                S.op('dve', I('scalar_tensor_tensor', t1[i][:], in0=ps[:], scalar=kscale, in1=rot[:, 0, tsl],
                                                             op0=ALU.mult, op1=ALU.mult), reads=[pk, 'rot'], writes=[('t1', i)])
                S.op('dve', I('scalar_tensor_tensor', t2[i][:], in0=ps2[:], scalar=kscale, in1=rot[:, 1, tsl],
                                                             op0=ALU.mult, op1=ALU.mult), reads=[pk2, 'rot'], writes=[('t2', i)])
                buf, bk = sb.next()
                S.op('dve', I('tensor_tensor', buf[:], t1[i][:], t2[i][:], op=ALU.add),
                     reads=[('t1', i), ('t2', i)], writes=[bk])
                sb.store(dst[sub * 128:(sub + 1) * 128, tsl], buf[:], bk)
            return f

        def epi_fm_scale(dst, scale):
            def f(sub, tt, ps, pk, dst=dst, scale=scale):
                buf, bk = sb.next()
                S.op('act', I('activation', buf[:], ps[:], AF.Identity, scale=scale), reads=[pk], writes=[bk])
                sb.store(dst[sub * 128:(sub + 1) * 128, tt * 512:(tt + 1) * 512], buf[:], bk)
            return f

        def epi_tm_copy(dst, cb):
            def f(t, ps, pk, dst=dst, cb=cb):
                buf, bk = sb.next()
                S.op('act', I('copy', buf[:], ps[:]), reads=[pk], writes=[bk])
                sb.store(dst[t * 128:(t + 1) * 128, cb * 512:(cb + 1) * 512], buf[:], bk)
            return f

        def epi_tm_gate(dst, cb):
            def f(t, ps, pk, dst=dst, cb=cb):
                i = cnt[0] % 2
                cnt[0] += 1
                S.op('act', I('activation', t1[i][:], ps[:], AF.Silu), reads=[pk], writes=[('t1', i)])
                buf, bk = sb.next()
                S.op('dve', I('tensor_tensor', buf[:], t1[i][:], retn[:, cb * 512:(cb + 1) * 512], op=ALU.mult),
                     reads=[('t1', i), 'retn'], writes=[bk])
                sb.store(dst[t * 128:(t + 1) * 128, cb * 512:(cb + 1) * 512], buf[:], bk)
            return f
        jobs = [dict(c0=0, role='rot', epi_fm=epi_rot(D['qaT'], 1.0)),
                dict(c0=512, role='rot', epi_fm=epi_rot(D['kaT'], 0.125)),
                dict(c0=1024, role='tm', epi_tm=epi_tm_copy(D['va'], 0)),
                dict(c0=1536, role='tm', epi_tm=epi_tm_copy(D['va'], 1)),
                dict(c0=2048, role='tm', epi_tm=epi_tm_gate(D['ga'], 0)),
                dict(c0=2560, role='tm', epi_tm=epi_tm_gate(D['ga'], 1)),
                dict(c0=3072, role='fm', epi_fm=epi_fm_scale(D['qbT'], 0.125)),
                dict(c0=3584, role='fm', epi_fm=epi_fm_scale(D['kbT'], 1.0)),
                dict(c0=4096, role='tm', epi_tm=epi_tm_copy(D['vb'], 0))]
        if JOBSEL is not None:
            jobs = [jobs[i] for i in JOBSEL]
        proj_phase(C, hnT, C.din['w_in0'], jobs)


def phase_outproj(C, yT_d, KC, w_ap, h_src, h_dst):
    nc, S = C.nc, C.S
    wv = w_ap.rearrange("(c p) n -> p c n", p=128)
    with ExitStack() as st:
        T = lambda n, s, d=F32: st.enter_context(nc.sbuf_tensor(uniq(n), s, d))
        yT = T("o_y", [128, KC, SEQ], BF16)
        yv = yT_d.rearrange("(c p) t -> p c t", p=128)
        for c in range(KC):
            S.dma('sp', yT[:, c, :], yv[:, c, :], writes=[('y', c)])
        wf = [T("o_wf%d" % i, [128, KC, 128]) for i in range(2)]
        wb = [T("o_wb%d" % i, [128, KC, 128], BF16) for i in range(2)]
        hb = [T("o_h%d" % i, [128, 512]) for i in range(3)]
        psum = [st.enter_context(nc.psum_tensor(uniq("o_ps%d" % i), [128, 512], F32)) for i in range(4)]
        k = 0
        for dmb in range(8):
            b = dmb % 2
            S.dma('sp', wf[b][:], wv[:, :, dmb * 128:(dmb + 1) * 128], writes=[('wf', b)])
            S.op('pool', I('tensor_copy', wb[b][:], wf[b][:]), reads=[('wf', b)], writes=[('wb', b)])
            for tt in range(8):
                pi = k % 4
                hi = k % 3
                k += 1
                tsl = slice(tt * 512, (tt + 1) * 512)
                rsl = slice(dmb * 128, (dmb + 1) * 128)
                S.dma('sp', hb[hi][:], h_src[rsl, tsl], reads=[('hd', dmb, tt)], writes=[('hb', hi)])
                mm_group(S, psum[pi][:], ('ops', pi), [(wb[b][:, c, :], yT[:, c, tsl]) for c in range(KC)],
                         [('wb', b)] + [('y', c) for c in range(KC)])
                S.op('dve', I('tensor_tensor', hb[hi][:], psum[pi][:], hb[hi][:], op=ALU.add),
                     reads=[('ops', pi), ('hb', hi)], writes=[('hb', hi)])
                S.dma('pool', h_dst[rsl, tsl], hb[hi][:], reads=[('hb', hi)], writes=[('hd', dmb, tt)])
        S.sync()


def phase_ffn(C, layer, hT):
    nc, S = C.nc, C.S
    ST = 2048
    NJ = DFF // 128
    wup = C.din['w_up'][layer].rearrange("(c p) n -> p c n", p=128)
    wdn = C.din['w_down'][layer].rearrange("(j p) n -> p j n", p=128)
    with ExitStack() as st0:
        T0 = lambda n, s, d=F32: st0.enter_context(nc.sbuf_tensor(uniq(n), s, d))
        m = T0("f_m", [128, NJ, ST], BF16)
        halo = T0("f_halo", [128, NJ, 2, 2])
        cp = T0("f_cp", [128, 4, 44])
        S.dma('sp', cp[:], C.din['convp'][:, layer], writes=['cp'])
        S.op('dve', I('memset', halo[:], 0.0), writes=['halo'])
        S.sync()
        for sti in range(SEQ // ST):
            tok0 = sti * ST
            with ExitStack() as st:
                T = lambda n, s, d=F32: st.enter_context(nc.sbuf_tensor(uniq(n), s, d))
                hnT = T("f_hn", [128, 8, ST], BF16)
                norm_pass(C, hT, 1 + 2 * layer, hnT, tok0, ST // 512)
                u = [T("f_u%d" % i, [128, ST + 2]) for i in range(2)]
                cc = [T("f_c%d" % i, [128, ST]) for i in range(2)]
                wf = [T("f_wf%d" % i, [128, 8, 2, 128]) for i in range(2)]
                wb = [T("f_wb%d" % i, [128, 8, 2, 128], BF16) for i in range(2)]
                psum = [st.enter_context(nc.psum_tensor(uniq("f_ps%d" % i), [128, 512], F32)) for i in range(8)]
                for j in range(NJ):
                    b = j % 2
                    for ab in range(2):
                        c0 = ab * DFF + j * 128
                        S.dma('sp', wf[b][:, :, ab, :], wup[:, :, c0:c0 + 128], writes=[('wf', b, ab)])
                    S.op('pool', I('tensor_copy', wb[b][:], wf[b][:]), reads=[('wf', b, 0), ('wf', b, 1)],
                         writes=[('wb', b)])
                    for ab in range(2):
                        S.op('pool', I('tensor_copy', u[ab][:, 0:2], halo[:, j, ab, :]),
                             reads=['halo', ('hl', j, ab)], writes=[('u', ab)])
                        for tt in range(ST // 512):
                            pi = ab * 4 + tt
                            mm_group(S, psum[pi][:], ('fps', pi),
                                     [(wb[b][:, c, ab, :], hnT[:, c, tt * 512:(tt + 1) * 512]) for c in range(8)], [('wb', b)])
                            S.op('act', I('copy', u[ab][:, 2 + tt * 512:2 + (tt + 1) * 512], psum[pi][:]),
                                 reads=[('fps', pi)], writes=[('u', ab, tt)])
                        ukeys = [('u', ab)] + [('u', ab, tt) for tt in range(ST // 512)]
                        jb = ab * NJ + j
                        S.op('pool', I('tensor_copy', halo[:, j, ab, :], u[ab][:, ST:ST + 2]),
                             reads=ukeys, writes=[('hl', j, ab)])
                        S.op('act', I('activation', cc[ab][:], u[ab][:, 2:ST + 2], AF.Identity,
                                                                         scale=cp[:, 2, jb:jb + 1], bias=cp[:, 3, jb:jb + 1]),
                             reads=ukeys + ['cp'], writes=[('cc', ab)])
                        eng = 'dve'
                        S.op(eng, I('scalar_tensor_tensor', cc[ab][:], in0=u[ab][:, 1:ST + 1], scalar=cp[:, 1, jb:jb + 1],
                                                                                 in1=cc[ab][:], op0=ALU.mult, op1=ALU.add),
                             reads=ukeys + [('cc', ab), 'cp'], writes=[('cc', ab)])
                        S.op(eng, I('scalar_tensor_tensor', cc[ab][:], in0=u[ab][:, 0:ST], scalar=cp[:, 0, jb:jb + 1],
                                                                                 in1=cc[ab][:], op0=ALU.mult, op1=ALU.add),
                             reads=ukeys + [('cc', ab), 'cp'], writes=[('cc', ab)])
                    S.op('act', I('activation', cc[0][:], cc[0][:], AF.Silu), reads=[('cc', 0)], writes=[('cc', 0)])
                    S.op('dve', I('tensor_tensor', m[:, j, :], cc[0][:], cc[1][:], op=ALU.mult),
                         reads=[('cc', 0), ('cc', 1)], writes=[('m', j)])
                S.sync()
            with ExitStack() as st:
                T = lambda n, s, d=F32: st.enter_context(nc.sbuf_tensor(uniq(n), s, d))
                wf = [T("g_wf%d" % i, [128, NJ, 128]) for i in range(2)]
                wb = [T("g_wb%d" % i, [128, NJ, 128], BF16) for i in range(2)]
                hb = [T("g_h%d" % i, [128, 512]) for i in range(3)]
                psum = [st.enter_context(nc.psum_tensor(uniq("g_ps%d" % i), [128, 512], F32)) for i in range(4)]
                k = 0
                for dmb in range(8):
                    b = dmb % 2
                    S.dma('sp', wf[b][:, 0:11, :], wdn[:, 0:11, dmb * 128:(dmb + 1) * 128], writes=[('wf', b, 0)])
                    S.dma('sp', wf[b][:, 11:22, :], wdn[:, 11:22, dmb * 128:(dmb + 1) * 128], writes=[('wf', b, 1)])
                    S.op('pool', I('tensor_copy', wb[b][:], wf[b][:]), reads=[('wf', b, 0), ('wf', b, 1)], writes=[('wb', b)])
                    for tt in range(ST // 512):
                        pi = k % 4
                        hi = k % 3
                        k += 1
                        tsl = slice(tok0 + tt * 512, tok0 + (tt + 1) * 512)
                        rsl = slice(dmb * 128, (dmb + 1) * 128)
                        S.dma('sp', hb[hi][:], hT[rsl, tsl], writes=[('hb', hi)])
                        mm_group(S, psum[pi][:], ('gps', pi), [(wb[b][:, j, :], m[:, j, tt * 512:(tt + 1) * 512]) for j in range(NJ)],
                                 [('wb', b)])
                        S.op('dve', I('tensor_tensor', hb[hi][:], psum[pi][:], hb[hi][:], op=ALU.add),
                             reads=[('gps', pi), ('hb', hi)], writes=[('hb', hi)])
                        S.dma('pool', hT[rsl, tsl], hb[hi][:], reads=[('hb', hi)])
                S.sync()


def phase_final(C, hT, outT):
    nc, S = C.nc, C.S
    with ExitStack() as st:
        T = lambda n, s, d=F32: st.enter_context(nc.sbuf_tensor(uniq(n), s, d))
        ht = [T("z_ht%d" % i, [128, 8, 512]) for i in range(2)]
        sq = [T("z_sq%d" % i, [128, 8, 512], BF16) for i in range(2)]
        rs = [T("z_rs%d" % i, [128, 512]) for i in range(2)]
        pss = [st.enter_context(nc.psum_tensor(uniq("z_ps%d" % i), [128, 512], F32)) for i in range(2)]
        srcv = hT.rearrange("(c p) t -> p c t", p=128)
        dstv = outT.rearrange("(c p) t -> p c t", p=128)
        for i in range(8):
            b = i % 2
            tsl = slice(i * 512, (i + 1) * 512)
            hk = [('ht', b, c) for c in range(8)]
            S.dma('sp', ht[b][:], srcv[:, :, tsl], writes=hk)
            S.op('act', I('activation', sq[b][:], ht[b][:], AF.Square), reads=hk, writes=[('sq', b)])
            mm_group(S, pss[b][:], ('nps', b), [(C.ones[:], sq[b][:, c, :]) for c in range(8)], [('sq', b)])
            S.op('act', I('activation', rs[b][:], pss[b][:], AF.Sqrt, scale=1.0 / DM, bias=C.epsc[:, 0:1]),
                 reads=[('nps', b)], writes=[('rs', b)])
            S.op('dve', I('reciprocal', rs[b][:], rs[b][:]), reads=[('rs', b)], writes=[('rs', b)])
            for c in range(8):
                eng = 'dve'
                S.op(eng, I('scalar_tensor_tensor',
                    ht[b][:, c, :], in0=ht[b][:, c, :], scalar=C.gains[:, 4, c:c + 1],
                    in1=rs[b][:], op0=ALU.mult, op1=ALU.mult), reads=[('ht', b, c), ('rs', b), ('sq', b)], writes=[('ht', b, c)])
            S.dma('pool', dstv[:, :, tsl], ht[b][:], reads=hk)
        S.sync()


PHASES = []


def build(debug=False, stop_after=None, feed=(), skip=()):
    nc = bass.Bass("TRN2", target_bir_lowering=False)
    C = Ctx()
    C.nc = nc
    C.debug = debug
    kindS = "ExternalOutput" if debug else "Internal"
    C.din = {}
    C.dsc = {}

    def din(name, shape, dt=F32):
        C.din[name] = nc.dram_tensor(name, shape, dt, kind="ExternalInput").ap()

    def dsc(name, shape, dt=BF16):
        C.dsc[name] = nc.dram_tensor(name, shape, dt, kind=("ExternalInput" if name in feed else kindS)).ap()
    din('xT', [DM, SEQ])
    din('w_in0', [DM, 4608])
    din('w_out0', [1536, DM])
    din('w_in1', [DM, 4096])
    din('w_out1', [DM, DM])
    din('w_up', [2, DM, 2 * DFF])
    din('w_down', [2, DFF, DM])
    din('gains', [128, 5, 8])
    din('convp', [128, 2, 4, 44])
    din('retn', [128, 1024])
    din('rot', [128, 2, SEQ])
    din('ones', [128, 128], BF16)
    din('ident', [128, 128], BF16)
    din('dect', [128, 8, 128])
    din('gq', [128, 8, 128])
    din('gk', [128, 8, 64])
    din('cdr', [128, 4])
    din('biasT', [128, 24, 256])
    din('mask2', [128, 256])
    din('lbrep', [128, 2, 1024])
    din('lbfm', [128, 2, 8])
    din('hgn', [128, 1024])
    din('t1m', [128, 128], BF16)
    din('t2m', [128, 128], BF16)
    for n in ('qaT', 'kaT', 'qbT', 'kbT'):
        dsc(n, [512, SEQ])
    dsc('va', [SEQ, 1024])
    dsc('ga', [SEQ, 1024])
    dsc('vb', [SEQ, 512])
    dsc('yT', [1536, SEQ])
    dsc('hT', [DM, SEQ], F32)
    dsc('q1T', [1024, SEQ])
    dsc('k1T', [1024, SEQ])
    dsc('lfh', [SEQ, 1024])
    dsc('lfl', [SEQ, 1024])
    dsc('k1', [SEQ, 1024])
    dsc('v1', [SEQ, 1024])
    dsc('g1', [SEQ, 1024])
    dsc('y1T', [1024, SEQ])
    if debug:
        dsc('dbg_ret', [SEQ, 1024], F32)
    outT = nc.dram_tensor('outT', [DM, SEQ], F32, kind="ExternalOutput").ap()
    with ExitStack() as st:
        C.S = Sched(nc, st)
        load_consts(C, st)
        plan = [
            ('A0', lambda: phase_A0(C)),
            ('B0', lambda: phase_B0(C)),
            ('C0', lambda: phase_C0(C)),
            ('D0', lambda: phase_outproj(C, C.dsc['yT'], 12, C.din['w_out0'], C.din['xT'], C.dsc['hT'])),
            ('E0', lambda: phase_ffn(C, 0, C.dsc['hT'])),
            ('A1', lambda: phase_A1(C, C.dsc['hT'])),
            ('B1', lambda: phase_B1(C)),
            ('D1', lambda: phase_outproj(C, C.dsc['y1T'], 8, C.din['w_out1'], C.dsc['hT'], C.dsc['hT'])),
            ('E1', lambda: phase_ffn(C, 1, C.dsc['hT'])),
            ('Z', lambda: phase_final(C, C.dsc['hT'], outT)),
        ]
        for name, fn in plan:
            if name in skip:
                continue
            fn()
            if stop_after == name:
                break
    print("ninst", C.S.ninst, "cnt", C.S.cnt, "dcnt max", max(C.S.dcnt))
    return nc


C_SKIP = set()
JOBSEL = None
B0_LEVEL = 9
TAPN = 0
B0_SUB = 9


def host_inputs(inputs, b):
    f = np.float32
    x = np.asarray(inputs['x'], f)
    d = {}
    d['xT'] = np.ascontiguousarray(x[b].T)
    d['w_in0'] = np.ascontiguousarray(inputs['even_w_in'][0], f)
    d['w_out0'] = np.ascontiguousarray(inputs['even_w_out'][0], f)
    d['w_in1'] = np.ascontiguousarray(inputs['odd_w_in'][0], f)
    d['w_out1'] = np.ascontiguousarray(inputs['odd_w_out'][0], f)
    d['w_up'] = np.ascontiguousarray(inputs['ffn_w_up'], f)
    d['w_down'] = np.ascontiguousarray(inputs['ffn_w_down'], f)
    g = np.stack([inputs['mix_norm'][0], inputs['ffn_norm'][0], inputs['mix_norm'][1], inputs['ffn_norm'][1],
                  inputs['final_norm']], 0).astype(f)
    d['gains'] = np.ascontiguousarray(g.reshape(5, 8, 128).transpose(2, 0, 1))
    cw = np.asarray(inputs['ffn_conv_w'], f)
    cb = np.asarray(inputs['ffn_conv_b'], f)
    cp = np.concatenate([cw, cb[:, None, :]], 1)
    d['convp'] = np.ascontiguousarray(cp.reshape(2, 4, 44, 128).transpose(3, 0, 1, 2))
    d['retn'] = np.ascontiguousarray(np.broadcast_to(np.asarray(inputs['ret_norm'], f)[0][None, :], (128, 1024)))
    j = np.arange(128) % 64
    inv = (10000.0 ** (-np.arange(0, 64, 2, dtype=np.float32) / 64)).astype(f)
    ang = np.arange(SEQ, dtype=f)[None, :] * inv[j % 32][:, None]
    cos = np.cos(ang).astype(f)
    sin = np.sin(ang).astype(f)
    sgn = np.where(j < 32, -1.0, 1.0).astype(f)[:, None]
    d['rot'] = np.ascontiguousarray(np.stack([cos, sin * sgn], 1))
    d['ones'] = np.ones((128, 128), ml_dtypes.bfloat16)
    gam = 1.0 - 2.0 ** (-5.0 - np.arange(8, dtype=np.float64))
    ii = np.arange(128)
    diff = ii[None, :] - ii[:, None]
    dec = np.where(diff[:, None, :] >= 0, gam[None, :, None] ** np.maximum(diff, 0)[:, None, :], 0.0)
    d['dect'] = np.ascontiguousarray(dec.astype(f))
    hp = 2 * np.arange(4)[None, :] + (np.arange(128) // 64)[:, None]
    d['gq'] = np.ascontiguousarray(np.broadcast_to((gam[:, None] ** (ii[None, :] + 1.0))[None], (128, 8, 128)).astype(f))
    d['gk'] = np.ascontiguousarray(np.broadcast_to((gam[None, :] ** (127.0 - ii[:, None]))[:, :, None], (128, 8, 64)).astype(f))
    d['cdr'] = np.ascontiguousarray((gam[hp] ** 128.0).astype(f))
    d['ident'] = np.eye(128).astype(ml_dtypes.bfloat16)
    lb = np.asarray(inputs['hgrn_lb'], f)
    d['lbrep'] = np.ascontiguousarray(np.broadcast_to(lb[None], (128, 2, 1024)))
    d['lbfm'] = np.ascontiguousarray(lb.reshape(2, 8, 128).transpose(2, 0, 1))
    d['hgn'] = np.ascontiguousarray(np.broadcast_to(np.asarray(inputs['hgrn_norm'], f)[0][None, :], (128, 1024)))
    jj = np.arange(128)
    same = (jj[:, None] // 64) == (jj[None, :] // 64)
    d['t1m'] = np.ascontiguousarray((same & (jj[:, None] <= jj[None, :])).astype(ml_dtypes.bfloat16))
    d['t2m'] = np.ascontiguousarray((same & (jj[:, None] > jj[None, :])).astype(ml_dtypes.bfloat16))
    rb = np.asarray(inputs['rel_bias'], f)
    cc_ = np.arange(128)[:, None]
    aa_ = np.arange(128)[None, :]
    dist2 = np.stack([aa_ - cc_, 128 + aa_ - cc_], 0)
    valid = np.stack([aa_ >= cc_, aa_ <= cc_], 0)
    bt = np.zeros((128, 3, 8, 2, 128), f)
    for bi, r in enumerate(DIL_R):
        dd_ = (np.maximum(dist2, 0) * r).astype(np.int64)
        df = dd_.astype(np.float32)
        large = 16 + (np.log(np.maximum(df, np.float32(1.0)) / np.float32(16)) / np.float32(math.log(2048 / 16)) * np.float32(16)).astype(np.int32)
        large = np.minimum(large, 31)
        bucket = np.where(dd_ < 16, dd_, large)
        gb = rb[bucket]
        gb = np.where(valid[..., None], gb, 0.0)
        bt[:, bi] = gb.transpose(1, 3, 0, 2)
    d['biasT'] = np.ascontiguousarray(bt.reshape(128, 24, 256))
    d['mask2'] = np.ascontiguousarray(valid.transpose(1, 0, 2).reshape(128, 256).astype(f))
    return d


_NC = {}


def kernel(**inputs):
    if 'nc' not in _NC:
        _NC['nc'] = build()
    nc = _NC['nc']
    in_maps = [host_inputs(inputs, c % 4) for c in range(8)]
    res = run_bass_kernel_spmd(nc, in_maps, core_ids=list(range(8)))
    out = np.stack([np.ascontiguousarray(res.results[b]['outT'].T) for b in range(4)], 0)
    return out.astype(np.float32)


def phase_B0(C):
    nc, S, D = C.nc, C.S, C.dsc
    with ExitStack() as st:
        T = lambda n, s, d=F32: st.enter_context(nc.sbuf_tensor(uniq(n), s, d))
        P = lambda n, s, d=F32: st.enter_context(nc.psum_tensor(uniq(n), s, d))
        dect = T("r_dec", [128, 8, 128])
        gq = T("r_gq", [128, 8, 128])
        gk = T("r_gk", [128, 8, 64])
        cdr = T("r_cd", [128, 4])
        S.dma('sp', dect[:], C.din['dect'], writes=['dect'])
        S.dma('sp', gq[:], C.din['gq'], writes=['gq'])
        S.dma('sp', gk[:], C.din['gk'], writes=['gk'])
        S.dma('sp', cdr[:], C.din['cdr'], writes=['cdr'])
        Sf = T("r_S", [128, 4, 256])
        Sb = [T("r_Sb%d" % i, [128, 4, 256], BF16) for i in range(2)]
        S.op('dve', I('memset', Sf[:], 0.0), writes=['Sf'])
        S.op('dve', I('memset', Sb[0][:], 0.0), writes=[('Sb', 0)])
        qT = [T("r_q%d" % i, [128, 8, 512], BF16) for i in range(2)]
        for i in range(2):
            S.op("dve", I('memset', qT[i][:], 0.0), writes=[("qT", i)])
        kT = [T("r_k%d" % i, [128, 4, 512], BF16) for i in range(2)]
        va = [T("r_v%d" % i, [128, 4, 1024], BF16) for i in range(2)]
        ga = [T("r_g%d" % i, [128, 4, 1024], BF16) for i in range(2)]
        kout = [T("r_ko%d" % i, [128, 8, 64], BF16) for i in range(2)]
        qin = [T("r_qi%d" % i, [128, 8, 128], BF16) for i in range(2)]
        sc = [T("r_sc%d" % i, [128, 8, 128], BF16) for i in range(2)]
        xs = T("r_xs", [128, 8, 128])
        sqb = T("r_sq", [128, 8, 128])
        xn = T("r_xn", [128, 8, 128])
        yb = T("r_yb", [128, 8, 128], BF16)
        stt = T("r_stat", [128, 8, 8])
        yst = [T("r_yst%d" % i, [128, 8, 512], BF16) for i in range(2)]
        ps_kt = P("r_pkt", [128, 512], BF16)
        ps_s = [P("r_ps%d" % i, [128, 512]) for i in range(2)]
        ps_o = [P("r_po%d" % i, [128, 512]) for i in range(2)]
        ps_inc = [P("r_pi%d" % i, [128, 512]) for i in range(2)]
        ps_yt = P("r_pyt", [128, 1024], BF16)
        qv = D['qaT'].rearrange("(g two d) t -> two d g t", two=2, d=64)
        kv = D['kaT'].rearrange("(g p) t -> p g t", p=128)
        vv = D['va'].rearrange("(c p) e -> p c e", p=128)
        gv = D['ga'].rearrange("(c p) e -> p c e", p=128)
        yv = D['yT'].rearrange("(h p) t -> p h t", p=128)
        sbi = 0
        for sci in range(SEQ // 512):
            b = sci % 2
            tsl = slice(sci * 512, (sci + 1) * 512)
            for half in range(2):
                S.dma('sp', qT[b][64 * half:64 * half + 64, half:8:2, :], qv[half][:, :, tsl], reads=[('qT', b)], writes=[('qT', b, half)])
            S.dma('sp', kT[b][:], kv[:, :, tsl], writes=[('kT', b)])
            S.dma('sp', va[b][:], vv[:, sci * 4:(sci + 1) * 4, :], writes=[('va', b)])
            S.dma('sp', ga[b][:], gv[:, sci * 4:(sci + 1) * 4, :], writes=[('ga', b)])
            for c4 in range(4):
                if B0_LEVEL < 1:
                    continue
                n = sci * 4 + c4
                kb = n % 2
                csl = slice(c4 * 128, (c4 + 1) * 128)
                S.ops('pe', [(I('transpose', ps_kt[:, g * 128:(g + 1) * 128], kT[b][:, g, csl], C.ident[:]))
                             for g in range(4)], reads=[('kT', b)], writes=['pskt'])
                if B0_SUB >= 2:
                  S.op('dve', I('tensor_tensor', kout[kb][:], ps_kt[:].rearrange("p (h d) -> p h d", d=64), gk[:], op=ALU.mult),
                     reads=['pskt', 'gk'], writes=[('kout', kb)])
                if B0_SUB < 3:
                    continue
                S.op('dve', I('tensor_tensor', qin[kb][:], qT[b][:, :, csl], gq[:], op=ALU.mult),
                     reads=[('qT', b, 0), ('qT', b, 1), 'gq'], writes=[('qin', kb)])
                if B0_LEVEL >= 2:
                    fns = []
                    for h in range(8):
                        g, r0 = h // 2, 64 * (h % 2)
                        fns.append(I('matmul',
                            ps_s[h % 2][:, (h // 2) * 128:(h // 2 + 1) * 128], lhsT=kT[b][:, g, csl],
                            rhs=qT[b][:, h, csl], start=True, stop=True))
                    S.ops('pe', fns, reads=[('kT', b), ('qT', b, 0), ('qT', b, 1)], writes=[('pss', 0), ('pss', 1)])
                    for i in range(2):
                        S.op('dve', I('tensor_tensor', sc[kb][:, i:8:2, :],
                                                                   ps_s[i][:].rearrange("p (h t) -> p h t", t=128),
                                                                   dect[:, i:8:2, :], op=ALU.mult),
                             reads=[('pss', i), 'dect'], writes=[('sc', kb, i)])
                if B0_LEVEL >= 3:
                    fns = []
                    for h in range(8):
                        g, r0 = h // 2, 64 * (h % 2)
                        osl = slice((h // 2) * 128, (h // 2 + 1) * 128)
                        fns.append(I('matmul',
                            ps_o[h % 2][:, osl], lhsT=sc[kb][:, h, :], rhs=va[b][:, c4, h * 128:(h + 1) * 128], start=True, stop=False))
                        fns.append(I('matmul',
                            ps_o[h % 2][:, osl], lhsT=qin[kb][:, h, :],
                            rhs=Sb[sbi][:, g, (h % 2) * 128:(h % 2 + 1) * 128], start=False, stop=True))
                    S.ops('pe', fns, reads=[('sc', kb, 0), ('sc', kb, 1), ('va', b), ('qin', kb), ('Sb', sbi)],
                          writes=[('pso', 0), ('pso', 1)])
                if n == TAPN:
                    C.tap('kout', kout[kb][:], [128, 8, 64], BF16, [('kout', kb)])
                    C.tap('qin', qin[kb][:], [128, 8, 128], BF16, [('qin', kb)])
                    C.tap('sc', sc[kb][:], [128, 8, 128], BF16, [('sc', kb, 0), ('sc', kb, 1)])
                    C.tap('qz', qT[b][:], [128, 8, 512], BF16, [('qT', b, 0), ('qT', b, 1)])
                    C.tap('sb', Sb[sbi][:], [128, 4, 256], BF16, [('Sb', sbi)])
                for i in range(2 if B0_LEVEL >= 4 else 0):
                    fns = []
                    for g in range(2 * i, 2 * i + 2):
                        fns.append(I('matmul',
                            ps_inc[i][:, (g % 2) * 256:(g % 2 + 1) * 256],
                            lhsT=kout[kb][:, 2 * g:2 * g + 2, :].rearrange("p h d -> p (h d)"),
                            rhs=va[b][:, c4, g * 256:(g + 1) * 256], start=True, stop=True))
                    S.ops('pe', fns, reads=[('kout', kb), ('va', b)], writes=[('psi', i)])
                    for g in range(2 * i, 2 * i + 2):
                        S.op('dve', I('scalar_tensor_tensor',
                            Sf[:, g, :], in0=Sf[:, g, :], scalar=cdr[:, g:g + 1], in1=ps_inc[i][:, (g % 2) * 256:(g % 2 + 1) * 256],
                            op0=ALU.mult, op1=ALU.add), reads=[('psi', i), 'cdr', ('Sf', g)], writes=[('Sf', g)])
                if B0_SUB >= 4:
                  S.op('pool', I('tensor_copy', Sb[1 - sbi][:], Sf[:]), reads=[('Sf', g) for g in range(4)] + ['Sf'],
                     writes=[('Sb', 1 - sbi)])
                sbi = 1 - sbi
                if B0_LEVEL < 5:
                    continue
                for i in range(2):
                    S.op('act', I('copy', xs[:, i:8:2, :], ps_o[i][:].rearrange("p (h t) -> p h t", t=128)),
                         reads=[('pso', i)], writes=[('xs', i)])
                xk = [('xs', 0), ('xs', 1)]
                if 'dbg_ret' in D:
                    S.dma('sp', D['dbg_ret'][n * 128:(n + 1) * 128, :], xs[:].rearrange("p h t -> p (h t)"), reads=xk)
                S.op('dve', I('tensor_reduce', stt[:, 0, :], xs[:], axis=AX.X, op=ALU.add), reads=xk, writes=['sums'])
                S.op('act', I('activation', sqb[:], xs[:], AF.Square), reads=xk, writes=['sqb'])
                S.op('dve', I('tensor_reduce', stt[:, 1, :], sqb[:], axis=AX.X, op=ALU.add), reads=['sqb'], writes=['sumsq'])
                S.op('dve', I('tensor_scalar', stt[:, 2, :], stt[:, 0, :], 1.0 / 128, None, op0=ALU.mult),
                     reads=['sums'], writes=['mean'])
                S.op('dve', I('tensor_tensor', stt[:, 3, :], stt[:, 2, :], stt[:, 2, :], op=ALU.mult), reads=['mean'], writes=['msq'])
                S.op('dve', I('scalar_tensor_tensor', stt[:, 4, :], in0=stt[:, 1, :], scalar=1.0 / 128, in1=stt[:, 3, :],
                                                             op0=ALU.mult, op1=ALU.subtract), reads=['sumsq', 'msq'], writes=['var'])
                S.op('act', I('activation', stt[:, 5, :], stt[:, 4, :], AF.Sqrt, bias=C.epsc[:, 0:1]), reads=['var'], writes=['sd'])
                S.op('dve', I('reciprocal', stt[:, 6, :], stt[:, 5, :]), reads=['sd'], writes=['rstd'])
                S.op('dve', I('tensor_tensor', xn[:], xs[:], stt[:, 2, :].unsqueeze(2).to_broadcast([128, 8, 128]), op=ALU.subtract),
                     reads=xk + ['mean'], writes=['xn'])
                S.op('dve', I('tensor_tensor', xn[:], xn[:], stt[:, 6, :].unsqueeze(2).to_broadcast([128, 8, 128]), op=ALU.mult),
                     reads=['xn', 'rstd'], writes=['xn'])
                S.op('dve', I('tensor_tensor', yb[:], xn[:], ga[b][:, c4, :].rearrange("p (h t) -> p h t", t=128), op=ALU.mult),
                     reads=['xn', ('ga', b)], writes=['yb'])
                S.ops('pe', [(I('transpose', ps_yt[:, h * 128:(h + 1) * 128], yb[:, h, :], C.ident[:])) for h in range(8)],
                      reads=['yb'], writes=['psyt'])
                S.op('act', I('copy', yst[b][:, :, csl], ps_yt[:].rearrange("p (h t) -> p h t", t=128)),
                     reads=['psyt'], writes=[('yst', b, c4)])
            S.dma('pool', yv[:, 0:8, tsl], yst[b][:], reads=[('yst', b, c4) for c4 in range(4)])
        S.sync()


DIL_R = (1, 4, 16)


def phase_C0(C):
    nc, S, D = C.nc, C.S, C.dsc
    with ExitStack() as st:
        T = lambda n, s, d=F32: st.enter_context(nc.sbuf_tensor(uniq(n), s, d))
        P = lambda n, s, d=F32: st.enter_context(nc.psum_tensor(uniq(n), s, d))
        EBT = T("c_ebt", [128, 24, 256], BF16)
        with ExitStack() as st2:
            bt = st2.enter_context(nc.sbuf_tensor(uniq("c_bt"), [128, 24, 256], F32))
            mk = st2.enter_context(nc.sbuf_tensor(uniq("c_mk"), [128, 256], F32))
            S.dma('sp', bt[:], C.din['biasT'], writes=['bt'])
            S.dma('sp', mk[:], C.din['mask2'], writes=['mk'])
            S.op('act', I('activation', bt[:], bt[:], AF.Exp), reads=['bt'], writes=['bt'])
            S.op('dve', I('tensor_tensor', EBT[:], bt[:], mk[:].unsqueeze(1).to_broadcast([128, 24, 256]), op=ALU.mult),
                 reads=['bt', 'mk'], writes=['ebt'])
            S.sync()
        kT = T("c_k", [128, SEQ], BF16)
        qz = T("c_q", [128, 2, SEQ], BF16)
        vp = [T("c_v%d" % i, [128, 32, 2, 64], BF16) for i in range(2)]
        onesb = T("c_ones", [128, 64], BF16)
        accn = T("c_an", [64, 2, SEQ])
        accd = T("c_ad", [64, 2, SEQ])
        pe_ = [T("c_pe%d" % i, [128, 2, 128], BF16) for i in range(3)]
        pt_ = [T("c_pt%d" % i, [128, 2, 128], BF16) for i in range(3)]
        rden = [T("c_rd%d" % i, [64, 512]) for i in range(2)]
        ystg = [T("c_ys%d" % i, [64, 512], BF16) for i in range(2)]
        ps = [P("c_ps%d" % i, [128, 256]) for i in range(2)]
        po = [P("c_po%d" % i, [64, 512]) for i in range(2)]
        pd = [P("c_pd%d" % i, [64, 512]) for i in range(2)]
        S.op('dve', I('memset', qz[:], 0.0), writes=['qz'])
        S.op('dve', I('memset', onesb[:], 1.0), writes=['onesb'])
        qv = D['qbT'].rearrange("(g two d) t -> g two d t", two=2, d=64)
        kv = D['kbT'].rearrange("(g p) t -> g p t", p=128)
        cnt = 0
        bcnt = 0
        vcnt = 0
        for g in range(4):
            S.dma('sp', kT[:], kv[g], writes=['kT'])
            for half in range(2):
                S.dma('sp', qz[64 * half:64 * half + 64, half, :], qv[g, half], reads=['qz'], writes=[('qz', half)])
            for bi, r in enumerate(DIL_R):
                nb = SEQ // (128 * r)
                vb_ = vp[vcnt % 2]
                vkey = ('vp', vcnt % 2)
                vcnt += 1
                vsrc = D['vb'].rearrange("(n a r) (g2 hh d) -> a r n g2 hh d", a=128, r=r, hh=2, d=64)
                vkeys = []
                for rr in range(r):
                    step = 8 if nb > 8 else nb
                    for n0 in range(0, nb, step):
                        S.dma('sp', vb_[:, rr * nb + n0:rr * nb + n0 + step, :, :], vsrc[:, rr, n0:n0 + step, g, :, :],
                              writes=[(vkey, rr, n0)])
                        vkeys.append((vkey, rr, n0))
                for hh in range(2):
                    h = 2 * g + hh
                    if r == 1:
                        batches = [[(0, n) for n in range(n0, n0 + 4)] for n0 in range(0, nb, 4)]
                    else:
                        batches = [[(rho, n) for rho in range(r0, r0 + 4)] for n in range(nb) for r0 in range(0, r, 4)]
                    for batch in batches:
                        bb = bcnt % 2
                        bcnt += 1
                        for slot, (rho, n) in enumerate(batch):
                            pi = cnt % 2
                            ei = cnt % 3
                            cnt += 1
                            nk = 1 if n == 0 else 2
                            qsl = slice(128 * n * r + rho, 128 * n * r + rho + 127 * r + 1, r)
                            fns = [I('matmul', ps[pi][:, 0:128], lhsT=kT[:, qsl], rhs=qz[:, hh, qsl], start=True, stop=True)]
                            if nk == 2:
                                ksl = slice(128 * (n - 1) * r + rho, 128 * (n - 1) * r + rho + 127 * r + 1, r)
                                fns.append(I('matmul', ps[pi][:, 128:256], lhsT=kT[:, ksl], rhs=qz[:, hh, qsl], start=True, stop=True))
                            S.ops('pe', fns, reads=['kT', ('qz', 0), ('qz', 1)], writes=[('ps', pi)])
                            S.op('act', I('activation', pe_[ei][:, 0:nk, :], ps[pi][:, 0:nk * 128].rearrange("p (s q) -> p s q", q=128), AF.Exp),
                                 reads=[('ps', pi)], writes=[('pe', ei)])
                            S.op('dve', I('tensor_tensor', pt_[ei][:, 0:nk, :], pe_[ei][:, 0:nk, :],
                                          EBT[:, bi * 8 + h, 0:nk * 128].rearrange("p (s q) -> p s q", q=128), op=ALU.mult),
                                 reads=[('pe', ei), 'ebt'], writes=[('pt', ei)])
                            ti = rho * nb + n
                            osl = slice(slot * 128, (slot + 1) * 128)
                            fo = [I('matmul', po[bb][:, osl], lhsT=vb_[:, ti, hh, :], rhs=pt_[ei][:, 0, :], start=True, stop=(nk == 1))]
                            fd = [I('matmul', pd[bb][:, osl], lhsT=onesb[:], rhs=pt_[ei][:, 0, :], start=True, stop=(nk == 1))]
                            if nk == 2:
                                fo.append(I('matmul', po[bb][:, osl], lhsT=vb_[:, ti - 1, hh, :], rhs=pt_[ei][:, 1, :], start=False, stop=True))
                                fd.append(I('matmul', pd[bb][:, osl], lhsT=onesb[:], rhs=pt_[ei][:, 1, :], start=False, stop=True))
                            S.ops('pe', fo + fd, reads=[('pt', ei), 'onesb'] + vkeys, writes=[('po', bb), ('pd', bb)])
                        rho0, n0 = batch[0]
                        if r == 1:
                            tok0 = 128 * n0
                            dn = accn[:, hh, tok0:tok0 + 512]
                            dd = accd[:, hh, tok0:tok0 + 512]
                            sn, sd = po[bb][:], pd[bb][:]
                        else:
                            base = 128 * n0 * r
                            dn = accn[:, hh, base:base + 128 * r].rearrange("p (a r) -> p a r", r=r)[:, :, rho0:rho0 + 4]
                            dd = accd[:, hh, base:base + 128 * r].rearrange("p (a r) -> p a r", r=r)[:, :, rho0:rho0 + 4]
                            sn = po[bb][:].rearrange("p (s a) -> p a s", a=128)
                            sd = pd[bb][:].rearrange("p (s a) -> p a s", a=128)
                        if bi == 0:
                            S.op('dve', I('tensor_copy', dn, sn), reads=[('po', bb)], writes=[('accn', hh)])
                            S.op('dve', I('tensor_copy', dd, sd), reads=[('pd', bb)], writes=[('accd', hh)])
                        else:
                            S.op('dve', I('tensor_tensor', dn, sn, dn, op=ALU.add), reads=[('po', bb), ('accn', hh)], writes=[('accn', hh)])
                            S.op('dve', I('tensor_tensor', dd, sd, dd, op=ALU.add), reads=[('pd', bb), ('accd', hh)], writes=[('accd', hh)])
            for hh in range(2):
                h = 2 * g + hh
                for tt in range(8):
                    i = tt % 2
                    tsl = slice(tt * 512, (tt + 1) * 512)
                    S.op('dve', I('reciprocal', rden[i][:], accd[:, hh, tsl]), reads=[('accd', hh)], writes=[('rden', i)])
                    S.op('dve', I('tensor_tensor', ystg[i][:], accn[:, hh, tsl], rden[i][:], op=ALU.mult),
                         reads=[('accn', hh), ('rden', i)], writes=[('ystg', i)])
                    S.dma('pool', D['yT'][1024 + 64 * h:1024 + 64 * h + 64, tsl], ystg[i][:], reads=[('ystg', i)])
        S.sync()


def phase_A1(C, hT):
    nc, S, D = C.nc, C.S, C.dsc
    with ExitStack() as st:
        T = lambda n, s, d=F32: st.enter_context(nc.sbuf_tensor(uniq(n), s, d))
        hnT = T("a1_hn", [128, 8, SEQ], BF16)
        norm_pass(C, hT, 2, hnT, 0, 8)
        lbr = T("a1_lbr", [128, 1024])
        omr = T("a1_omr", [128, 1024])
        hgn = T("a1_hgn", [128, 1024])
        lbf = T("a1_lbf", [128, 8])
        omf = T("a1_omf", [128, 8])
        with ExitStack() as st2:
            raw = st2.enter_context(nc.sbuf_tensor(uniq("a1_raw"), [128, 2, 1024], F32))
            rawf = st2.enter_context(nc.sbuf_tensor(uniq("a1_rawf"), [128, 2, 8], F32))
            S.dma('sp', raw[:], C.din['lbrep'], writes=['raw'])
            S.dma('sp', rawf[:], C.din['lbfm'], writes=['rawf'])
            S.dma('sp', hgn[:], C.din['hgn'], writes=['hgn'])
            S.op('dve', I('tensor_tensor', lbr[:], raw[:, 1, :], raw[:, 0, :], op=ALU.subtract), reads=['raw'], writes=['lbr'])
            S.op('act', I('activation', lbr[:], lbr[:], AF.Sigmoid), reads=['lbr'], writes=['lbr'])
            S.op('dve', I('tensor_scalar', omr[:], lbr[:], -1.0, 1.0, op0=ALU.mult, op1=ALU.add), reads=['lbr'], writes=['omr'])
            S.op('dve', I('tensor_tensor', lbf[:], rawf[:, 1, :], rawf[:, 0, :], op=ALU.subtract), reads=['rawf'], writes=['lbf'])
            S.op('act', I('activation', lbf[:], lbf[:], AF.Sigmoid), reads=['lbf'], writes=['lbf'])
            S.op('dve', I('tensor_scalar', omf[:], lbf[:], -1.0, 1.0, op0=ALU.mult, op1=ALU.add), reads=['lbf'], writes=['omf'])
            S.sync()
        sb = Stager(C, st, "a1_sb", [128, 512], BF16, 6)
        sf = Stager(C, st, "a1_sf", [128, 512], F32, 3)
        t1 = [T("a1_t1%d" % i, [128, 512]) for i in range(2)]
        t2 = [T("a1_t2%d" % i, [128, 512]) for i in range(2)]
        cnt = [0]

        def epi_fm_silu(dst, cb):
            def f(sub, tt, ps, pk):
                buf, bk = sb.next()
                S.op('act', I('activation', buf[:], ps[:], AF.Silu), reads=[pk], writes=[bk])
                sb.store(dst[cb * 512 + sub * 128:cb * 512 + (sub + 1) * 128, tt * 512:(tt + 1) * 512], buf[:], bk)
            return f

        def epi_fm_k(dst, cb):
            def f(sub, tt, ps, pk):
                i = cnt[0] % 2
                cnt[0] += 1
                ci = cb * 4 + sub
                S.op('act', I('activation', t1[i][:], ps[:], AF.Sigmoid, scale=-1.0), reads=[pk], writes=[('t1', i)])
                buf, bk = sb.next()
                S.op('dve', I('tensor_scalar', buf[:], t1[i][:], omf[:, ci:ci + 1], None, op0=ALU.mult), reads=[('t1', i)], writes=[bk])
                sb.store(dst[ci * 128:(ci + 1) * 128, tt * 512:(tt + 1) * 512], buf[:], bk)
            return f

        def epi_tm_f(cb):
            def f(t, ps, pk):
                i = cnt[0] % 2
                cnt[0] += 1
                csl = slice(cb * 512, (cb + 1) * 512)
                rsl = slice(t * 128, (t + 1) * 128)
                S.op('act', I('activation', t1[i][:], ps[:], AF.Sigmoid), reads=[pk], writes=[('t1', i)])
                S.op('dve', I('tensor_tensor', t2[i][:], t1[i][:], omr[:, csl], op=ALU.mult), reads=[('t1', i)], writes=[('t2', i)])
                S.op('dve', I('tensor_tensor', t2[i][:], t2[i][:], lbr[:, csl], op=ALU.add), reads=[('t2', i)], writes=[('t2', i)])
                fb, fk = sf.next()
                S.op('act', I('activation', fb[:], t2[i][:], AF.Ln), reads=[('t2', i)], writes=[fk])
                hb_, hk_ = sb.next()
                S.op('dve', I('tensor_copy', hb_[:], fb[:]), reads=[fk], writes=[hk_])
                sb.store(D['lfh'][rsl, csl], hb_[:], hk_)
                lb_, lk_ = sb.next()
                S.op('dve', I('tensor_tensor', lb_[:], fb[:], hb_[:], op=ALU.subtract), reads=[fk, hk_], writes=[lk_])
                sb.store(D['lfl'][rsl, csl], lb_[:], lk_)
                buf, bk = sb.next()
                S.op('dve', I('tensor_scalar', buf[:], t2[i][:], -1.0, 1.0, op0=ALU.mult, op1=ALU.add), reads=[('t2', i)], writes=[bk])
                sb.store(D['k1'][rsl, csl], buf[:], bk)
            return f

        def epi_tm_copy(dst, cb):
            def f(t, ps, pk):
                buf, bk = sb.next()
                S.op('act', I('copy', buf[:], ps[:]), reads=[pk], writes=[bk])
                sb.store(dst[t * 128:(t + 1) * 128, cb * 512:(cb + 1) * 512], buf[:], bk)
            return f

        def epi_tm_gate(dst, cb):
            def f(t, ps, pk):
                i = cnt[0] % 2
                cnt[0] += 1
                S.op('act', I('activation', t1[i][:], ps[:], AF.Silu), reads=[pk], writes=[('t1', i)])
                buf, bk = sb.next()
                S.op('dve', I('tensor_tensor', buf[:], t1[i][:], hgn[:, cb * 512:(cb + 1) * 512], op=ALU.mult),
                     reads=[('t1', i)], writes=[bk])
                sb.store(dst[t * 128:(t + 1) * 128, cb * 512:(cb + 1) * 512], buf[:], bk)
            return f
        jobs = []
        for cb in range(2):
            jobs.append(dict(c0=cb * 512, role='fm', epi_fm=epi_fm_silu(D['q1T'], cb)))
        for cb in range(2):
            jobs.append(dict(c0=1024 + cb * 512, role='both', epi_fm=epi_fm_k(D['k1T'], cb), epi_tm=epi_tm_f(cb)))
        for cb in range(2):
            jobs.append(dict(c0=2048 + cb * 512, role='tm', epi_tm=epi_tm_copy(D['v1'], cb)))
        for cb in range(2):
            jobs.append(dict(c0=3072 + cb * 512, role='tm', epi_tm=epi_tm_gate(D['g1'], cb)))
        proj_phase(C, hnT, C.din['w_in1'], jobs)


def phase_B1(C):
    nc, S, D = C.nc, C.S, C.dsc
    with ExitStack() as st:
        T = lambda n, s, d=F32: st.enter_context(nc.sbuf_tensor(uniq(n), s, d))
        P = lambda n, s, d=F32: st.enter_context(nc.psum_tensor(uniq(n), s, d))
        T1 = T("h_t1", [128, 128], BF16)
        T2 = T("h_t2", [128, 128], BF16)
        S.dma('sp', T1[:], C.din['t1m'], writes=['T1'])
        S.dma('sp', T2[:], C.din['t2m'], writes=['T2'])
        NSB = 256
        qT = [T("h_q%d" % i, [128, 8, NSB], BF16) for i in range(2)]
        kT = [T("h_k%d" % i, [128, 8, NSB], BF16) for i in range(2)]
        lf = [T("h_lf%d" % i, [128, 2, 2, 1024], BF16) for i in range(2)]
        kk = [T("h_kk%d" % i, [128, 2, 1024], BF16) for i in range(2)]
        vv = [T("h_v%d" % i, [128, 2, 1024], BF16) for i in range(2)]
        gg = [T("h_g%d" % i, [128, 2, 1024], BF16) for i in range(2)]
        eB = T("h_eB", [128, 8, 128])
        eNB = T("h_eNB", [128, 8, 128])
        eRB = T("h_eRB", [128, 1024])
        qlo = [T("h_qlo%d" % i, [128, 8, 128], BF16) for i in range(2)]
        qhi = [T("h_qhi%d" % i, [128, 8, 128], BF16) for i in range(2)]
        kt = [T("h_kt%d" % i, [128, 8, 128], BF16) for i in range(2)]
        klo = [T("h_klo%d" % i, [128, 1024], BF16) for i in range(2)]
        khi = [T("h_khi%d" % i, [128, 1024], BF16) for i in range(2)]
        sc = [T("h_sc%d" % i, [128, 8, 128], BF16) for i in range(2)]
        Sf = T("h_S", [128, 8, 128])
        Sbp = [T("h_Sbp%d" % i, [128, 8, 128], BF16) for i in range(2)]
        Sbm = [T("h_Sbm%d" % i, [128, 8, 128], BF16) for i in range(2)]
        sq = T("h_sq", [128, 8, 128])
        xn = T("h_xn", [128, 8, 128])
        yb = T("h_yb", [128, 8, 128], BF16)
        stt = T("h_stt", [128, 3, 8])
        yst = [T("h_yst%d" % i, [128, 8, NSB], BF16) for i in range(2)]
        bA = [P("h_pA%d" % i, [128, 512]) for i in range(2)]
        bB = [P("h_pB%d" % i, [128, 512]) for i in range(2)]
        bC = [P("h_pC%d" % i, [128, 512]) for i in range(2)]
        bD = P("h_pD", [128, 1024], BF16)
        for i in range(2):
            S.op('dve', I('memset', qlo[i][:], 0.0), writes=[('qlo', i)])
            S.op('dve', I('memset', qhi[i][:], 0.0), writes=[('qhi', i)])
            S.op('dve', I('memset', klo[i][:], 0.0), writes=[('klo', i)])
            S.op('dve', I('memset', khi[i][:], 0.0), writes=[('khi', i)])
        S.op('dve', I('memset', Sf[:], 0.0), writes=[('Sf', h) for h in range(8)])
        S.op('dve', I('memset', Sbp[0][:], 0.0), writes=[('Sbp', 0, h) for h in range(8)])
        qv = D['q1T'].rearrange("(h p) t -> p h t", p=128)
        kv = D['k1T'].rearrange("(h p) t -> p h t", p=128)
        lvh = D['lfh'].rearrange("(c p) e -> p c e", p=128)
        lvl = D['lfl'].rearrange("(c p) e -> p c e", p=128)
        k2v = D['k1'].rearrange("(c p) e -> p c e", p=128)
        vv_ = D['v1'].rearrange("(c p) e -> p c e", p=128)
        gv = D['g1'].rearrange("(c p) e -> p c e", p=128)
        yv = D['y1T'].rearrange("(h p) t -> p h t", p=128)
        for sbi_ in range(SEQ // NSB):
            b = sbi_ % 2
            tsl = slice(sbi_ * NSB, (sbi_ + 1) * NSB)
            csl2 = slice(sbi_ * 2, sbi_ * 2 + 2)
            S.dma('sp', qT[b][:], qv[:, :, tsl], writes=[('qT', b)])
            S.dma('sp', kT[b][:], kv[:, :, tsl], writes=[('kT', b)])
            S.dma('sp', lf[b][:, 0], lvh[:, csl2, :], writes=[('lf', b, 0)])
            S.dma('sp', lf[b][:, 1], lvl[:, csl2, :], writes=[('lf', b, 1)])
            S.dma('sp', kk[b][:], k2v[:, csl2, :], writes=[('kk', b)])
            S.dma('sp', vv[b][:], vv_[:, csl2, :], writes=[('vv', b)])
            S.dma('sp', gg[b][:], gv[:, csl2, :], writes=[('gg', b)])
            for blk in range(2):
                n = sbi_ * 2 + blk
                p2 = n % 2
                bsl = slice(blk * 128, (blk + 1) * 128)
                for i in range(2):
                    lfk = [('lf', b, 0), ('lf', b, 1)]
                    S.ops('pe', [I('matmul', bA[i][:, (h % 4) * 128:(h % 4 + 1) * 128], lhsT=lf[b][:, hl, blk, h * 128:(h + 1) * 128], rhs=T1[:],
                                   start=(hl == 0), stop=(hl == 1)) for h in range(4 * i, 4 * i + 4) for hl in range(2)],
                          reads=lfk + ['T1'], writes=[('bA', i)])
                    S.ops('pe', [I('matmul', bB[i][:], lhsT=T2[:], rhs=lf[b][:, hl, blk, i * 512:(i + 1) * 512], start=(hl == 0), stop=(hl == 1))
                                 for hl in range(2)], reads=lfk + ['T2'], writes=[('bB', i)])
                    S.op('act', I('activation', eB[:, 4 * i:4 * i + 4, :], bA[i][:].rearrange("p (h t) -> p h t", t=128), AF.Exp),
                         reads=[('bA', i)], writes=[('eB', i)])
                    S.op('act', I('activation', eNB[:, 4 * i:4 * i + 4, :], bA[i][:].rearrange("p (h t) -> p h t", t=128), AF.Exp, scale=-1.0),
                         reads=[('bA', i)], writes=[('eNB', i)])
                    S.op('act', I('activation', eRB[:, i * 512:(i + 1) * 512], bB[i][:], AF.Exp), reads=[('bB', i)], writes=[('eRB', i)])
                ek = [('eB', 0), ('eB', 1)]
                S.op('dve', I('tensor_tensor', qlo[p2][:, :, 0:64], qT[b][:, :, blk * 128:blk * 128 + 64], eB[:, :, 0:64], op=ALU.mult),
                     reads=ek + [('qT', b), ('qlo', p2)], writes=[('qlo', p2)])
                S.op('dve', I('tensor_tensor', qhi[p2][:, :, 64:128], qT[b][:, :, blk * 128 + 64:blk * 128 + 128], eB[:, :, 64:128], op=ALU.mult),
                     reads=ek + [('qT', b), ('qhi', p2)], writes=[('qhi', p2)])
                S.op('dve', I('tensor_tensor', kt[p2][:], kT[b][:, :, bsl], eNB[:], op=ALU.mult),
                     reads=[('eNB', 0), ('eNB', 1), ('kT', b)], writes=[('kt', p2)])
                S.op('dve', I('tensor_tensor', klo[p2][0:64, :], kk[b][0:64, blk, :], eRB[0:64, :], op=ALU.mult),
                     reads=[('eRB', 0), ('eRB', 1), ('kk', b), ('klo', p2)], writes=[('klo', p2)])
                S.op('dve', I('tensor_tensor', khi[p2][64:128, :], kk[b][64:128, blk, :], eRB[64:128, :], op=ALU.mult),
                     reads=[('eRB', 0), ('eRB', 1), ('kk', b), ('khi', p2)], writes=[('khi', p2)])
                for i in range(2):
                    fns = []
                    for h in range(4 * i, 4 * i + 4):
                        o0 = (h % 4) * 128
                        fns.append(I('matmul', bA[i][:, o0:o0 + 64], lhsT=kt[p2][:, h, :], rhs=qlo[p2][:, h, 0:64], start=True, stop=True))
                        fns.append(I('matmul', bA[i][:, o0 + 64:o0 + 128], lhsT=kt[p2][:, h, :], rhs=qhi[p2][:, h, 64:128], start=True, stop=True))
                    S.ops('pe', fns, reads=[('kt', p2), ('qlo', p2), ('qhi', p2), ('eB', i), ('eNB', i)], writes=[('bA', i)])
                    S.op('dve', I('tensor_tensor', sc[p2][:, 4 * i:4 * i + 4, :], bA[i][:].rearrange("p (h t) -> p h t", t=128),
                                  T1[:].unsqueeze(1).to_broadcast([128, 4, 128]), op=ALU.mult), reads=[('bA', i), 'T1'], writes=[('sc', p2, i)])
                for half, (ksrc, kkey, dst_) in enumerate(((klo[p2], ('klo', p2), Sbm[p2]), (khi[p2], ('khi', p2), Sbp[1 - p2]))):
                    dkey = 'Sbm' if half == 0 else 'Sbp'
                    dpar = p2 if half == 0 else 1 - p2
                    col = 63 if half == 0 else 127
                    for i in range(2):
                        S.ops('pe', [I('matmul', bC[i][:, (h % 4) * 128:(h % 4 + 1) * 128], lhsT=ksrc[:, h * 128:(h + 1) * 128],
                                       rhs=vv[b][:, blk, h * 128:(h + 1) * 128], start=True, stop=True) for h in range(4 * i, 4 * i + 4)],
                              reads=[kkey, ('vv', b)], writes=[('bC', i)])
                        for h in range(4 * i, 4 * i + 4):
                            S.op('dve', I('scalar_tensor_tensor', Sf[:, h, :], in0=Sf[:, h, :], scalar=eB[:, h, col:col + 1],
                                          in1=bC[i][:, (h % 4) * 128:(h % 4 + 1) * 128], op0=ALU.mult, op1=ALU.add),
                                 reads=[('bC', i), ('eB', i), ('Sf', h)], writes=[('Sf', h)])
                        S.op('pool', I('tensor_copy', dst_[:, 4 * i:4 * i + 4, :], Sf[:, 4 * i:4 * i + 4, :]),
                             reads=[('Sf', h) for h in range(4 * i, 4 * i + 4)], writes=[(dkey, dpar, h) for h in range(4 * i, 4 * i + 4)])
                for i in range(2):
                    fns = []
                    for h in range(4 * i, 4 * i + 4):
                        osl = slice((h % 4) * 128, (h % 4 + 1) * 128)
                        fns.append(I('matmul', bB[i][:, osl], lhsT=sc[p2][:, h, :], rhs=vv[b][:, blk, h * 128:(h + 1) * 128], start=True, stop=False))
                        fns.append(I('matmul', bB[i][:, osl], lhsT=qlo[p2][:, h, :], rhs=Sbp[p2][:, h, :], start=False, stop=False))
                        fns.append(I('matmul', bB[i][:, osl], lhsT=qhi[p2][:, h, :], rhs=Sbm[p2][:, h, :], start=False, stop=True))
                    S.ops('pe', fns, reads=[('sc', p2, i), ('vv', b), ('qlo', p2), ('qhi', p2), ('eRB', i)]
                          + [('Sbp', p2, h) for h in range(4 * i, 4 * i + 4)] + [('Sbm', p2, h) for h in range(4 * i, 4 * i + 4)],
                          writes=[('bB', i)])
                    S.op('act', I('activation', sq[:, 4 * i:4 * i + 4, :], bB[i][:].rearrange("p (h t) -> p h t", t=128), AF.Square),
                         reads=[('bB', i)], writes=[('sq', i)])
                S.op('dve', I('tensor_reduce', stt[:, 0, :], sq[:], axis=AX.X, op=ALU.add), reads=[('sq', 0), ('sq', 1)], writes=['ss'])
                S.op('act', I('activation', stt[:, 1, :], stt[:, 0, :], AF.Sqrt, scale=1.0 / 128, bias=C.epsc[:, 0:1]), reads=['ss'], writes=['sd'])
                S.op('dve', I('reciprocal', stt[:, 2, :], stt[:, 1, :]), reads=['sd'], writes=['rstd'])
                for i in range(2):
                    S.op('dve', I('tensor_tensor', xn[:, 4 * i:4 * i + 4, :], bB[i][:].rearrange("p (h t) -> p h t", t=128),
                                  stt[:, 2, 4 * i:4 * i + 4].unsqueeze(2).to_broadcast([128, 4, 128]), op=ALU.mult),
                         reads=[('bB', i), 'rstd'], writes=[('xn', i)])
                S.op('dve', I('tensor_tensor', yb[:], xn[:], gg[b][:, blk, :].rearrange("p (h t) -> p h t", t=128), op=ALU.mult),
                     reads=[('xn', 0), ('xn', 1), ('gg', b)], writes=['yb'])
                S.ops('pe', [I('transpose', bD[:, h * 128:(h + 1) * 128], yb[:, h, :], C.ident[:]) for h in range(8)], reads=['yb'], writes=['bD'])
                S.op('act', I('copy', yst[b][:, :, bsl], bD[:].rearrange("p (h t) -> p h t", t=128)), reads=['bD'], writes=[('yst', b, blk)])
            S.dma('pool', yv[:, :, tsl], yst[b][:], reads=[('yst', b, 0), ('yst', b, 1)])
        S.sync()
```

```python
import math
import numpy as np
import ml_dtypes
from contextlib import ExitStack
import concourse.bass as bass
import concourse.mybir as mybir
from concourse.bass_utils import run_bass_kernel_spmd

F32 = mybir.dt.float32
BF16 = mybir.dt.bfloat16
AF = mybir.ActivationFunctionType
ALU = mybir.AluOpType
AX = mybir.AxisListType

SEQ = 4096
DM = 1024
DFF = 2816
EPS = 1e-6
STOP_AFTER = None


class Sched:
    CE = ('pe', 'act', 'dve', 'pool')
    ALLQ = ('pe', 'act', 'dve', 'pool', 'sp')

    def __init__(self, nc, stack, ndma=24):
        self.nc = nc
        self.sem = {e: stack.enter_context(nc.semaphore('s_' + e)) for e in self.CE}
        self.cnt = {e: 0 for e in self.CE}
        self.dsem = [stack.enter_context(nc.semaphore('d%d' % i)) for i in range(ndma)]
        self.dcnt = [0] * ndma
        self.dnext = 0
        self.waited = {}
        self.lastw = {}
        self.readers = {}
        self.q = {e: [] for e in self.ALLQ}
        self.ninst = {e: 0 for e in self.ALLQ}

    def _semof(self, key):
        return self.sem[key[1]] if key[0] == 'c' else self.dsem[key[1]]

    def _deps(self, eng, reads, writes):
        raw = set()
        toks = set()
        for k in reads:
            t = self.lastw.get(k)
            if t is not None:
                raw.add(t)
        for k in writes:
            t = self.lastw.get(k)
            if t is not None:
                toks.add(t)
            r = self.readers.get(k)
            if r:
                toks.update(r.values())
        waits = {}
        for t in raw | toks:
            kind, who, val = t
            if kind == 'c' and who == eng and t not in raw:
                continue
            key = (kind, who)
            if self.waited.get((eng,) + key, 0) >= val:
                continue
            if waits.get(key, 0) < val:
                waits[key] = val
        for key, val in waits.items():
            self.waited[(eng,) + key] = val
        return waits

    def _record(self, tok, reads, writes):
        for k in writes:
            self.lastw[k] = tok
            self.readers[k] = {}
        for k in reads:
            d = self.readers.setdefault(k, {})
            if tok[0] == 'c':
                d[('c', tok[1])] = tok
            else:
                d[tok] = tok

    def op(self, eng, fn, reads=(), writes=()):
        return self.ops(eng, [fn], reads, writes)

    def ops(self, eng, fns, reads=(), writes=()):
        waits = self._deps(eng, reads, writes)
        self.cnt[eng] += 1
        tok = ('c', eng, self.cnt[eng])
        sem = self.sem[eng]
        wl = [(self._semof(k), v) for k, v in waits.items()]

        def run(e, fns=fns, wl=wl, sem=sem):
            for s, v in wl:
                e.wait_ge(s, v)
            for f in fns[:-1]:
                f(e)
            fns[-1](e).then_inc(sem, 1)
        self.q[eng].append(run)
        self.ninst[eng] += len(fns) + len(wl)
        self._record(tok, reads, writes)
        return tok

    def dma(self, qeng, out, in_, reads=(), writes=()):
        i = self.dnext
        self.dnext = (i + 1) % len(self.dsem)
        waits = self._deps(qeng, reads, writes)
        prev = self.dcnt[i]
        if prev and self.waited.get((qeng, 'd', i), 0) < prev:
            waits[('d', i)] = prev
            self.waited[(qeng, 'd', i)] = prev
        self.dcnt[i] += 16
        tok = ('d', i, self.dcnt[i])
        sem = self.dsem[i]
        wl = [(self._semof(k), v) for k, v in waits.items()]

        def run(e, wl=wl, sem=sem, out=out, in_=in_):
            for s, v in wl:
                e.wait_ge(s, v)
            e.dma_start(out=out, in_=in_).then_inc(sem, 16)
        self.q[qeng].append(run)
        self.ninst[qeng] += 1 + len(wl)
        self._record(tok, reads, writes)
        return tok

    def barrier(self):
        for eng in self.ALLQ:
            wl = []
            for e in self.CE:
                if e != eng and self.cnt[e] > self.waited.get((eng, 'c', e), 0):
                    wl.append((self.sem[e], self.cnt[e]))
                    self.waited[(eng, 'c', e)] = self.cnt[e]
            for i, c in enumerate(self.dcnt):
                if c > self.waited.get((eng, 'd', i), 0):
                    wl.append((self.dsem[i], c))
                    self.waited[(eng, 'd', i)] = c

            def run(e, wl=wl):
                for s, v in wl:
                    e.wait_ge(s, v)
            self.q[eng].append(run)
        self.lastw.clear()
        self.readers.clear()

    def flush(self):
        nc = self.nc
        q = self.q
        with nc.Block() as block:
            @block.tensor
            def _(e):
                for f in q['pe']:
                    f(e)

            @block.scalar
            def _(e):
                for f in q['act']:
                    f(e)

            @block.vector
            def _(e):
                for f in q['dve']:
                    f(e)

            @block.gpsimd
            def _(e):
                for f in q['pool']:
                    f(e)

            @block.sync
            def _(e):
                for f in q['sp']:
                    f(e)
        self.q = {e: [] for e in self.ALLQ}

    def sync(self):
        self.barrier()
        self.flush()


class Ctx:
    debug = False

    def tap(self, name, ap, shape, dt, keys):
        if not self.debug:
            return
        d = self.nc.dram_tensor("tap_" + name, list(shape), dt, kind="ExternalOutput").ap()
        self.S.dma('sp', d, ap, reads=list(keys))


_UID = [0]


def uniq(n):
    _UID[0] += 1
    return '%s_%d' % (n, _UID[0])


def I(name, *args, **kw):
    return lambda e: getattr(e, name)(*args, **kw)


def mm_group(S, ps_ap, pskey, pairs, reads):
    n = len(pairs)
    fns = [(I('matmul', ps_ap, lhsT=l, rhs=r, start=(i == 0), stop=(i == n - 1)))
           for i, (l, r) in enumerate(pairs)]
    S.ops('pe', fns, reads=reads, writes=[pskey])


def norm_pass(C, src, gidx, hnT, tok0, ntile):
    nc, S = C.nc, C.S
    srcv = src.rearrange("(c p) t -> p c t", p=128)
    with ExitStack() as st:
        T = lambda n, s, d=F32: st.enter_context(nc.sbuf_tensor(uniq(n), s, d))
        ht = [T("n_ht%d" % i, [128, 8, 512]) for i in range(2)]
        sq = [T("n_sq%d" % i, [128, 8, 512], BF16) for i in range(2)]
        rs = [T("n_rs%d" % i, [128, 512]) for i in range(2)]
        pss = [st.enter_context(nc.psum_tensor(uniq("n_ps%d" % i), [128, 512], F32)) for i in range(2)]
        for i in range(ntile):
            b = i % 2
            t0 = tok0 + i * 512
            S.dma('sp', ht[b][:], srcv[:, :, t0:t0 + 512], writes=[('ht', b)])
            S.op('act', I('activation', sq[b][:], ht[b][:], AF.Square), reads=[('ht', b)], writes=[('sq', b)])
            mm_group(S, pss[b][:], ('nps', b), [(C.ones[:], sq[b][:, c, :]) for c in range(8)], [('sq', b)])
            S.op('act', I('activation', rs[b][:], pss[b][:], AF.Sqrt, scale=1.0 / DM, bias=C.epsc[:, 0:1]),
                 reads=[('nps', b)], writes=[('rs', b)])
            S.op('dve', I('reciprocal', rs[b][:], rs[b][:]), reads=[('rs', b)], writes=[('rs', b)])
            for c in range(8):
                eng = 'dve'
                S.op(eng, I('scalar_tensor_tensor',
                    hnT[:, c, i * 512:(i + 1) * 512], in0=ht[b][:, c, :], scalar=C.gains[:, gidx, c:c + 1],
                    in1=rs[b][:], op0=ALU.mult, op1=ALU.mult), reads=[('ht', b), ('rs', b)], writes=[('hn', i, c)])
        S.sync()


def load_consts(C, st):
    nc, S = C.nc, C.S
    T = lambda n, s, d=F32: st.enter_context(nc.sbuf_tensor(uniq(n), s, d))
    C.ones = T("c_ones", [128, 128], BF16)
    C.ident = T("c_ident", [128, 128], BF16)
    C.gains = T("c_gains", [128, 5, 8])
    C.epsc = T("c_eps", [128, 1])
    S.dma('sp', C.ones[:], C.din['ones'], writes=['c1'])
    S.dma('sp', C.ident[:], C.din['ident'], writes=['c2'])
    S.dma('sp', C.gains[:], C.din['gains'], writes=['c3'])
    S.op('dve', I('memset', C.epsc[:], EPS), writes=['c4'])
    S.sync()


def proj_phase(C, hnT, w_ap, jobs, ntok=SEQ):
    nc, S = C.nc, C.S
    wv = w_ap.rearrange("(c p) n -> p c n", p=128)
    with ExitStack() as st:
        T = lambda n, s, d=F32: st.enter_context(nc.sbuf_tensor(uniq(n), s, d))
        wf = [T("p_wf%d" % i, [128, 8, 512]) for i in range(2)]
        wb = [T("p_wb%d" % i, [128, 8, 512], BF16) for i in range(2)]
        wsw = [T("p_ws%d" % i, [128, 8, 512], BF16) for i in range(2)]
        psum = [st.enter_context(nc.psum_tensor(uniq("p_ps%d" % i), [128, 512], F32)) for i in range(6)]
        C.pp = 0

        def next_ps():
            i = C.pp % 6
            C.pp += 1
            return psum[i], ('pps', i)
        ntt = ntok // 512

        def prep(ji):
            job = jobs[ji]
            b = ji % 2
            c0 = job['c0']
            S.dma('sp', wf[b][:], wv[:, :, c0:c0 + 512], writes=[('wf', b)])
            S.op('pool', I('tensor_copy', wb[b][:], wf[b][:]), reads=[('wf', b)], writes=[('wb', b)])
            if job['role'] == 'rot':
                src = wf[b][:].rearrange("p c (h two j) -> p c h two j", two=2, j=32)
                dst = wsw[b][:].rearrange("p c (h two j) -> p c h two j", two=2, j=32)
                for c in range(8):
                    S.op('act', I('copy', dst[:, c, :, 0, :], src[:, c, :, 1, :]),
                         reads=[('wf', b)], writes=[('wsa', b, c)])
                    S.op('act', I('copy', dst[:, c, :, 1, :], src[:, c, :, 0, :]),
                         reads=[('wf', b)], writes=[('wsb', b, c)])
        if jobs:
            prep(0)
        for ji, job in enumerate(jobs):
            b = ji % 2
            if ji + 1 < len(jobs):
                prep(ji + 1)
            role = job['role']
            if role in ('fm', 'both', 'rot'):
                for sub in range(4):
                    for tt in range(ntt):
                        ps, pk = next_ps()
                        mm_group(S, ps[:], pk, [(wb[b][:, c, sub * 128:(sub + 1) * 128], hnT[:, c, tt * 512:(tt + 1) * 512])
                                                for c in range(8)], [('wb', b)])
                        if role == 'rot':
                            ps2, pk2 = next_ps()
                            mm_group(S, ps2[:], pk2, [(wsw[b][:, c, sub * 128:(sub + 1) * 128], hnT[:, c, tt * 512:(tt + 1) * 512])
                                                      for c in range(8)],
                                     [('wsa', b, c) for c in range(8)] + [('wsb', b, c) for c in range(8)])
                            job['epi_fm'](sub, tt, ps, pk, ps2, pk2)
                        else:
                            job['epi_fm'](sub, tt, ps, pk)
            if role in ('tm', 'both'):
                for t in range(ntok // 128):
                    ps, pk = next_ps()
                    mm_group(S, ps[:], pk, [(hnT[:, c, t * 128:(t + 1) * 128], wb[b][:, c, :]) for c in range(8)], [('wb', b)])
                    job['epi_tm'](t, ps, pk)
        S.sync()


class Stager:
    def __init__(self, C, st, name, shape, dt, n=4):
        self.C = C
        self.bufs = [st.enter_context(C.nc.sbuf_tensor(uniq("%s%d" % (name, i)), shape, dt)) for i in range(n)]
        self.name = name
        self.i = 0

    def next(self):
        i = self.i % len(self.bufs)
        self.i += 1
        return self.bufs[i], (self.name, i)

    def store(self, dst_ap, buf_ap, key, wkeys=()):
        self.C.S.dma('pool', dst_ap, buf_ap, reads=[key], writes=list(wkeys))


def phase_A0(C):
    nc, S, D = C.nc, C.S, C.dsc
    with ExitStack() as st:
        T = lambda n, s, d=F32: st.enter_context(nc.sbuf_tensor(uniq(n), s, d))
        hnT = T("a0_hn", [128, 8, SEQ], BF16)
        norm_pass(C, C.din['xT'], 0, hnT, 0, 8)
        rot = T("a0_rot", [128, 2, SEQ])
        retn = T("a0_retn", [128, 1024])
        S.dma('sp', rot[:], C.din['rot'], writes=['rot'])
        S.dma('sp', retn[:], C.din['retn'], writes=['retn'])
        sb = Stager(C, st, "a0_sb", [128, 512], BF16, 4)
        t1 = [T("a0_t1%d" % i, [128, 512]) for i in range(2)]
        t2 = [T("a0_t2%d" % i, [128, 512]) for i in range(2)]
        cnt = [0]

        def epi_rot(dst, kscale):
            def f(sub, tt, ps, pk, ps2, pk2, dst=dst, kscale=kscale):
                i = cnt[0] % 2
                cnt[0] += 1
                tsl = slice(tt * 512, (tt + 1) * 512)
                S.op('dve', I('scalar_tensor_tensor', t1[i][:], in0=ps[:], scalar=kscale, in1=rot[:, 0, tsl],
                                                             op0=ALU.mult, op1=ALU.mult), reads=[pk, 'rot'], writes=[('t1', i)])
                S.op('dve', I('scalar_tensor_tensor', t2[i][:], in0=ps2[:], scalar=kscale, in1=rot[:, 1, tsl],
                                                             op0=ALU.mult, op1=ALU.mult), reads=[pk2, 'rot'], writes=[('t2', i)])
                buf, bk = sb.next()
                S.op('dve', I('tensor_tensor', buf[:], t1[i][:], t2[i][:], op=ALU.add),
                     reads=[('t1', i), ('t2', i)], writes=[bk])
                sb.store(dst[sub * 128:(sub + 1) * 128, tsl], buf[:], bk)
            return f

        def epi_fm_scale(dst, scale):
            def f(sub, tt, ps, pk, dst=dst, scale=scale):
                buf, bk = sb.next()
                S.op('act', I('activation', buf[:], ps[:], AF.Identity, scale=scale), reads=[pk], writes=[bk])
                sb.store(dst[sub * 128:(sub + 1) * 128, tt * 512:(tt + 1) * 512], buf[:], bk)
            return f

        def epi_tm_copy(dst, cb):
            def f(t, ps, pk, dst=dst, cb=cb):
                buf, bk = sb.next()
                S.op('act', I('copy', buf[:], ps[:]), reads=[pk], writes=[bk])
                sb.store(dst[t * 128:(t + 1) * 128, cb * 512:(cb + 1) * 512], buf[:], bk)
            return f

        def epi_tm_gate(dst, cb):
            def f(t, ps, pk, dst=dst, cb=cb):
                i = cnt[0] % 2
                cnt[0] += 1
                S.op('act', I('activation', t1[i][:], ps[:], AF.Silu), reads=[pk], writes=[('t1', i)])
                buf, bk = sb.next()
                S.op('dve', I('tensor_tensor', buf[:], t1[i][:], retn[:, cb * 512:(cb + 1) * 512], op=ALU.mult),
                     reads=[('t1', i), 'retn'], writes=[bk])
                sb.store(dst[t * 128:(t + 1) * 128, cb * 512:(cb + 1) * 512], buf[:], bk)
            return f
        jobs = [dict(c0=0, role='rot', epi_fm=epi_rot(D['qaT'], 1.0)),
                dict(c0=512, role='rot', epi_fm=epi_rot(D['kaT'], 0.125)),
                dict(c0=1024, role='tm', epi_tm=epi_tm_copy(D['va'], 0)),
                dict(c0=1536, role='tm', epi_tm=epi_tm_copy(D['va'], 1)),
                dict(c0=2048, role='tm', epi_tm=epi_tm_gate(D['ga'], 0)),
                dict(c0=2560, role='tm', epi_tm=epi_tm_gate(D['ga'], 1)),
                dict(c0=3072, role='fm', epi_fm=epi_fm_scale(D['qbT'], 0.125)),
                dict(c0=3584, role='fm', epi_fm=epi_fm_scale(D['kbT'], 1.0)),
                dict(c0=4096, role='tm', epi_tm=epi_tm_copy(D['vb'], 0))]
        if JOBSEL is not None:
            jobs = [jobs[i] for i in JOBSEL]
        proj_phase(C, hnT, C.din['w_in0'], jobs)


def phase_outproj(C, yT_d, KC, w_ap, h_src, h_dst):
    nc, S = C.nc, C.S
    wv = w_ap.rearrange("(c p) n -> p c n", p=128)
    with ExitStack() as st:
        T = lambda n, s, d=F32: st.enter_context(nc.sbuf_tensor(uniq(n), s, d))
        yT = T("o_y", [128, KC, SEQ], BF16)
        yv = yT_d.rearrange("(c p) t -> p c t", p=128)
        for c in range(KC):
            S.dma('sp', yT[:, c, :], yv[:, c, :], writes=[('y', c)])
        wf = [T("o_wf%d" % i, [128, KC, 128]) for i in range(2)]
        wb = [T("o_wb%d" % i, [128, KC, 128], BF16) for i in range(2)]
        hb = [T("o_h%d" % i, [128, 512]) for i in range(3)]
        psum = [st.enter_context(nc.psum_tensor(uniq("o_ps%d" % i), [128, 512], F32)) for i in range(4)]
        k = 0

        def prep(dmb):
            b = dmb % 2
            S.dma('sp', wf[b][:], wv[:, :, dmb * 128:(dmb + 1) * 128], writes=[('wf', b)])
            S.op('pool', I('tensor_copy', wb[b][:], wf[b][:]), reads=[('wf', b)], writes=[('wb', b)])
        prep(0)
        for dmb in range(8):
            b = dmb % 2
            if dmb + 1 < 8:
                prep(dmb + 1)
            for tt in range(8):
                pi = k % 4
                hi = k % 3
                k += 1
                tsl = slice(tt * 512, (tt + 1) * 512)
                rsl = slice(dmb * 128, (dmb + 1) * 128)
                S.dma('sp', hb[hi][:], h_src[rsl, tsl], reads=[('hd', dmb, tt)], writes=[('hb', hi)])
                mm_group(S, psum[pi][:], ('ops', pi), [(wb[b][:, c, :], yT[:, c, tsl]) for c in range(KC)],
                         [('wb', b)] + [('y', c) for c in range(KC)])
                S.op('dve', I('tensor_tensor', hb[hi][:], psum[pi][:], hb[hi][:], op=ALU.add),
                     reads=[('ops', pi), ('hb', hi)], writes=[('hb', hi)])
                S.dma('pool', h_dst[rsl, tsl], hb[hi][:], reads=[('hb', hi)], writes=[('hd', dmb, tt)])
        S.sync()


def phase_ffn(C, layer, hT):
    nc, S = C.nc, C.S
    ST = 2048
    NJ = DFF // 128
    wup = C.din['w_up'][layer].rearrange("(c p) n -> p c n", p=128)
    wdn = C.din['w_down'][layer].rearrange("(j p) n -> p j n", p=128)
    with ExitStack() as st0:
        T0 = lambda n, s, d=F32: st0.enter_context(nc.sbuf_tensor(uniq(n), s, d))
        m = T0("f_m", [128, NJ, ST], BF16)
        halo = T0("f_halo", [128, NJ, 2, 2])
        cp = T0("f_cp", [128, 4, 44])
        S.dma('sp', cp[:], C.din['convp'][:, layer], writes=['cp'])
        S.op('dve', I('memset', halo[:], 0.0), writes=['halo'])
        S.sync()
        for sti in range(SEQ // ST):
            tok0 = sti * ST
            with ExitStack() as st:
                T = lambda n, s, d=F32: st.enter_context(nc.sbuf_tensor(uniq(n), s, d))
                hnT = T("f_hn", [128, 8, ST], BF16)
                norm_pass(C, hT, 1 + 2 * layer, hnT, tok0, ST // 512)
                u = [T("f_u%d" % i, [128, ST + 2]) for i in range(2)]
                cc = [T("f_c%d" % i, [128, ST]) for i in range(2)]
                wf = [T("f_wf%d" % i, [128, 8, 2, 128]) for i in range(2)]
                wb = [T("f_wb%d" % i, [128, 8, 2, 128], BF16) for i in range(2)]
                psum = [st.enter_context(nc.psum_tensor(uniq("f_ps%d" % i), [128, 512], F32)) for i in range(8)]
                def prep_up(j):
                    b = j % 2
                    for ab in range(2):
                        c0 = ab * DFF + j * 128
                        S.dma('sp', wf[b][:, :, ab, :], wup[:, :, c0:c0 + 128], writes=[('wf', b, ab)])
                    S.op('pool', I('tensor_copy', wb[b][:], wf[b][:]), reads=[('wf', b, 0), ('wf', b, 1)],
                         writes=[('wb', b)])
                prep_up(0)
                for j in range(NJ):
                    b = j % 2
                    if j + 1 < NJ:
                        prep_up(j + 1)
                    for ab in range(2):
                        S.op('pool', I('tensor_copy', u[ab][:, 0:2], halo[:, j, ab, :]),
                             reads=['halo', ('hl', j, ab)], writes=[('u', ab)])
                        for tt in range(ST // 512):
                            pi = ab * 4 + tt
                            mm_group(S, psum[pi][:], ('fps', pi),
                                     [(wb[b][:, c, ab, :], hnT[:, c, tt * 512:(tt + 1) * 512]) for c in range(8)], [('wb', b)])
                            S.op('act', I('copy', u[ab][:, 2 + tt * 512:2 + (tt + 1) * 512], psum[pi][:]),
                                 reads=[('fps', pi)], writes=[('u', ab, tt)])
                        ukeys = [('u', ab)] + [('u', ab, tt) for tt in range(ST // 512)]
                        jb = ab * NJ + j
                        S.op('pool', I('tensor_copy', halo[:, j, ab, :], u[ab][:, ST:ST + 2]),
                             reads=ukeys, writes=[('hl', j, ab)])
                        S.op('act', I('activation', cc[ab][:], u[ab][:, 2:ST + 2], AF.Identity,
                                                                         scale=cp[:, 2, jb:jb + 1], bias=cp[:, 3, jb:jb + 1]),
                             reads=ukeys + ['cp'], writes=[('cc', ab)])
                        eng = 'dve'
                        S.op(eng, I('scalar_tensor_tensor', cc[ab][:], in0=u[ab][:, 1:ST + 1], scalar=cp[:, 1, jb:jb + 1],
                                                                                 in1=cc[ab][:], op0=ALU.mult, op1=ALU.add),
                             reads=ukeys + [('cc', ab), 'cp'], writes=[('cc', ab)])
                        S.op(eng, I('scalar_tensor_tensor', cc[ab][:], in0=u[ab][:, 0:ST], scalar=cp[:, 0, jb:jb + 1],
                                                                                 in1=cc[ab][:], op0=ALU.mult, op1=ALU.add),
                             reads=ukeys + [('cc', ab), 'cp'], writes=[('cc', ab)])
                    S.op('act', I('activation', cc[0][:], cc[0][:], AF.Silu), reads=[('cc', 0)], writes=[('cc', 0)])
                    S.op('dve', I('tensor_tensor', m[:, j, :], cc[0][:], cc[1][:], op=ALU.mult),
                         reads=[('cc', 0), ('cc', 1)], writes=[('m', j)])
                S.sync()
            with ExitStack() as st:
                T = lambda n, s, d=F32: st.enter_context(nc.sbuf_tensor(uniq(n), s, d))
                wf = [T("g_wf%d" % i, [128, NJ, 128]) for i in range(2)]
                wb = [T("g_wb%d" % i, [128, NJ, 128], BF16) for i in range(2)]
                hb = [T("g_h%d" % i, [128, 512]) for i in range(3)]
                psum = [st.enter_context(nc.psum_tensor(uniq("g_ps%d" % i), [128, 512], F32)) for i in range(4)]
                k = 0

                def prep_dn(dmb):
                    b = dmb % 2
                    S.dma('sp', wf[b][:, 0:11, :], wdn[:, 0:11, dmb * 128:(dmb + 1) * 128], writes=[('wf', b, 0)])
                    S.dma('sp', wf[b][:, 11:22, :], wdn[:, 11:22, dmb * 128:(dmb + 1) * 128], writes=[('wf', b, 1)])
                    S.op('pool', I('tensor_copy', wb[b][:], wf[b][:]), reads=[('wf', b, 0), ('wf', b, 1)], writes=[('wb', b)])
                prep_dn(0)
                for dmb in range(8):
                    b = dmb % 2
                    if dmb + 1 < 8:
                        prep_dn(dmb + 1)
                    for tt in range(ST // 512):
                        pi = k % 4
                        hi = k % 3
                        k += 1
                        tsl = slice(tok0 + tt * 512, tok0 + (tt + 1) * 512)
                        rsl = slice(dmb * 128, (dmb + 1) * 128)
                        S.dma('sp', hb[hi][:], hT[rsl, tsl], writes=[('hb', hi)])
                        mm_group(S, psum[pi][:], ('gps', pi), [(wb[b][:, j, :], m[:, j, tt * 512:(tt + 1) * 512]) for j in range(NJ)],
                                 [('wb', b)])
                        S.op('dve', I('tensor_tensor', hb[hi][:], psum[pi][:], hb[hi][:], op=ALU.add),
                             reads=[('gps', pi), ('hb', hi)], writes=[('hb', hi)])
                        S.dma('pool', hT[rsl, tsl], hb[hi][:], reads=[('hb', hi)])
                S.sync()


def phase_final(C, hT, outT):
    nc, S = C.nc, C.S
    with ExitStack() as st:
        T = lambda n, s, d=F32: st.enter_context(nc.sbuf_tensor(uniq(n), s, d))
        ht = [T("z_ht%d" % i, [128, 8, 512]) for i in range(2)]
        sq = [T("z_sq%d" % i, [128, 8, 512], BF16) for i in range(2)]
        rs = [T("z_rs%d" % i, [128, 512]) for i in range(2)]
        pss = [st.enter_context(nc.psum_tensor(uniq("z_ps%d" % i), [128, 512], F32)) for i in range(2)]
        srcv = hT.rearrange("(c p) t -> p c t", p=128)
        dstv = outT.rearrange("(c p) t -> p c t", p=128)
        for i in range(8):
            b = i % 2
            tsl = slice(i * 512, (i + 1) * 512)
            hk = [('ht', b, c) for c in range(8)]
            S.dma('sp', ht[b][:], srcv[:, :, tsl], writes=hk)
            S.op('act', I('activation', sq[b][:], ht[b][:], AF.Square), reads=hk, writes=[('sq', b)])
            mm_group(S, pss[b][:], ('nps', b), [(C.ones[:], sq[b][:, c, :]) for c in range(8)], [('sq', b)])
            S.op('act', I('activation', rs[b][:], pss[b][:], AF.Sqrt, scale=1.0 / DM, bias=C.epsc[:, 0:1]),
                 reads=[('nps', b)], writes=[('rs', b)])
            S.op('dve', I('reciprocal', rs[b][:], rs[b][:]), reads=[('rs', b)], writes=[('rs', b)])
            for c in range(8):
                eng = 'dve'
                S.op(eng, I('scalar_tensor_tensor',
                    ht[b][:, c, :], in0=ht[b][:, c, :], scalar=C.gains[:, 4, c:c + 1],
                    in1=rs[b][:], op0=ALU.mult, op1=ALU.mult), reads=[('ht', b, c), ('rs', b), ('sq', b)], writes=[('ht', b, c)])
            S.dma('pool', dstv[:, :, tsl], ht[b][:], reads=hk)
        S.sync()


PHASES = []


def build(debug=False, stop_after=None, feed=(), skip=()):
    nc = bass.Bass("TRN2", target_bir_lowering=False)
    C = Ctx()
    C.nc = nc
    C.debug = debug
    kindS = "ExternalOutput" if debug else "Internal"
    C.din = {}
    C.dsc = {}

    def din(name, shape, dt=F32):
        C.din[name] = nc.dram_tensor(name, shape, dt, kind="ExternalInput").ap()

    def dsc(name, shape, dt=BF16):
        C.dsc[name] = nc.dram_tensor(name, shape, dt, kind=("ExternalInput" if name in feed else kindS)).ap()
    din('xT', [DM, SEQ])
    din('w_in0', [DM, 4608])
    din('w_out0', [1536, DM])
    din('w_in1', [DM, 4096])
    din('w_out1', [DM, DM])
    din('w_up', [2, DM, 2 * DFF])
    din('w_down', [2, DFF, DM])
    din('gains', [128, 5, 8])
    din('convp', [128, 2, 4, 44])
    din('retn', [128, 1024])
    din('rot', [128, 2, SEQ])
    din('ones', [128, 128], BF16)
    din('ident', [128, 128], BF16)
    din('dect', [128, 8, 128])
    din('gq', [128, 8, 128])
    din('gk', [128, 8, 64])
    din('cdr', [128, 4])
    din('biasT', [128, 24, 256])
    din('mask2', [128, 256])
    din('lbrep', [128, 2, 1024])
    din('lbfm', [128, 2, 8])
    din('hgn', [128, 1024])
    din('t1m', [128, 128], BF16)
    din('t2m', [128, 128], BF16)
    for n in ('qaT', 'kaT', 'qbT', 'kbT'):
        dsc(n, [512, SEQ])
    dsc('va', [SEQ, 1024])
    dsc('ga', [SEQ, 1024])
    dsc('vb', [SEQ, 512])
    dsc('yT', [1536, SEQ])
    dsc('hT', [DM, SEQ], F32)
    dsc('q1T', [1024, SEQ])
    dsc('k1T', [1024, SEQ])
    dsc('lfh', [SEQ, 1024])
    dsc('lfl', [SEQ, 1024])
    dsc('k1', [SEQ, 1024])
    dsc('v1', [SEQ, 1024])
    dsc('g1', [SEQ, 1024])
    dsc('y1T', [1024, SEQ])
    if debug:
        dsc('dbg_ret', [SEQ, 1024], F32)
    outT = nc.dram_tensor('outT', [DM, SEQ], F32, kind="ExternalOutput").ap()
    with ExitStack() as st:
        C.S = Sched(nc, st)
        load_consts(C, st)
        plan = [
            ('A0', lambda: phase_A0(C)),
            ('B0', lambda: phase_B0(C)),
            ('C0', lambda: phase_C0(C)),
            ('D0', lambda: phase_outproj(C, C.dsc['yT'], 12, C.din['w_out0'], C.din['xT'], C.dsc['hT'])),
            ('E0', lambda: phase_ffn(C, 0, C.dsc['hT'])),
            ('A1', lambda: phase_A1(C, C.dsc['hT'])),
            ('B1', lambda: phase_B1(C)),
            ('D1', lambda: phase_outproj(C, C.dsc['y1T'], 8, C.din['w_out1'], C.dsc['hT'], C.dsc['hT'])),
            ('E1', lambda: phase_ffn(C, 1, C.dsc['hT'])),
            ('Z', lambda: phase_final(C, C.dsc['hT'], outT)),
        ]
        for name, fn in plan:
            if name in skip:
                continue
            fn()
            if stop_after == name:
                break
    print("ninst", C.S.ninst, "cnt", C.S.cnt, "dcnt max", max(C.S.dcnt))
    return nc


C_SKIP = set()
JOBSEL = None
B0_LEVEL = 9
TAPN = 0
B0_SUB = 9


def host_inputs(inputs, b):
    f = np.float32
    x = np.asarray(inputs['x'], f)
    d = {}
    d['xT'] = np.ascontiguousarray(x[b].T)
    d['w_in0'] = np.ascontiguousarray(inputs['even_w_in'][0], f)
    d['w_out0'] = np.ascontiguousarray(inputs['even_w_out'][0], f)
    d['w_in1'] = np.ascontiguousarray(inputs['odd_w_in'][0], f)
    d['w_out1'] = np.ascontiguousarray(inputs['odd_w_out'][0], f)
    d['w_up'] = np.ascontiguousarray(inputs['ffn_w_up'], f)
    d['w_down'] = np.ascontiguousarray(inputs['ffn_w_down'], f)
    g = np.stack([inputs['mix_norm'][0], inputs['ffn_norm'][0], inputs['mix_norm'][1], inputs['ffn_norm'][1],
                  inputs['final_norm']], 0).astype(f)
    d['gains'] = np.ascontiguousarray(g.reshape(5, 8, 128).transpose(2, 0, 1))
    cw = np.asarray(inputs['ffn_conv_w'], f)
    cb = np.asarray(inputs['ffn_conv_b'], f)
    cp = np.concatenate([cw, cb[:, None, :]], 1)
    d['convp'] = np.ascontiguousarray(cp.reshape(2, 4, 44, 128).transpose(3, 0, 1, 2))
    d['retn'] = np.ascontiguousarray(np.broadcast_to(np.asarray(inputs['ret_norm'], f)[0][None, :], (128, 1024)))
    j = np.arange(128) % 64
    inv = (10000.0 ** (-np.arange(0, 64, 2, dtype=np.float32) / 64)).astype(f)
    ang = np.arange(SEQ, dtype=f)[None, :] * inv[j % 32][:, None]
    cos = np.cos(ang).astype(f)
    sin = np.sin(ang).astype(f)
    sgn = np.where(j < 32, -1.0, 1.0).astype(f)[:, None]
    d['rot'] = np.ascontiguousarray(np.stack([cos, sin * sgn], 1))
    d['ones'] = np.ones((128, 128), ml_dtypes.bfloat16)
    gam = 1.0 - 2.0 ** (-5.0 - np.arange(8, dtype=np.float64))
    ii = np.arange(128)
    diff = ii[None, :] - ii[:, None]
    dec = np.where(diff[:, None, :] >= 0, gam[None, :, None] ** np.maximum(diff, 0)[:, None, :], 0.0)
    d['dect'] = np.ascontiguousarray(dec.astype(f))
    hp = 2 * np.arange(4)[None, :] + (np.arange(128) // 64)[:, None]
    d['gq'] = np.ascontiguousarray(np.broadcast_to((gam[:, None] ** (ii[None, :] + 1.0))[None], (128, 8, 128)).astype(f))
    d['gk'] = np.ascontiguousarray(np.broadcast_to((gam[None, :] ** (127.0 - ii[:, None]))[:, :, None], (128, 8, 64)).astype(f))
    d['cdr'] = np.ascontiguousarray((gam[hp] ** 128.0).astype(f))
    d['ident'] = np.eye(128).astype(ml_dtypes.bfloat16)
    lb = np.asarray(inputs['hgrn_lb'], f)
    d['lbrep'] = np.ascontiguousarray(np.broadcast_to(lb[None], (128, 2, 1024)))
    d['lbfm'] = np.ascontiguousarray(lb.reshape(2, 8, 128).transpose(2, 0, 1))
    d['hgn'] = np.ascontiguousarray(np.broadcast_to(np.asarray(inputs['hgrn_norm'], f)[0][None, :], (128, 1024)))
    jj = np.arange(128)
    same = (jj[:, None] // 64) == (jj[None, :] // 64)
    d['t1m'] = np.ascontiguousarray((same & (jj[:, None] <= jj[None, :])).astype(ml_dtypes.bfloat16))
    d['t2m'] = np.ascontiguousarray((same & (jj[:, None] > jj[None, :])).astype(ml_dtypes.bfloat16))
    rb = np.asarray(inputs['rel_bias'], f)
    cc_ = np.arange(128)[:, None]
    aa_ = np.arange(128)[None, :]
    dist2 = np.stack([aa_ - cc_, 128 + aa_ - cc_], 0)
    valid = np.stack([aa_ >= cc_, aa_ <= cc_], 0)
    bt = np.zeros((128, 3, 8, 2, 128), f)
    for bi, r in enumerate(DIL_R):
        dd_ = (np.maximum(dist2, 0) * r).astype(np.int64)
        df = dd_.astype(np.float32)
        large = 16 + (np.log(np.maximum(df, np.float32(1.0)) / np.float32(16)) / np.float32(math.log(2048 / 16)) * np.float32(16)).astype(np.int32)
        large = np.minimum(large, 31)
        bucket = np.where(dd_ < 16, dd_, large)
        gb = rb[bucket]
        gb = np.where(valid[..., None], gb, 0.0)
        bt[:, bi] = gb.transpose(1, 3, 0, 2)
    d['biasT'] = np.ascontiguousarray(bt.reshape(128, 24, 256))
    d['mask2'] = np.ascontiguousarray(valid.transpose(1, 0, 2).reshape(128, 256).astype(f))
    return d


_NC = {}


def kernel(**inputs):
    if 'nc' not in _NC:
        _NC['nc'] = build()
    nc = _NC['nc']
    in_maps = [host_inputs(inputs, c % 4) for c in range(8)]
    res = run_bass_kernel_spmd(nc, in_maps, core_ids=list(range(8)))
    out = np.stack([np.ascontiguousarray(res.results[b]['outT'].T) for b in range(4)], 0)
    return out.astype(np.float32)


def phase_B0(C):
    nc, S, D = C.nc, C.S, C.dsc
    with ExitStack() as st:
        T = lambda n, s, d=F32: st.enter_context(nc.sbuf_tensor(uniq(n), s, d))
        P = lambda n, s, d=F32: st.enter_context(nc.psum_tensor(uniq(n), s, d))
        dect = T("r_dec", [128, 8, 128])
        gq = T("r_gq", [128, 8, 128])
        gk = T("r_gk", [128, 8, 64])
        cdr = T("r_cd", [128, 4])
        S.dma('sp', dect[:], C.din['dect'], writes=['dect'])
        S.dma('sp', gq[:], C.din['gq'], writes=['gq'])
        S.dma('sp', gk[:], C.din['gk'], writes=['gk'])
        S.dma('sp', cdr[:], C.din['cdr'], writes=['cdr'])
        Sf = T("r_S", [128, 4, 256])
        Sb = [T("r_Sb%d" % i, [128, 4, 256], BF16) for i in range(2)]
        S.op('dve', I('memset', Sf[:], 0.0), writes=['Sf'])
        S.op('dve', I('memset', Sb[0][:], 0.0), writes=[('Sb', 0)])
        qT = [T("r_q%d" % i, [128, 8, 512], BF16) for i in range(2)]
        for i in range(2):
            S.op("dve", I('memset', qT[i][:], 0.0), writes=[("qT", i)])
        kT = [T("r_k%d" % i, [128, 4, 512], BF16) for i in range(2)]
        va = [T("r_v%d" % i, [128, 4, 1024], BF16) for i in range(2)]
        ga = [T("r_g%d" % i, [128, 4, 1024], BF16) for i in range(2)]
        kout = [T("r_ko%d" % i, [128, 8, 64], BF16) for i in range(2)]
        qin = [T("r_qi%d" % i, [128, 8, 128], BF16) for i in range(2)]
        sc = [T("r_sc%d" % i, [128, 8, 128], BF16) for i in range(2)]
        xs = T("r_xs", [128, 8, 128])
        sqb = T("r_sq", [128, 8, 128])
        xn = T("r_xn", [128, 8, 128])
        yb = T("r_yb", [128, 8, 128], BF16)
        stt = T("r_stat", [128, 8, 8])
        yst = [T("r_yst%d" % i, [128, 8, 512], BF16) for i in range(2)]
        ps_kt = P("r_pkt", [128, 512], BF16)
        ps_s = [P("r_ps%d" % i, [128, 512]) for i in range(2)]
        ps_o = [P("r_po%d" % i, [128, 512]) for i in range(2)]
        ps_inc = [P("r_pi%d" % i, [128, 512]) for i in range(2)]
        ps_yt = P("r_pyt", [128, 1024], BF16)
        qv = D['qaT'].rearrange("(g two d) t -> two d g t", two=2, d=64)
        kv = D['kaT'].rearrange("(g p) t -> p g t", p=128)
        vv = D['va'].rearrange("(c p) e -> p c e", p=128)
        gv = D['ga'].rearrange("(c p) e -> p c e", p=128)
        yv = D['yT'].rearrange("(h p) t -> p h t", p=128)
        sbi = 0
        for sci in range(SEQ // 512):
            b = sci % 2
            tsl = slice(sci * 512, (sci + 1) * 512)
            for half in range(2):
                S.dma('sp', qT[b][64 * half:64 * half + 64, half:8:2, :], qv[half][:, :, tsl], reads=[('qT', b)], writes=[('qT', b, half)])
            S.dma('sp', kT[b][:], kv[:, :, tsl], writes=[('kT', b)])
            S.dma('sp', va[b][:], vv[:, sci * 4:(sci + 1) * 4, :], writes=[('va', b)])
            S.dma('sp', ga[b][:], gv[:, sci * 4:(sci + 1) * 4, :], writes=[('ga', b)])
            for c4 in range(4):
                if B0_LEVEL < 1:
                    continue
                n = sci * 4 + c4
                kb = n % 2
                csl = slice(c4 * 128, (c4 + 1) * 128)
                S.ops('pe', [(I('transpose', ps_kt[:, g * 128:(g + 1) * 128], kT[b][:, g, csl], C.ident[:]))
                             for g in range(4)], reads=[('kT', b)], writes=['pskt'])
                if B0_SUB >= 2:
                  S.op('dve', I('tensor_tensor', kout[kb][:], ps_kt[:].rearrange("p (h d) -> p h d", d=64), gk[:], op=ALU.mult),
                     reads=['pskt', 'gk'], writes=[('kout', kb)])
                if B0_SUB < 3:
                    continue
                S.op('dve', I('tensor_tensor', qin[kb][:], qT[b][:, :, csl], gq[:], op=ALU.mult),
                     reads=[('qT', b, 0), ('qT', b, 1), 'gq'], writes=[('qin', kb)])
                if B0_LEVEL >= 2:
                    fns = []
                    for h in range(8):
                        g, r0 = h // 2, 64 * (h % 2)
                        fns.append(I('matmul',
                            ps_s[h % 2][:, (h // 2) * 128:(h // 2 + 1) * 128], lhsT=kT[b][:, g, csl],
                            rhs=qT[b][:, h, csl], start=True, stop=True))
                    S.ops('pe', fns, reads=[('kT', b), ('qT', b, 0), ('qT', b, 1)], writes=[('pss', 0), ('pss', 1)])
                    for i in range(2):
                        S.op('dve', I('tensor_tensor', sc[kb][:, i:8:2, :],
                                                                   ps_s[i][:].rearrange("p (h t) -> p h t", t=128),
                                                                   dect[:, i:8:2, :], op=ALU.mult),
                             reads=[('pss', i), 'dect'], writes=[('sc', kb, i)])
                if B0_LEVEL >= 3:
                    fns = []
                    for h in range(8):
                        g, r0 = h // 2, 64 * (h % 2)
                        osl = slice((h // 2) * 128, (h // 2 + 1) * 128)
                        fns.append(I('matmul',
                            ps_o[h % 2][:, osl], lhsT=sc[kb][:, h, :], rhs=va[b][:, c4, h * 128:(h + 1) * 128], start=True, stop=False))
                        fns.append(I('matmul',
                            ps_o[h % 2][:, osl], lhsT=qin[kb][:, h, :],
                            rhs=Sb[sbi][:, g, (h % 2) * 128:(h % 2 + 1) * 128], start=False, stop=True))
                    S.ops('pe', fns, reads=[('sc', kb, 0), ('sc', kb, 1), ('va', b), ('qin', kb), ('Sb', sbi)],
                          writes=[('pso', 0), ('pso', 1)])
                if n == TAPN:
                    C.tap('kout', kout[kb][:], [128, 8, 64], BF16, [('kout', kb)])
                    C.tap('qin', qin[kb][:], [128, 8, 128], BF16, [('qin', kb)])
                    C.tap('sc', sc[kb][:], [128, 8, 128], BF16, [('sc', kb, 0), ('sc', kb, 1)])
                    C.tap('qz', qT[b][:], [128, 8, 512], BF16, [('qT', b, 0), ('qT', b, 1)])
                    C.tap('sb', Sb[sbi][:], [128, 4, 256], BF16, [('Sb', sbi)])
                for i in range(2 if B0_LEVEL >= 4 else 0):
                    fns = []
                    for g in range(2 * i, 2 * i + 2):
                        fns.append(I('matmul',
                            ps_inc[i][:, (g % 2) * 256:(g % 2 + 1) * 256],
                            lhsT=kout[kb][:, 2 * g:2 * g + 2, :].rearrange("p h d -> p (h d)"),
                            rhs=va[b][:, c4, g * 256:(g + 1) * 256], start=True, stop=True))
                    S.ops('pe', fns, reads=[('kout', kb), ('va', b)], writes=[('psi', i)])
                    for g in range(2 * i, 2 * i + 2):
                        S.op('dve', I('scalar_tensor_tensor',
                            Sf[:, g, :], in0=Sf[:, g, :], scalar=cdr[:, g:g + 1], in1=ps_inc[i][:, (g % 2) * 256:(g % 2 + 1) * 256],
                            op0=ALU.mult, op1=ALU.add), reads=[('psi', i), 'cdr', ('Sf', g)], writes=[('Sf', g)])
                if B0_SUB >= 4:
                  S.op('pool', I('tensor_copy', Sb[1 - sbi][:], Sf[:]), reads=[('Sf', g) for g in range(4)] + ['Sf'],
                     writes=[('Sb', 1 - sbi)])
                sbi = 1 - sbi
                if B0_LEVEL < 5:
                    continue
                for i in range(2):
                    S.op('act', I('copy', xs[:, i:8:2, :], ps_o[i][:].rearrange("p (h t) -> p h t", t=128)),
                         reads=[('pso', i)], writes=[('xs', i)])
                xk = [('xs', 0), ('xs', 1)]
                if 'dbg_ret' in D:
                    S.dma('sp', D['dbg_ret'][n * 128:(n + 1) * 128, :], xs[:].rearrange("p h t -> p (h t)"), reads=xk)
                S.op('dve', I('tensor_reduce', stt[:, 0, :], xs[:], axis=AX.X, op=ALU.add), reads=xk, writes=['sums'])
                S.op('act', I('activation', sqb[:], xs[:], AF.Square), reads=xk, writes=['sqb'])
                S.op('dve', I('tensor_reduce', stt[:, 1, :], sqb[:], axis=AX.X, op=ALU.add), reads=['sqb'], writes=['sumsq'])
                S.op('dve', I('tensor_scalar', stt[:, 2, :], stt[:, 0, :], 1.0 / 128, None, op0=ALU.mult),
                     reads=['sums'], writes=['mean'])
                S.op('dve', I('tensor_tensor', stt[:, 3, :], stt[:, 2, :], stt[:, 2, :], op=ALU.mult), reads=['mean'], writes=['msq'])
                S.op('dve', I('scalar_tensor_tensor', stt[:, 4, :], in0=stt[:, 1, :], scalar=1.0 / 128, in1=stt[:, 3, :],
                                                             op0=ALU.mult, op1=ALU.subtract), reads=['sumsq', 'msq'], writes=['var'])
                S.op('act', I('activation', stt[:, 5, :], stt[:, 4, :], AF.Sqrt, bias=C.epsc[:, 0:1]), reads=['var'], writes=['sd'])
                S.op('dve', I('reciprocal', stt[:, 6, :], stt[:, 5, :]), reads=['sd'], writes=['rstd'])
                S.op('dve', I('tensor_tensor', xn[:], xs[:], stt[:, 2, :].unsqueeze(2).to_broadcast([128, 8, 128]), op=ALU.subtract),
                     reads=xk + ['mean'], writes=['xn'])
                S.op('dve', I('tensor_tensor', xn[:], xn[:], stt[:, 6, :].unsqueeze(2).to_broadcast([128, 8, 128]), op=ALU.mult),
                     reads=['xn', 'rstd'], writes=['xn'])
                S.op('dve', I('tensor_tensor', yb[:], xn[:], ga[b][:, c4, :].rearrange("p (h t) -> p h t", t=128), op=ALU.mult),
                     reads=['xn', ('ga', b)], writes=['yb'])
                S.ops('pe', [(I('transpose', ps_yt[:, h * 128:(h + 1) * 128], yb[:, h, :], C.ident[:])) for h in range(8)],
                      reads=['yb'], writes=['psyt'])
                S.op('act', I('copy', yst[b][:, :, csl], ps_yt[:].rearrange("p (h t) -> p h t", t=128)),
                     reads=['psyt'], writes=[('yst', b, c4)])
            S.dma('pool', yv[:, 0:8, tsl], yst[b][:], reads=[('yst', b, c4) for c4 in range(4)])
        S.sync()


DIL_R = (1, 4, 16)


def phase_C0(C):
    nc, S, D = C.nc, C.S, C.dsc
    with ExitStack() as st:
        T = lambda n, s, d=F32: st.enter_context(nc.sbuf_tensor(uniq(n), s, d))
        P = lambda n, s, d=F32: st.enter_context(nc.psum_tensor(uniq(n), s, d))
        EBT = T("c_ebt", [128, 24, 256], BF16)
        with ExitStack() as st2:
            bt = st2.enter_context(nc.sbuf_tensor(uniq("c_bt"), [128, 24, 256], F32))
            mk = st2.enter_context(nc.sbuf_tensor(uniq("c_mk"), [128, 256], F32))
            S.dma('sp', bt[:], C.din['biasT'], writes=['bt'])
            S.dma('sp', mk[:], C.din['mask2'], writes=['mk'])
            S.op('act', I('activation', bt[:], bt[:], AF.Exp), reads=['bt'], writes=['bt'])
            S.op('dve', I('tensor_tensor', EBT[:], bt[:], mk[:].unsqueeze(1).to_broadcast([128, 24, 256]), op=ALU.mult),
                 reads=['bt', 'mk'], writes=['ebt'])
            S.sync()
        kT = T("c_k", [128, SEQ], BF16)
        qz = T("c_q", [128, 2, SEQ], BF16)
        vp = [T("c_v%d" % i, [128, 32, 2, 64], BF16) for i in range(2)]
        onesb = T("c_ones", [128, 64], BF16)
        accn = T("c_an", [64, 2, SEQ])
        accd = T("c_ad", [64, 2, SEQ])
        NROT = 4
        pe_ = [T("c_pe%d" % i, [128, 2, 128], BF16) for i in range(NROT)]
        pt_ = [T("c_pt%d" % i, [128, 2, 128], BF16) for i in range(NROT)]
        rden = [T("c_rd%d" % i, [64, 512]) for i in range(2)]
        ystg = [T("c_ys%d" % i, [64, 512], BF16) for i in range(2)]
        ps = [P("c_ps%d" % i, [128, 256]) for i in range(NROT)]
        po = [P("c_po%d" % i, [64, 512]) for i in range(2)]
        pd = [P("c_pd%d" % i, [64, 512]) for i in range(2)]
        S.op('dve', I('memset', qz[:], 0.0), writes=['qz'])
        S.op('dve', I('memset', onesb[:], 1.0), writes=['onesb'])
        qv = D['qbT'].rearrange("(g two d) t -> g two d t", two=2, d=64)
        kv = D['kbT'].rearrange("(g p) t -> g p t", p=128)
        st_ = dict(cnt=0, bcnt=0, vcnt=0)
        DEPTH = 2

        def sl(start, r):
            return slice(start, start + 127 * r + 1, r)

        for g in range(4):
            S.dma('sp', kT[:], kv[g], writes=['kT'])
            for half in range(2):
                S.dma('sp', qz[64 * half:64 * half + 64, half, :], qv[g, half], reads=['qz'], writes=[('qz', half)])
            vinfo = {}

            def issue_v(bi, g=g, vinfo=vinfo):
                r = DIL_R[bi]
                nb = SEQ // (128 * r)
                vb_ = vp[st_['vcnt'] % 2]
                vkey = ('vp', st_['vcnt'] % 2)
                st_['vcnt'] += 1
                vsrc = D['vb'].rearrange("(n a r) (g2 hh d) -> a r n g2 hh d", a=128, r=r, hh=2, d=64)
                vkeys = []
                for rr in range(r):
                    step = 8 if nb > 8 else nb
                    for n0 in range(0, nb, step):
                        S.dma('sp', vb_[:, rr * nb + n0:rr * nb + n0 + step, :, :], vsrc[:, rr, n0:n0 + step, g, :, :],
                              writes=[(vkey, rr, n0)])
                        vkeys.append((vkey, rr, n0))
                vinfo[bi] = (vb_, vkeys)

            tiles = []
            for bi, r in enumerate(DIL_R):
                nb = SEQ // (128 * r)
                first = True
                for hh in range(2):
                    if r == 1:
                        batches = [[(0, n) for n in range(n0, n0 + 4)] for n0 in range(0, nb, 4)]
                    else:
                        batches = [[(rho, n) for rho in range(r0, r0 + 4)] for n in range(nb) for r0 in range(0, r, 4)]
                    for batch in batches:
                        bb = st_['bcnt'] % 2
                        st_['bcnt'] += 1
                        for slot, (rho, n) in enumerate(batch):
                            tiles.append(dict(bi=bi, r=r, nb=nb, hh=hh, bb=bb, slot=slot, rho=rho, n=n, batch=batch,
                                              last=(slot == len(batch) - 1), first_of_branch=first))
                            first = False

            def front(t, g=g):
                r, n, rho, hh = t['r'], t['n'], t['rho'], t['hh']
                h = 2 * g + hh
                pi = ei = st_['cnt'] % NROT
                st_['cnt'] += 1
                t['ei'] = ei
                nk = 1 if n == 0 else 2
                t['nk'] = nk
                qsl = sl(128 * n * r + rho, r)
                fns = [I('matmul', ps[pi][:, 0:128], lhsT=kT[:, qsl], rhs=qz[:, hh, qsl], start=True, stop=True)]
                if nk == 2:
                    ksl = sl(128 * (n - 1) * r + rho, r)
                    fns.append(I('matmul', ps[pi][:, 128:256], lhsT=kT[:, ksl], rhs=qz[:, hh, qsl], start=True, stop=True))
                S.ops('pe', fns, reads=['kT', ('qz', 0), ('qz', 1)], writes=[('ps', pi)])
                S.op('act', I('activation', pe_[ei][:, 0:nk, :], ps[pi][:, 0:nk * 128].rearrange("p (s q) -> p s q", q=128), AF.Exp),
                     reads=[('ps', pi)], writes=[('pe', ei)])
                S.op('dve', I('tensor_tensor', pt_[ei][:, 0:nk, :], pe_[ei][:, 0:nk, :],
                              EBT[:, t['bi'] * 8 + h, 0:nk * 128].rearrange("p (s q) -> p s q", q=128), op=ALU.mult),
                     reads=[('pe', ei), 'ebt'], writes=[('pt', ei)])

            def back(t, g=g, vinfo=vinfo):
                r, n, rho, hh, bb, slot, nb, bi = t['r'], t['n'], t['rho'], t['hh'], t['bb'], t['slot'], t['nb'], t['bi']
                ei, nk = t['ei'], t['nk']
                vb_, vkeys = vinfo[bi]
                ti = rho * nb + n
                osl = slice(slot * 128, (slot + 1) * 128)
                fo = [I('matmul', po[bb][:, osl], lhsT=vb_[:, ti, hh, :], rhs=pt_[ei][:, 0, :], start=True, stop=(nk == 1))]
                fd = [I('matmul', pd[bb][:, osl], lhsT=onesb[:], rhs=pt_[ei][:, 0, :], start=True, stop=(nk == 1))]
                if nk == 2:
                    fo.append(I('matmul', po[bb][:, osl], lhsT=vb_[:, ti - 1, hh, :], rhs=pt_[ei][:, 1, :], start=False, stop=True))
                    fd.append(I('matmul', pd[bb][:, osl], lhsT=onesb[:], rhs=pt_[ei][:, 1, :], start=False, stop=True))
                S.ops('pe', fo + fd, reads=[('pt', ei), 'onesb'] + vkeys, writes=[('po', bb), ('pd', bb)])
                if not t['last']:
                    return
                rho0, n0 = t['batch'][0]
                if r == 1:
                    tok0 = 128 * n0
                    dn = accn[:, hh, tok0:tok0 + 512]
                    dd = accd[:, hh, tok0:tok0 + 512]
                    sn, sd = po[bb][:], pd[bb][:]
                else:
                    base = 128 * n0 * r
                    dn = accn[:, hh, base:base + 128 * r].rearrange("p (a r) -> p a r", r=r)[:, :, rho0:rho0 + 4]
                    dd = accd[:, hh, base:base + 128 * r].rearrange("p (a r) -> p a r", r=r)[:, :, rho0:rho0 + 4]
                    sn = po[bb][:].rearrange("p (s a) -> p a s", a=128)
                    sd = pd[bb][:].rearrange("p (s a) -> p a s", a=128)
                if bi == 0:
                    S.op('dve', I('tensor_copy', dn, sn), reads=[('po', bb)], writes=[('accn', hh)])
                    S.op('dve', I('tensor_copy', dd, sd), reads=[('pd', bb)], writes=[('accd', hh)])
                else:
                    S.op('dve', I('tensor_tensor', dn, sn, dn, op=ALU.add), reads=[('po', bb), ('accn', hh)], writes=[('accn', hh)])
                    S.op('dve', I('tensor_tensor', dd, sd, dd, op=ALU.add), reads=[('pd', bb), ('accd', hh)], writes=[('accd', hh)])

            issue_v(0)
            for i in range(len(tiles) + DEPTH):
                if i < len(tiles):
                    front(tiles[i])
                if i - DEPTH >= 0:
                    tb = tiles[i - DEPTH]
                    back(tb)
                    if tb['first_of_branch'] and tb['bi'] + 1 < len(DIL_R):
                        issue_v(tb['bi'] + 1)
            for hh in range(2):
                h = 2 * g + hh
                for tt in range(8):
                    i = tt % 2
                    tsl = slice(tt * 512, (tt + 1) * 512)
                    S.op('dve', I('reciprocal', rden[i][:], accd[:, hh, tsl]), reads=[('accd', hh)], writes=[('rden', i)])
                    S.op('dve', I('tensor_tensor', ystg[i][:], accn[:, hh, tsl], rden[i][:], op=ALU.mult),
                         reads=[('accn', hh), ('rden', i)], writes=[('ystg', i)])
                    S.dma('pool', D['yT'][1024 + 64 * h:1024 + 64 * h + 64, tsl], ystg[i][:], reads=[('ystg', i)])
        S.sync()


def phase_A1(C, hT):
    nc, S, D = C.nc, C.S, C.dsc
    with ExitStack() as st:
        T = lambda n, s, d=F32: st.enter_context(nc.sbuf_tensor(uniq(n), s, d))
        hnT = T("a1_hn", [128, 8, SEQ], BF16)
        norm_pass(C, hT, 2, hnT, 0, 8)
        lbr = T("a1_lbr", [128, 1024])
        omr = T("a1_omr", [128, 1024])
        hgn = T("a1_hgn", [128, 1024])
        lbf = T("a1_lbf", [128, 8])
        omf = T("a1_omf", [128, 8])
        with ExitStack() as st2:
            raw = st2.enter_context(nc.sbuf_tensor(uniq("a1_raw"), [128, 2, 1024], F32))
            rawf = st2.enter_context(nc.sbuf_tensor(uniq("a1_rawf"), [128, 2, 8], F32))
            S.dma('sp', raw[:], C.din['lbrep'], writes=['raw'])
            S.dma('sp', rawf[:], C.din['lbfm'], writes=['rawf'])
            S.dma('sp', hgn[:], C.din['hgn'], writes=['hgn'])
            S.op('dve', I('tensor_tensor', lbr[:], raw[:, 1, :], raw[:, 0, :], op=ALU.subtract), reads=['raw'], writes=['lbr'])
            S.op('act', I('activation', lbr[:], lbr[:], AF.Sigmoid), reads=['lbr'], writes=['lbr'])
            S.op('dve', I('tensor_scalar', omr[:], lbr[:], -1.0, 1.0, op0=ALU.mult, op1=ALU.add), reads=['lbr'], writes=['omr'])
            S.op('dve', I('tensor_tensor', lbf[:], rawf[:, 1, :], rawf[:, 0, :], op=ALU.subtract), reads=['rawf'], writes=['lbf'])
            S.op('act', I('activation', lbf[:], lbf[:], AF.Sigmoid), reads=['lbf'], writes=['lbf'])
            S.op('dve', I('tensor_scalar', omf[:], lbf[:], -1.0, 1.0, op0=ALU.mult, op1=ALU.add), reads=['lbf'], writes=['omf'])
            S.sync()
        sb = Stager(C, st, "a1_sb", [128, 512], BF16, 6)
        sf = Stager(C, st, "a1_sf", [128, 512], F32, 3)
        t1 = [T("a1_t1%d" % i, [128, 512]) for i in range(2)]
        t2 = [T("a1_t2%d" % i, [128, 512]) for i in range(2)]
        cnt = [0]

        def epi_fm_silu(dst, cb):
            def f(sub, tt, ps, pk):
                buf, bk = sb.next()
                S.op('act', I('activation', buf[:], ps[:], AF.Silu), reads=[pk], writes=[bk])
                sb.store(dst[cb * 512 + sub * 128:cb * 512 + (sub + 1) * 128, tt * 512:(tt + 1) * 512], buf[:], bk)
            return f

        def epi_fm_k(dst, cb):
            def f(sub, tt, ps, pk):
                i = cnt[0] % 2
                cnt[0] += 1
                ci = cb * 4 + sub
                S.op('act', I('activation', t1[i][:], ps[:], AF.Sigmoid, scale=-1.0), reads=[pk], writes=[('t1', i)])
                buf, bk = sb.next()
                S.op('dve', I('tensor_scalar', buf[:], t1[i][:], omf[:, ci:ci + 1], None, op0=ALU.mult), reads=[('t1', i)], writes=[bk])
                sb.store(dst[ci * 128:(ci + 1) * 128, tt * 512:(tt + 1) * 512], buf[:], bk)
            return f

        def epi_tm_f(cb):
            def f(t, ps, pk):
                i = cnt[0] % 2
                cnt[0] += 1
                csl = slice(cb * 512, (cb + 1) * 512)
                rsl = slice(t * 128, (t + 1) * 128)
                S.op('act', I('activation', t1[i][:], ps[:], AF.Sigmoid), reads=[pk], writes=[('t1', i)])
                S.op('dve', I('tensor_tensor', t2[i][:], t1[i][:], omr[:, csl], op=ALU.mult), reads=[('t1', i)], writes=[('t2', i)])
                S.op('dve', I('tensor_tensor', t2[i][:], t2[i][:], lbr[:, csl], op=ALU.add), reads=[('t2', i)], writes=[('t2', i)])
                fb, fk = sf.next()
                S.op('act', I('activation', fb[:], t2[i][:], AF.Ln), reads=[('t2', i)], writes=[fk])
                hb_, hk_ = sb.next()
                S.op('dve', I('tensor_copy', hb_[:], fb[:]), reads=[fk], writes=[hk_])
                sb.store(D['lfh'][rsl, csl], hb_[:], hk_)
                lb_, lk_ = sb.next()
                S.op('dve', I('tensor_tensor', lb_[:], fb[:], hb_[:], op=ALU.subtract), reads=[fk, hk_], writes=[lk_])
                sb.store(D['lfl'][rsl, csl], lb_[:], lk_)
                buf, bk = sb.next()
                S.op('dve', I('tensor_scalar', buf[:], t2[i][:], -1.0, 1.0, op0=ALU.mult, op1=ALU.add), reads=[('t2', i)], writes=[bk])
                sb.store(D['k1'][rsl, csl], buf[:], bk)
            return f

        def epi_tm_copy(dst, cb):
            def f(t, ps, pk):
                buf, bk = sb.next()
                S.op('act', I('copy', buf[:], ps[:]), reads=[pk], writes=[bk])
                sb.store(dst[t * 128:(t + 1) * 128, cb * 512:(cb + 1) * 512], buf[:], bk)
            return f

        def epi_tm_gate(dst, cb):
            def f(t, ps, pk):
                i = cnt[0] % 2
                cnt[0] += 1
                S.op('act', I('activation', t1[i][:], ps[:], AF.Silu), reads=[pk], writes=[('t1', i)])
                buf, bk = sb.next()
                S.op('dve', I('tensor_tensor', buf[:], t1[i][:], hgn[:, cb * 512:(cb + 1) * 512], op=ALU.mult),
                     reads=[('t1', i)], writes=[bk])
                sb.store(dst[t * 128:(t + 1) * 128, cb * 512:(cb + 1) * 512], buf[:], bk)
            return f
        jobs = []
        for cb in range(2):
            jobs.append(dict(c0=cb * 512, role='fm', epi_fm=epi_fm_silu(D['q1T'], cb)))
        for cb in range(2):
            jobs.append(dict(c0=1024 + cb * 512, role='both', epi_fm=epi_fm_k(D['k1T'], cb), epi_tm=epi_tm_f(cb)))
        for cb in range(2):
            jobs.append(dict(c0=2048 + cb * 512, role='tm', epi_tm=epi_tm_copy(D['v1'], cb)))
        for cb in range(2):
            jobs.append(dict(c0=3072 + cb * 512, role='tm', epi_tm=epi_tm_gate(D['g1'], cb)))
        proj_phase(C, hnT, C.din['w_in1'], jobs)


def phase_B1(C):
    nc, S, D = C.nc, C.S, C.dsc
    with ExitStack() as st:
        T = lambda n, s, d=F32: st.enter_context(nc.sbuf_tensor(uniq(n), s, d))
        P = lambda n, s, d=F32: st.enter_context(nc.psum_tensor(uniq(n), s, d))
        T1 = T("h_t1", [128, 128], BF16)
        T2 = T("h_t2", [128, 128], BF16)
        S.dma('sp', T1[:], C.din['t1m'], writes=['T1'])
        S.dma('sp', T2[:], C.din['t2m'], writes=['T2'])
        NSB = 256
        qT = [T("h_q%d" % i, [128, 8, NSB], BF16) for i in range(2)]
        kT = [T("h_k%d" % i, [128, 8, NSB], BF16) for i in range(2)]
        lf = [T("h_lf%d" % i, [128, 2, 2, 1024], BF16) for i in range(2)]
        kk = [T("h_kk%d" % i, [128, 2, 1024], BF16) for i in range(2)]
        vv = [T("h_v%d" % i, [128, 2, 1024], BF16) for i in range(2)]
        gg = [T("h_g%d" % i, [128, 2, 1024], BF16) for i in range(2)]
        eB = T("h_eB", [128, 8, 128])
        eNB = T("h_eNB", [128, 8, 128])
        eRB = T("h_eRB", [128, 1024])
        qlo = [T("h_qlo%d" % i, [128, 8, 128], BF16) for i in range(2)]
        qhi = [T("h_qhi%d" % i, [128, 8, 128], BF16) for i in range(2)]
        kt = [T("h_kt%d" % i, [128, 8, 128], BF16) for i in range(2)]
        klo = [T("h_klo%d" % i, [128, 1024], BF16) for i in range(2)]
        khi = [T("h_khi%d" % i, [128, 1024], BF16) for i in range(2)]
        sc = [T("h_sc%d" % i, [128, 8, 128], BF16) for i in range(2)]
        Sf = T("h_S", [128, 8, 128])
        Sbp = [T("h_Sbp%d" % i, [128, 8, 128], BF16) for i in range(2)]
        Sbm = [T("h_Sbm%d" % i, [128, 8, 128], BF16) for i in range(2)]
        sq = T("h_sq", [128, 8, 128])
        xn = T("h_xn", [128, 8, 128])
        yb = T("h_yb", [128, 8, 128], BF16)
        stt = T("h_stt", [128, 3, 8])
        yst = [T("h_yst%d" % i, [128, 8, NSB], BF16) for i in range(2)]
        bA = [P("h_pA%d" % i, [128, 512]) for i in range(2)]
        bB = [P("h_pB%d" % i, [128, 512]) for i in range(2)]
        bC = [P("h_pC%d" % i, [128, 512]) for i in range(2)]
        bD = P("h_pD", [128, 1024], BF16)
        for i in range(2):
            S.op('dve', I('memset', qlo[i][:], 0.0), writes=[('qlo', i)])
            S.op('dve', I('memset', qhi[i][:], 0.0), writes=[('qhi', i)])
            S.op('dve', I('memset', klo[i][:], 0.0), writes=[('klo', i)])
            S.op('dve', I('memset', khi[i][:], 0.0), writes=[('khi', i)])
        S.op('dve', I('memset', Sf[:], 0.0), writes=[('Sf', h) for h in range(8)])
        S.op('dve', I('memset', Sbp[0][:], 0.0), writes=[('Sbp', 0, h) for h in range(8)])
        qv = D['q1T'].rearrange("(h p) t -> p h t", p=128)
        kv = D['k1T'].rearrange("(h p) t -> p h t", p=128)
        lvh = D['lfh'].rearrange("(c p) e -> p c e", p=128)
        lvl = D['lfl'].rearrange("(c p) e -> p c e", p=128)
        k2v = D['k1'].rearrange("(c p) e -> p c e", p=128)
        vv_ = D['v1'].rearrange("(c p) e -> p c e", p=128)
        gv = D['g1'].rearrange("(c p) e -> p c e", p=128)
        yv = D['y1T'].rearrange("(h p) t -> p h t", p=128)
        for sbi_ in range(SEQ // NSB):
            b = sbi_ % 2
            tsl = slice(sbi_ * NSB, (sbi_ + 1) * NSB)
            csl2 = slice(sbi_ * 2, sbi_ * 2 + 2)
            S.dma('sp', qT[b][:], qv[:, :, tsl], writes=[('qT', b)])
            S.dma('sp', kT[b][:], kv[:, :, tsl], writes=[('kT', b)])
            S.dma('sp', lf[b][:, 0], lvh[:, csl2, :], writes=[('lf', b, 0)])
            S.dma('sp', lf[b][:, 1], lvl[:, csl2, :], writes=[('lf', b, 1)])
            S.dma('sp', kk[b][:], k2v[:, csl2, :], writes=[('kk', b)])
            S.dma('sp', vv[b][:], vv_[:, csl2, :], writes=[('vv', b)])
            S.dma('sp', gg[b][:], gv[:, csl2, :], writes=[('gg', b)])
            for blk in range(2):
                n = sbi_ * 2 + blk
                p2 = n % 2
                bsl = slice(blk * 128, (blk + 1) * 128)
                for i in range(2):
                    lfk = [('lf', b, 0), ('lf', b, 1)]
                    S.ops('pe', [I('matmul', bA[i][:, (h % 4) * 128:(h % 4 + 1) * 128], lhsT=lf[b][:, hl, blk, h * 128:(h + 1) * 128], rhs=T1[:],
                                   start=(hl == 0), stop=(hl == 1)) for h in range(4 * i, 4 * i + 4) for hl in range(2)],
                          reads=lfk + ['T1'], writes=[('bA', i)])
                    S.ops('pe', [I('matmul', bB[i][:], lhsT=T2[:], rhs=lf[b][:, hl, blk, i * 512:(i + 1) * 512], start=(hl == 0), stop=(hl == 1))
                                 for hl in range(2)], reads=lfk + ['T2'], writes=[('bB', i)])
                    S.op('act', I('activation', eB[:, 4 * i:4 * i + 4, :], bA[i][:].rearrange("p (h t) -> p h t", t=128), AF.Exp),
                         reads=[('bA', i)], writes=[('eB', i)])
                    S.op('act', I('activation', eNB[:, 4 * i:4 * i + 4, :], bA[i][:].rearrange("p (h t) -> p h t", t=128), AF.Exp, scale=-1.0),
                         reads=[('bA', i)], writes=[('eNB', i)])
                    S.op('act', I('activation', eRB[:, i * 512:(i + 1) * 512], bB[i][:], AF.Exp), reads=[('bB', i)], writes=[('eRB', i)])
                ek = [('eB', 0), ('eB', 1)]
                S.op('dve', I('tensor_tensor', qlo[p2][:, :, 0:64], qT[b][:, :, blk * 128:blk * 128 + 64], eB[:, :, 0:64], op=ALU.mult),
                     reads=ek + [('qT', b), ('qlo', p2)], writes=[('qlo', p2)])
                S.op('dve', I('tensor_tensor', qhi[p2][:, :, 64:128], qT[b][:, :, blk * 128 + 64:blk * 128 + 128], eB[:, :, 64:128], op=ALU.mult),
                     reads=ek + [('qT', b), ('qhi', p2)], writes=[('qhi', p2)])
                S.op('dve', I('tensor_tensor', kt[p2][:], kT[b][:, :, bsl], eNB[:], op=ALU.mult),
                     reads=[('eNB', 0), ('eNB', 1), ('kT', b)], writes=[('kt', p2)])
                S.op('dve', I('tensor_tensor', klo[p2][0:64, :], kk[b][0:64, blk, :], eRB[0:64, :], op=ALU.mult),
                     reads=[('eRB', 0), ('eRB', 1), ('kk', b), ('klo', p2)], writes=[('klo', p2)])
                S.op('dve', I('tensor_tensor', khi[p2][64:128, :], kk[b][64:128, blk, :], eRB[64:128, :], op=ALU.mult),
                     reads=[('eRB', 0), ('eRB', 1), ('kk', b), ('khi', p2)], writes=[('khi', p2)])
                for i in range(2):
                    fns = []
                    for h in range(4 * i, 4 * i + 4):
                        o0 = (h % 4) * 128
                        fns.append(I('matmul', bA[i][:, o0:o0 + 64], lhsT=kt[p2][:, h, :], rhs=qlo[p2][:, h, 0:64], start=True, stop=True))
                        fns.append(I('matmul', bA[i][:, o0 + 64:o0 + 128], lhsT=kt[p2][:, h, :], rhs=qhi[p2][:, h, 64:128], start=True, stop=True))
                    S.ops('pe', fns, reads=[('kt', p2), ('qlo', p2), ('qhi', p2), ('eB', i), ('eNB', i)], writes=[('bA', i)])
                    S.op('dve', I('tensor_tensor', sc[p2][:, 4 * i:4 * i + 4, :], bA[i][:].rearrange("p (h t) -> p h t", t=128),
                                  T1[:].unsqueeze(1).to_broadcast([128, 4, 128]), op=ALU.mult), reads=[('bA', i), 'T1'], writes=[('sc', p2, i)])
                for half, (ksrc, kkey, dst_) in enumerate(((klo[p2], ('klo', p2), Sbm[p2]), (khi[p2], ('khi', p2), Sbp[1 - p2]))):
                    dkey = 'Sbm' if half == 0 else 'Sbp'
                    dpar = p2 if half == 0 else 1 - p2
                    col = 63 if half == 0 else 127
                    for i in range(2):
                        S.ops('pe', [I('matmul', bC[i][:, (h % 4) * 128:(h % 4 + 1) * 128], lhsT=ksrc[:, h * 128:(h + 1) * 128],
                                       rhs=vv[b][:, blk, h * 128:(h + 1) * 128], start=True, stop=True) for h in range(4 * i, 4 * i + 4)],
                              reads=[kkey, ('vv', b)], writes=[('bC', i)])
                        for h in range(4 * i, 4 * i + 4):
                            S.op('dve', I('scalar_tensor_tensor', Sf[:, h, :], in0=Sf[:, h, :], scalar=eB[:, h, col:col + 1],
                                          in1=bC[i][:, (h % 4) * 128:(h % 4 + 1) * 128], op0=ALU.mult, op1=ALU.add),
                                 reads=[('bC', i), ('eB', i), ('Sf', h)], writes=[('Sf', h)])
                        S.op('pool', I('tensor_copy', dst_[:, 4 * i:4 * i + 4, :], Sf[:, 4 * i:4 * i + 4, :]),
                             reads=[('Sf', h) for h in range(4 * i, 4 * i + 4)], writes=[(dkey, dpar, h) for h in range(4 * i, 4 * i + 4)])
                for i in range(2):
                    fns = []
                    for h in range(4 * i, 4 * i + 4):
                        osl = slice((h % 4) * 128, (h % 4 + 1) * 128)
                        fns.append(I('matmul', bB[i][:, osl], lhsT=sc[p2][:, h, :], rhs=vv[b][:, blk, h * 128:(h + 1) * 128], start=True, stop=False))
                        fns.append(I('matmul', bB[i][:, osl], lhsT=qlo[p2][:, h, :], rhs=Sbp[p2][:, h, :], start=False, stop=False))
                        fns.append(I('matmul', bB[i][:, osl], lhsT=qhi[p2][:, h, :], rhs=Sbm[p2][:, h, :], start=False, stop=True))
                    S.ops('pe', fns, reads=[('sc', p2, i), ('vv', b), ('qlo', p2), ('qhi', p2), ('eRB', i)]
                          + [('Sbp', p2, h) for h in range(4 * i, 4 * i + 4)] + [('Sbm', p2, h) for h in range(4 * i, 4 * i + 4)],
                          writes=[('bB', i)])
                    S.op('act', I('activation', sq[:, 4 * i:4 * i + 4, :], bB[i][:].rearrange("p (h t) -> p h t", t=128), AF.Square),
                         reads=[('bB', i)], writes=[('sq', i)])
                S.op('dve', I('tensor_reduce', stt[:, 0, :], sq[:], axis=AX.X, op=ALU.add), reads=[('sq', 0), ('sq', 1)], writes=['ss'])
                S.op('act', I('activation', stt[:, 1, :], stt[:, 0, :], AF.Sqrt, scale=1.0 / 128, bias=C.epsc[:, 0:1]), reads=['ss'], writes=['sd'])
                S.op('dve', I('reciprocal', stt[:, 2, :], stt[:, 1, :]), reads=['sd'], writes=['rstd'])
                for i in range(2):
                    S.op('dve', I('tensor_tensor', xn[:, 4 * i:4 * i + 4, :], bB[i][:].rearrange("p (h t) -> p h t", t=128),
                                  stt[:, 2, 4 * i:4 * i + 4].unsqueeze(2).to_broadcast([128, 4, 128]), op=ALU.mult),
                         reads=[('bB', i), 'rstd'], writes=[('xn', i)])
                S.op('dve', I('tensor_tensor', yb[:], xn[:], gg[b][:, blk, :].rearrange("p (h t) -> p h t", t=128), op=ALU.mult),
                     reads=[('xn', 0), ('xn', 1), ('gg', b)], writes=['yb'])
                S.ops('pe', [I('transpose', bD[:, h * 128:(h + 1) * 128], yb[:, h, :], C.ident[:]) for h in range(8)], reads=['yb'], writes=['bD'])
                S.op('act', I('copy', yst[b][:, :, bsl], bD[:].rearrange("p (h t) -> p h t", t=128)), reads=['bD'], writes=[('yst', b, blk)])
            S.dma('pool', yv[:, :, tsl], yst[b][:], reads=[('yst', b, 0), ('yst', b, 1)])
        S.sync()
```

```python
import math
import numpy as np
import ml_dtypes
from contextlib import ExitStack
import concourse.bass as bass
import concourse.mybir as mybir
from concourse.bass_utils import run_bass_kernel_spmd

F32 = mybir.dt.float32
BF16 = mybir.dt.bfloat16
AF = mybir.ActivationFunctionType
ALU = mybir.AluOpType
AX = mybir.AxisListType

SEQ = 4096
DM = 1024
DFF = 2816
EPS = 1e-6
STOP_AFTER = None


class Sched:
    CE = ('pe', 'act', 'dve', 'pool')
    ALLQ = ('pe', 'act', 'dve', 'pool', 'sp')

    def __init__(self, nc, stack, ndma=24):
        self.nc = nc
        self.sem = {e: stack.enter_context(nc.semaphore('s_' + e)) for e in self.CE}
        self.cnt = {e: 0 for e in self.CE}
        self.dsem = [stack.enter_context(nc.semaphore('d%d' % i)) for i in range(ndma)]
        self.dcnt = [0] * ndma
        self.dnext = 0
        self.waited = {}
        self.lastw = {}
        self.readers = {}
        self.q = {e: [] for e in self.ALLQ}
        self.ninst = {e: 0 for e in self.ALLQ}

    def _semof(self, key):
        return self.sem[key[1]] if key[0] == 'c' else self.dsem[key[1]]

    def _deps(self, eng, reads, writes):
        raw = set()
        toks = set()
        for k in reads:
            t = self.lastw.get(k)
            if t is not None:
                raw.add(t)
        for k in writes:
            t = self.lastw.get(k)
            if t is not None:
                toks.add(t)
            r = self.readers.get(k)
            if r:
                toks.update(r.values())
        waits = {}
        for t in raw | toks:
            kind, who, val = t
            if kind == 'c' and who == eng and t not in raw:
                continue
            key = (kind, who)
            if self.waited.get((eng,) + key, 0) >= val:
                continue
            if waits.get(key, 0) < val:
                waits[key] = val
        for key, val in waits.items():
            self.waited[(eng,) + key] = val
        return waits

    def _record(self, tok, reads, writes):
        for k in writes:
            self.lastw[k] = tok
            self.readers[k] = {}
        for k in reads:
            d = self.readers.setdefault(k, {})
            if tok[0] == 'c':
                d[('c', tok[1])] = tok
            else:
                d[tok] = tok

    def op(self, eng, fn, reads=(), writes=()):
        return self.ops(eng, [fn], reads, writes)

    def ops(self, eng, fns, reads=(), writes=()):
        waits = self._deps(eng, reads, writes)
        self.cnt[eng] += 1
        tok = ('c', eng, self.cnt[eng])
        sem = self.sem[eng]
        wl = [(self._semof(k), v) for k, v in waits.items()]

        def run(e, fns=fns, wl=wl, sem=sem):
            for s, v in wl:
                e.wait_ge(s, v)
            for f in fns[:-1]:
                f(e)
            fns[-1](e).then_inc(sem, 1)
        self.q[eng].append(run)
        self.ninst[eng] += len(fns) + len(wl)
        self._record(tok, reads, writes)
        return tok

    def dma(self, qeng, out, in_, reads=(), writes=()):
        i = self.dnext
        self.dnext = (i + 1) % len(self.dsem)
        waits = self._deps(qeng, reads, writes)
        prev = self.dcnt[i]
        if prev and self.waited.get((qeng, 'd', i), 0) < prev:
            waits[('d', i)] = prev
            self.waited[(qeng, 'd', i)] = prev
        self.dcnt[i] += 16
        tok = ('d', i, self.dcnt[i])
        sem = self.dsem[i]
        wl = [(self._semof(k), v) for k, v in waits.items()]

        def run(e, wl=wl, sem=sem, out=out, in_=in_):
            for s, v in wl:
                e.wait_ge(s, v)
            e.dma_start(out=out, in_=in_).then_inc(sem, 16)
        self.q[qeng].append(run)
        self.ninst[qeng] += 1 + len(wl)
        self._record(tok, reads, writes)
        return tok

    def barrier(self):
        for eng in self.ALLQ:
            wl = []
            for e in self.CE:
                if e != eng and self.cnt[e] > self.waited.get((eng, 'c', e), 0):
                    wl.append((self.sem[e], self.cnt[e]))
                    self.waited[(eng, 'c', e)] = self.cnt[e]
            for i, c in enumerate(self.dcnt):
                if c > self.waited.get((eng, 'd', i), 0):
                    wl.append((self.dsem[i], c))
                    self.waited[(eng, 'd', i)] = c

            def run(e, wl=wl):
                for s, v in wl:
                    e.wait_ge(s, v)
            self.q[eng].append(run)
        self.lastw.clear()
        self.readers.clear()

    def flush(self):
        nc = self.nc
        q = self.q
        with nc.Block() as block:
            @block.tensor
            def _(e):
                for f in q['pe']:
                    f(e)

            @block.scalar
            def _(e):
                for f in q['act']:
                    f(e)

            @block.vector
            def _(e):
                for f in q['dve']:
                    f(e)

            @block.gpsimd
            def _(e):
                for f in q['pool']:
                    f(e)

            @block.sync
            def _(e):
                for f in q['sp']:
                    f(e)
        self.q = {e: [] for e in self.ALLQ}

    def sync(self):
        self.barrier()
        self.flush()


class Ctx:
    debug = False

    def tap(self, name, ap, shape, dt, keys):
        if not self.debug:
            return
        d = self.nc.dram_tensor("tap_" + name, list(shape), dt, kind="ExternalOutput").ap()
        self.S.dma('sp', d, ap, reads=list(keys))


_UID = [0]


def uniq(n):
    _UID[0] += 1
    return '%s_%d' % (n, _UID[0])


def I(name, *args, **kw):
    return lambda e: getattr(e, name)(*args, **kw)


def mm_group(S, ps_ap, pskey, pairs, reads):
    n = len(pairs)
    fns = [(I('matmul', ps_ap, lhsT=l, rhs=r, start=(i == 0), stop=(i == n - 1)))
           for i, (l, r) in enumerate(pairs)]
    S.ops('pe', fns, reads=reads, writes=[pskey])


def norm_pass(C, src, gidx, hnT, tok0, ntile):
    nc, S = C.nc, C.S
    srcv = src.rearrange("(c p) t -> p c t", p=128)
    with ExitStack() as st:
        T = lambda n, s, d=F32: st.enter_context(nc.sbuf_tensor(uniq(n), s, d))
        ht = [T("n_ht%d" % i, [128, 8, 512]) for i in range(3)]
        sq = [T("n_sq%d" % i, [128, 8, 512], BF16) for i in range(3)]
        rs = [T("n_rs%d" % i, [128, 512]) for i in range(3)]
        pss = [st.enter_context(nc.psum_tensor(uniq("n_ps%d" % i), [128, 512], F32)) for i in range(3)]
        for i in range(ntile):
            b = i % 3
            t0 = tok0 + i * 512
            S.dma('sp', ht[b][:], srcv[:, :, t0:t0 + 512], writes=[('ht', b)])
            S.op('act', I('activation', sq[b][:], ht[b][:], AF.Square), reads=[('ht', b)], writes=[('sq', b)])
            mm_group(S, pss[b][:], ('nps', b), [(C.ones[:], sq[b][:, c, :]) for c in range(8)], [('sq', b)])
            S.op('act', I('activation', rs[b][:], pss[b][:], AF.Sqrt, scale=1.0 / DM, bias=C.epsc[:, 0:1]),
                 reads=[('nps', b)], writes=[('rs', b)])
            S.op('dve', I('reciprocal', rs[b][:], rs[b][:]), reads=[('rs', b)], writes=[('rs', b)])
            for c in range(8):
                eng = 'dve'
                S.op(eng, I('scalar_tensor_tensor',
                    hnT[:, c, i * 512:(i + 1) * 512], in0=ht[b][:, c, :], scalar=C.gains[:, gidx, c:c + 1],
                    in1=rs[b][:], op0=ALU.mult, op1=ALU.mult), reads=[('ht', b), ('rs', b)], writes=[('hn', i, c)])
        S.sync()


def load_consts(C, st):
    nc, S = C.nc, C.S
    T = lambda n, s, d=F32: st.enter_context(nc.sbuf_tensor(uniq(n), s, d))
    C.ones = T("c_ones", [128, 128], BF16)
    C.ident = T("c_ident", [128, 128], BF16)
    C.gains = T("c_gains", [128, 5, 8])
    C.epsc = T("c_eps", [128, 1])
    C.onec = T("c_one", [128, 1])
    S.dma('sp', C.ones[:], C.din['ones'], writes=['c1'])
    S.dma('sp', C.ident[:], C.din['ident'], writes=['c2'])
    S.dma('sp', C.gains[:], C.din['gains'], writes=['c3'])
    S.op('dve', I('memset', C.epsc[:], EPS), writes=['c4'])
    S.op('dve', I('memset', C.onec[:], 1.0), writes=['c5'])
    S.sync()


def proj_phase(C, hnT, w_ap, jobs, ntok=SEQ):
    nc, S = C.nc, C.S
    wv = w_ap.rearrange("(c p) n -> p c n", p=128)
    with ExitStack() as st:
        T = lambda n, s, d=F32: st.enter_context(nc.sbuf_tensor(uniq(n), s, d))
        wf = [T("p_wf%d" % i, [128, 8, 512]) for i in range(2)]
        wb = [T("p_wb%d" % i, [128, 8, 512], BF16) for i in range(2)]
        wsw = [T("p_ws%d" % i, [128, 8, 512], BF16) for i in range(2)]
        psum = [st.enter_context(nc.psum_tensor(uniq("p_ps%d" % i), [128, 512], F32)) for i in range(6)]
        C.pp = 0

        def next_ps():
            i = C.pp % 6
            C.pp += 1
            return psum[i], ('pps', i)
        ntt = ntok // 512

        def prep(ji):
            job = jobs[ji]
            b = ji % 2
            c0 = job['c0']
            S.dma('sp', wf[b][:], wv[:, :, c0:c0 + 512], writes=[('wf', b)])
            S.op('pool', I('tensor_copy', wb[b][:], wf[b][:]), reads=[('wf', b)], writes=[('wb', b)])
            if job['role'] == 'rot':
                src = wf[b][:].rearrange("p c (h two j) -> p c h two j", two=2, j=32)
                dst = wsw[b][:].rearrange("p c (h two j) -> p c h two j", two=2, j=32)
                for c in range(8):
                    S.op('act', I('copy', dst[:, c, :, 0, :], src[:, c, :, 1, :]),
                         reads=[('wf', b)], writes=[('wsa', b, c)])
                    S.op('act', I('copy', dst[:, c, :, 1, :], src[:, c, :, 0, :]),
                         reads=[('wf', b)], writes=[('wsb', b, c)])
        if jobs:
            prep(0)
        for ji, job in enumerate(jobs):
            b = ji % 2
            if ji + 1 < len(jobs):
                prep(ji + 1)
            role = job['role']
            if role in ('fm', 'both', 'rot'):
                for sub in range(4):
                    for tt in range(ntt):
                        ps, pk = next_ps()
                        mm_group(S, ps[:], pk, [(wb[b][:, c, sub * 128:(sub + 1) * 128], hnT[:, c, tt * 512:(tt + 1) * 512])
                                                for c in range(8)], [('wb', b)])
                        if role == 'rot':
                            ps2, pk2 = next_ps()
                            mm_group(S, ps2[:], pk2, [(wsw[b][:, c, sub * 128:(sub + 1) * 128], hnT[:, c, tt * 512:(tt + 1) * 512])
                                                      for c in range(8)],
                                     [('wsa', b, c) for c in range(8)] + [('wsb', b, c) for c in range(8)])
                            job['epi_fm'](sub, tt, ps, pk, ps2, pk2)
                        else:
                            job['epi_fm'](sub, tt, ps, pk)
            if role in ('tm', 'both'):
                for t in range(ntok // 128):
                    ps, pk = next_ps()
                    mm_group(S, ps[:], pk, [(hnT[:, c, t * 128:(t + 1) * 128], wb[b][:, c, :]) for c in range(8)], [('wb', b)])
                    job['epi_tm'](t, ps, pk)
        S.sync()


class Stager:
    def __init__(self, C, st, name, shape, dt, n=4):
        self.C = C
        self.bufs = [st.enter_context(C.nc.sbuf_tensor(uniq("%s%d" % (name, i)), shape, dt)) for i in range(n)]
        self.name = name
        self.i = 0

    def next(self):
        i = self.i % len(self.bufs)
        self.i += 1
        return self.bufs[i], (self.name, i)

    def store(self, dst_ap, buf_ap, key, wkeys=()):
        self.C.S.dma('pool', dst_ap, buf_ap, reads=[key], writes=list(wkeys))


def phase_A0(C):
    nc, S, D = C.nc, C.S, C.dsc
    with ExitStack() as st:
        T = lambda n, s, d=F32: st.enter_context(nc.sbuf_tensor(uniq(n), s, d))
        hnT = T("a0_hn", [128, 8, SEQ], BF16)
        norm_pass(C, C.din['xT'], 0, hnT, 0, 8)
        rot = T("a0_rot", [128, 2, SEQ])
        retn = T("a0_retn", [128, 1024])
        S.dma('sp', rot[:], C.din['rot'], writes=['rot'])
        S.dma('sp', retn[:], C.din['retn'], writes=['retn'])
        sb = Stager(C, st, "a0_sb", [128, 512], BF16, 4)
        t1 = [T("a0_t1%d" % i, [128, 512]) for i in range(2)]
        t2 = [T("a0_t2%d" % i, [128, 512]) for i in range(2)]
        cnt = [0]

        def epi_rot(dst, kscale):
            def f(sub, tt, ps, pk, ps2, pk2, dst=dst, kscale=kscale):
                i = cnt[0] % 2
                cnt[0] += 1
                tsl = slice(tt * 512, (tt + 1) * 512)
                S.op('dve', I('scalar_tensor_tensor', t1[i][:], in0=ps[:], scalar=kscale, in1=rot[:, 0, tsl],
                                                             op0=ALU.mult, op1=ALU.mult), reads=[pk, 'rot'], writes=[('t1', i)])
                S.op('dve', I('scalar_tensor_tensor', t2[i][:], in0=ps2[:], scalar=kscale, in1=rot[:, 1, tsl],
                                                             op0=ALU.mult, op1=ALU.mult), reads=[pk2, 'rot'], writes=[('t2', i)])
                buf, bk = sb.next()
                S.op('dve', I('tensor_tensor', buf[:], t1[i][:], t2[i][:], op=ALU.add),
                     reads=[('t1', i), ('t2', i)], writes=[bk])
                sb.store(dst[sub * 128:(sub + 1) * 128, tsl], buf[:], bk)
            return f

        def epi_fm_scale(dst, scale):
            def f(sub, tt, ps, pk, dst=dst, scale=scale):
                buf, bk = sb.next()
                S.op('act', I('activation', buf[:], ps[:], AF.Identity, scale=scale), reads=[pk], writes=[bk])
                sb.store(dst[sub * 128:(sub + 1) * 128, tt * 512:(tt + 1) * 512], buf[:], bk)
            return f

        def epi_tm_copy(dst, cb):
            def f(t, ps, pk, dst=dst, cb=cb):
                buf, bk = sb.next()
                S.op('act', I('copy', buf[:], ps[:]), reads=[pk], writes=[bk])
                sb.store(dst[t * 128:(t + 1) * 128, cb * 512:(cb + 1) * 512], buf[:], bk)
            return f

        def epi_tm_gate(dst, cb):
            def f(t, ps, pk, dst=dst, cb=cb):
                i = cnt[0] % 2
                cnt[0] += 1
                S.op('act', I('activation', t1[i][:], ps[:], AF.Silu), reads=[pk], writes=[('t1', i)])
                buf, bk = sb.next()
                S.op('dve', I('tensor_tensor', buf[:], t1[i][:], retn[:, cb * 512:(cb + 1) * 512], op=ALU.mult),
                     reads=[('t1', i), 'retn'], writes=[bk])
                sb.store(dst[t * 128:(t + 1) * 128, cb * 512:(cb + 1) * 512], buf[:], bk)
            return f
        jobs = [dict(c0=0, role='rot', epi_fm=epi_rot(D['qaT'], 1.0)),
                dict(c0=512, role='rot', epi_fm=epi_rot(D['kaT'], 0.125)),
                dict(c0=1024, role='tm', epi_tm=epi_tm_copy(D['va'], 0)),
                dict(c0=1536, role='tm', epi_tm=epi_tm_copy(D['va'], 1)),
                dict(c0=2048, role='tm', epi_tm=epi_tm_gate(D['ga'], 0)),
                dict(c0=2560, role='tm', epi_tm=epi_tm_gate(D['ga'], 1)),
                dict(c0=3072, role='fm', epi_fm=epi_fm_scale(D['qbT'], 0.125)),
                dict(c0=3584, role='fm', epi_fm=epi_fm_scale(D['kbT'], 1.0)),
                dict(c0=4096, role='tm', epi_tm=epi_tm_copy(D['vb'], 0))]
        if JOBSEL is not None:
            jobs = [jobs[i] for i in JOBSEL]
        proj_phase(C, hnT, C.din['w_in0'], jobs)


def phase_outproj(C, yT_d, KC, w_ap, h_src, h_dst):
    nc, S = C.nc, C.S
    wv = w_ap.rearrange("(c p) n -> p c n", p=128)
    with ExitStack() as st:
        T = lambda n, s, d=F32: st.enter_context(nc.sbuf_tensor(uniq(n), s, d))
        yT = T("o_y", [128, KC, SEQ], BF16)
        yv = yT_d.rearrange("(c p) t -> p c t", p=128)
        for c in range(KC):
            S.dma('sp', yT[:, c, :], yv[:, c, :], writes=[('y', c)])
        wf = [T("o_wf%d" % i, [128, KC, 128]) for i in range(2)]
        wb = [T("o_wb%d" % i, [128, KC, 128], BF16) for i in range(2)]
        hb = [T("o_h%d" % i, [128, 512]) for i in range(3)]
        psum = [st.enter_context(nc.psum_tensor(uniq("o_ps%d" % i), [128, 512], F32)) for i in range(4)]
        k = 0

        def prep(dmb):
            b = dmb % 2
            S.dma('sp', wf[b][:], wv[:, :, dmb * 128:(dmb + 1) * 128], writes=[('wf', b)])
            S.op('pool', I('tensor_copy', wb[b][:], wf[b][:]), reads=[('wf', b)], writes=[('wb', b)])
        prep(0)
        for dmb in range(8):
            b = dmb % 2
            if dmb + 1 < 8:
                prep(dmb + 1)
            for tt in range(8):
                pi = k % 4
                hi = k % 3
                k += 1
                tsl = slice(tt * 512, (tt + 1) * 512)
                rsl = slice(dmb * 128, (dmb + 1) * 128)
                S.dma('sp', hb[hi][:], h_src[rsl, tsl], reads=[('hd', dmb, tt)], writes=[('hb', hi)])
                mm_group(S, psum[pi][:], ('ops', pi), [(wb[b][:, c, :], yT[:, c, tsl]) for c in range(KC)],
                         [('wb', b)] + [('y', c) for c in range(KC)])
                S.op('dve', I('tensor_tensor', hb[hi][:], psum[pi][:], hb[hi][:], op=ALU.add),
                     reads=[('ops', pi), ('hb', hi)], writes=[('hb', hi)])
                S.dma('pool', h_dst[rsl, tsl], hb[hi][:], reads=[('hb', hi)], writes=[('hd', dmb, tt)])
        S.sync()


def phase_ffn(C, layer, hT):
    nc, S = C.nc, C.S
    ST = 2048
    NJ = DFF // 128
    wup = C.din['w_up'][layer].rearrange("(c p) n -> p c n", p=128)
    wdn = C.din['w_down'][layer].rearrange("(j p) n -> p j n", p=128)
    with ExitStack() as st0:
        T0 = lambda n, s, d=F32: st0.enter_context(nc.sbuf_tensor(uniq(n), s, d))
        m = T0("f_m", [128, NJ, ST], BF16)
        halo = T0("f_halo", [128, NJ, 2, 2])
        cp = T0("f_cp", [128, 4, 44])
        S.dma('sp', cp[:], C.din['convp'][:, layer], writes=['cp'])
        S.op('dve', I('memset', halo[:], 0.0), writes=['halo'])
        S.sync()
        for sti in range(SEQ // ST):
            tok0 = sti * ST
            with ExitStack() as st:
                T = lambda n, s, d=F32: st.enter_context(nc.sbuf_tensor(uniq(n), s, d))
                hnT = T("f_hn", [128, 8, ST], BF16)
                norm_pass(C, hT, 1 + 2 * layer, hnT, tok0, ST // 512)
                u = [T("f_u%d" % i, [128, ST + 2]) for i in range(2)]
                ccd = [[T("f_c%d_%d" % (par, i), [128, ST]) for i in range(2)] for par in range(2)]
                wf = [T("f_wf%d" % i, [128, 8, 2, 128]) for i in range(2)]
                wb = [T("f_wb%d" % i, [128, 8, 2, 128], BF16) for i in range(2)]
                psum = [st.enter_context(nc.psum_tensor(uniq("f_ps%d" % i), [128, 512], F32)) for i in range(8)]
                def prep_up(j):
                    b = j % 2
                    for ab in range(2):
                        c0 = ab * DFF + j * 128
                        S.dma('sp', wf[b][:, :, ab, :], wup[:, :, c0:c0 + 128], writes=[('wf', b, ab)])
                    S.op('pool', I('tensor_copy', wb[b][:], wf[b][:]), reads=[('wf', b, 0), ('wf', b, 1)],
                         writes=[('wb', b)])
                prep_up(0)
                for j in range(NJ):
                    b = j % 2
                    cc = ccd[j % 2]
                    cp_ = j % 2
                    if j + 1 < NJ:
                        prep_up(j + 1)
                    for ab in range(2):
                        S.op('pool', I('tensor_copy', u[ab][:, 0:2], halo[:, j, ab, :]),
                             reads=['halo', ('hl', j, ab)], writes=[('u', ab)])
                        for tt in range(ST // 512):
                            pi = ab * 4 + tt
                            mm_group(S, psum[pi][:], ('fps', pi),
                                     [(wb[b][:, c, ab, :], hnT[:, c, tt * 512:(tt + 1) * 512]) for c in range(8)], [('wb', b)])
                            S.op('act', I('copy', u[ab][:, 2 + tt * 512:2 + (tt + 1) * 512], psum[pi][:]),
                                 reads=[('fps', pi)], writes=[('u', ab, tt)])
                        ukeys = [('u', ab)] + [('u', ab, tt) for tt in range(ST // 512)]
                        jb = ab * NJ + j
                        S.op('pool', I('tensor_copy', halo[:, j, ab, :], u[ab][:, ST:ST + 2]),
                             reads=ukeys, writes=[('hl', j, ab)])
                        S.op('act', I('activation', cc[ab][:], u[ab][:, 2:ST + 2], AF.Identity,
                                                                         scale=cp[:, 2, jb:jb + 1], bias=cp[:, 3, jb:jb + 1]),
                             reads=ukeys + ['cp'], writes=[('cc', cp_, ab)])
                        eng = 'dve'
                        S.op(eng, I('scalar_tensor_tensor', cc[ab][:], in0=u[ab][:, 1:ST + 1], scalar=cp[:, 1, jb:jb + 1],
                                                                                 in1=cc[ab][:], op0=ALU.mult, op1=ALU.add),
                             reads=ukeys + [('cc', cp_, ab), 'cp'], writes=[('cc', cp_, ab)])
                        S.op(eng, I('scalar_tensor_tensor', cc[ab][:], in0=u[ab][:, 0:ST], scalar=cp[:, 0, jb:jb + 1],
                                                                                 in1=cc[ab][:], op0=ALU.mult, op1=ALU.add),
                             reads=ukeys + [('cc', cp_, ab), 'cp'], writes=[('cc', cp_, ab)])
                    S.op('act', I('activation', cc[0][:], cc[0][:], AF.Silu), reads=[('cc', cp_, 0)], writes=[('cc', cp_, 0)])
                    S.op('dve', I('tensor_tensor', m[:, j, :], cc[0][:], cc[1][:], op=ALU.mult),
                         reads=[('cc', cp_, 0), ('cc', cp_, 1)], writes=[('m', j)])
                S.sync()
            with ExitStack() as st:
                T = lambda n, s, d=F32: st.enter_context(nc.sbuf_tensor(uniq(n), s, d))
                wf = [T("g_wf%d" % i, [128, NJ, 128]) for i in range(2)]
                wb = [T("g_wb%d" % i, [128, NJ, 128], BF16) for i in range(2)]
                hb = [T("g_h%d" % i, [128, 512]) for i in range(3)]
                psum = [st.enter_context(nc.psum_tensor(uniq("g_ps%d" % i), [128, 512], F32)) for i in range(4)]
                k = 0

                def prep_dn(dmb):
                    b = dmb % 2
                    S.dma('sp', wf[b][:, 0:11, :], wdn[:, 0:11, dmb * 128:(dmb + 1) * 128], writes=[('wf', b, 0)])
                    S.dma('sp', wf[b][:, 11:22, :], wdn[:, 11:22, dmb * 128:(dmb + 1) * 128], writes=[('wf', b, 1)])
                    S.op('pool', I('tensor_copy', wb[b][:], wf[b][:]), reads=[('wf', b, 0), ('wf', b, 1)], writes=[('wb', b)])
                prep_dn(0)
                for dmb in range(8):
                    b = dmb % 2
                    if dmb + 1 < 8:
                        prep_dn(dmb + 1)
                    for tt in range(ST // 512):
                        pi = k % 4
                        hi = k % 3
                        k += 1
                        tsl = slice(tok0 + tt * 512, tok0 + (tt + 1) * 512)
                        rsl = slice(dmb * 128, (dmb + 1) * 128)
                        S.dma('sp', hb[hi][:], hT[rsl, tsl], writes=[('hb', hi)])
                        mm_group(S, psum[pi][:], ('gps', pi), [(wb[b][:, j, :], m[:, j, tt * 512:(tt + 1) * 512]) for j in range(NJ)],
                                 [('wb', b)])
                        S.op('dve', I('tensor_tensor', hb[hi][:], psum[pi][:], hb[hi][:], op=ALU.add),
                             reads=[('gps', pi), ('hb', hi)], writes=[('hb', hi)])
                        S.dma('pool', hT[rsl, tsl], hb[hi][:], reads=[('hb', hi)])
                S.sync()


def phase_final(C, hT, outT):
    nc, S = C.nc, C.S
    with ExitStack() as st:
        T = lambda n, s, d=F32: st.enter_context(nc.sbuf_tensor(uniq(n), s, d))
        ht = [T("z_ht%d" % i, [128, 8, 512]) for i in range(3)]
        sq = [T("z_sq%d" % i, [128, 8, 512], BF16) for i in range(3)]
        rs = [T("z_rs%d" % i, [128, 512]) for i in range(3)]
        pss = [st.enter_context(nc.psum_tensor(uniq("z_ps%d" % i), [128, 512], F32)) for i in range(3)]
        srcv = hT.rearrange("(c p) t -> p c t", p=128)
        dstv = outT.rearrange("(c p) t -> p c t", p=128)
        for i in range(8):
            b = i % 3
            tsl = slice(i * 512, (i + 1) * 512)
            hk = [('ht', b, c) for c in range(8)]
            S.dma('sp', ht[b][:], srcv[:, :, tsl], writes=hk)
            S.op('act', I('activation', sq[b][:], ht[b][:], AF.Square), reads=hk, writes=[('sq', b)])
            mm_group(S, pss[b][:], ('nps', b), [(C.ones[:], sq[b][:, c, :]) for c in range(8)], [('sq', b)])
            S.op('act', I('activation', rs[b][:], pss[b][:], AF.Sqrt, scale=1.0 / DM, bias=C.epsc[:, 0:1]),
                 reads=[('nps', b)], writes=[('rs', b)])
            S.op('dve', I('reciprocal', rs[b][:], rs[b][:]), reads=[('rs', b)], writes=[('rs', b)])
            for c in range(8):
                eng = 'dve'
                S.op(eng, I('scalar_tensor_tensor',
                    ht[b][:, c, :], in0=ht[b][:, c, :], scalar=C.gains[:, 4, c:c + 1],
                    in1=rs[b][:], op0=ALU.mult, op1=ALU.mult), reads=[('ht', b, c), ('rs', b), ('sq', b)], writes=[('ht', b, c)])
            S.dma('pool', dstv[:, :, tsl], ht[b][:], reads=hk)
        S.sync()


PHASES = []


def build(debug=False, stop_after=None, feed=(), skip=()):
    nc = bass.Bass("TRN2", target_bir_lowering=False)
    C = Ctx()
    C.nc = nc
    C.debug = debug
    kindS = "ExternalOutput" if debug else "Internal"
    C.din = {}
    C.dsc = {}

    def din(name, shape, dt=F32):
        C.din[name] = nc.dram_tensor(name, shape, dt, kind="ExternalInput").ap()

    def dsc(name, shape, dt=BF16):
        C.dsc[name] = nc.dram_tensor(name, shape, dt, kind=("ExternalInput" if name in feed else kindS)).ap()
    din('xT', [DM, SEQ])
    din('w_in0', [DM, 4608])
    din('w_out0', [1536, DM])
    din('w_in1', [DM, 4096])
    din('w_out1', [DM, DM])
    din('w_up', [2, DM, 2 * DFF])
    din('w_down', [2, DFF, DM])
    din('gains', [128, 5, 8])
    din('convp', [128, 2, 4, 44])
    din('retn', [128, 1024])
    din('rot', [128, 2, SEQ])
    din('ones', [128, 128], BF16)
    din('ident', [128, 128], BF16)
    din('dect', [128, 8, 128])
    din('gq', [128, 8, 128])
    din('gk', [128, 8, 64])
    din('cdr', [128, 4])
    din('biasT', [128, 24, 256])
    din('mask2', [128, 256])
    din('lbrep', [128, 2, 1024])
    din('lbfm', [128, 2, 8])
    din('hgn', [128, 1024])
    din('t1m', [128, 128], BF16)
    din('t2m', [128, 128], BF16)
    for n in ('qaT', 'kaT', 'qbT', 'kbT'):
        dsc(n, [512, SEQ])
    dsc('va', [SEQ, 1024])
    dsc('ga', [SEQ, 1024])
    dsc('vb', [SEQ, 512])
    dsc('yT', [1536, SEQ])
    dsc('hT', [DM, SEQ], F32)
    dsc('q1T', [1024, SEQ])
    dsc('k1T', [1024, SEQ])
    dsc('lfh', [SEQ, 1024])
    dsc('lfl', [SEQ, 1024])
    dsc('k1', [SEQ, 1024])
    dsc('v1', [SEQ, 1024])
    dsc('g1', [SEQ, 1024])
    dsc('y1T', [1024, SEQ])
    if debug:
        dsc('dbg_ret', [SEQ, 1024], F32)
    outT = nc.dram_tensor('outT', [DM, SEQ], F32, kind="ExternalOutput").ap()
    with ExitStack() as st:
        C.S = Sched(nc, st)
        load_consts(C, st)
        plan = [
            ('A0', lambda: phase_A0(C)),
            ('B0', lambda: phase_B0(C)),
            ('C0', lambda: phase_C0(C)),
            ('D0', lambda: phase_outproj(C, C.dsc['yT'], 12, C.din['w_out0'], C.din['xT'], C.dsc['hT'])),
            ('E0', lambda: phase_ffn(C, 0, C.dsc['hT'])),
            ('A1', lambda: phase_A1(C, C.dsc['hT'])),
            ('B1', lambda: phase_B1(C)),
            ('D1', lambda: phase_outproj(C, C.dsc['y1T'], 8, C.din['w_out1'], C.dsc['hT'], C.dsc['hT'])),
            ('E1', lambda: phase_ffn(C, 1, C.dsc['hT'])),
            ('Z', lambda: phase_final(C, C.dsc['hT'], outT)),
        ]
        for name, fn in plan:
            if name in skip:
                continue
            fn()
            if stop_after == name:
                break
    print("ninst", C.S.ninst, "cnt", C.S.cnt, "dcnt max", max(C.S.dcnt))
    return nc


C_SKIP = set()
JOBSEL = None
B0_LEVEL = 9
TAPN = 0
B0_SUB = 9


def host_inputs(inputs, b):
    f = np.float32
    x = np.asarray(inputs['x'], f)
    d = {}
    d['xT'] = np.ascontiguousarray(x[b].T)
    d['w_in0'] = np.ascontiguousarray(inputs['even_w_in'][0], f)
    d['w_out0'] = np.ascontiguousarray(inputs['even_w_out'][0], f)
    d['w_in1'] = np.ascontiguousarray(inputs['odd_w_in'][0], f)
    d['w_out1'] = np.ascontiguousarray(inputs['odd_w_out'][0], f)
    d['w_up'] = np.ascontiguousarray(inputs['ffn_w_up'], f)
    d['w_down'] = np.ascontiguousarray(inputs['ffn_w_down'], f)
    g = np.stack([inputs['mix_norm'][0], inputs['ffn_norm'][0], inputs['mix_norm'][1], inputs['ffn_norm'][1],
                  inputs['final_norm']], 0).astype(f)
    d['gains'] = np.ascontiguousarray(g.reshape(5, 8, 128).transpose(2, 0, 1))
    cw = np.asarray(inputs['ffn_conv_w'], f)
    cb = np.asarray(inputs['ffn_conv_b'], f)
    cp = np.concatenate([cw, cb[:, None, :]], 1)
    d['convp'] = np.ascontiguousarray(cp.reshape(2, 4, 44, 128).transpose(3, 0, 1, 2))
    d['retn'] = np.ascontiguousarray(np.broadcast_to(np.asarray(inputs['ret_norm'], f)[0][None, :], (128, 1024)))
    j = np.arange(128) % 64
    inv = (10000.0 ** (-np.arange(0, 64, 2, dtype=np.float32) / 64)).astype(f)
    ang = np.arange(SEQ, dtype=f)[None, :] * inv[j % 32][:, None]
    cos = np.cos(ang).astype(f)
    sin = np.sin(ang).astype(f)
    sgn = np.where(j < 32, -1.0, 1.0).astype(f)[:, None]
    d['rot'] = np.ascontiguousarray(np.stack([cos, sin * sgn], 1))
    d['ones'] = np.ones((128, 128), ml_dtypes.bfloat16)
    gam = 1.0 - 2.0 ** (-5.0 - np.arange(8, dtype=np.float64))
    ii = np.arange(128)
    diff = ii[None, :] - ii[:, None]
    dec = np.where(diff[:, None, :] >= 0, gam[None, :, None] ** np.maximum(diff, 0)[:, None, :], 0.0)
    d['dect'] = np.ascontiguousarray(dec.astype(f))
    hp = 2 * np.arange(4)[None, :] + (np.arange(128) // 64)[:, None]
    d['gq'] = np.ascontiguousarray(np.broadcast_to((gam[:, None] ** (ii[None, :] + 1.0))[None], (128, 8, 128)).astype(f))
    d['gk'] = np.ascontiguousarray(np.broadcast_to((gam[None, :] ** (127.0 - ii[:, None]))[:, :, None], (128, 8, 64)).astype(f))
    d['cdr'] = np.ascontiguousarray((gam[hp] ** 128.0).astype(f))
    d['ident'] = np.eye(128).astype(ml_dtypes.bfloat16)
    lb = np.asarray(inputs['hgrn_lb'], f)
    d['lbrep'] = np.ascontiguousarray(np.broadcast_to(lb[None], (128, 2, 1024)))
    d['lbfm'] = np.ascontiguousarray(lb.reshape(2, 8, 128).transpose(2, 0, 1))
    d['hgn'] = np.ascontiguousarray(np.broadcast_to(np.asarray(inputs['hgrn_norm'], f)[0][None, :], (128, 1024)))
    jj = np.arange(128)
    same = (jj[:, None] // 64) == (jj[None, :] // 64)
    d['t1m'] = np.ascontiguousarray((same & (jj[:, None] <= jj[None, :])).astype(ml_dtypes.bfloat16))
    d['t2m'] = np.ascontiguousarray((same & (jj[:, None] > jj[None, :])).astype(ml_dtypes.bfloat16))
    rb = np.asarray(inputs['rel_bias'], f)
    cc_ = np.arange(128)[:, None]
    aa_ = np.arange(128)[None, :]
    dist2 = np.stack([aa_ - cc_, 128 + aa_ - cc_], 0)
    valid = np.stack([aa_ >= cc_, aa_ <= cc_], 0)
    bt = np.zeros((128, 3, 8, 2, 128), f)
    for bi, r in enumerate(DIL_R):
        dd_ = (np.maximum(dist2, 0) * r).astype(np.int64)
        df = dd_.astype(np.float32)
        large = 16 + (np.log(np.maximum(df, np.float32(1.0)) / np.float32(16)) / np.float32(math.log(2048 / 16)) * np.float32(16)).astype(np.int32)
        large = np.minimum(large, 31)
        bucket = np.where(dd_ < 16, dd_, large)
        gb = rb[bucket]
        gb = np.where(valid[..., None], gb, 0.0)
        bt[:, bi] = gb.transpose(1, 3, 0, 2)
    d['biasT'] = np.ascontiguousarray(bt.reshape(128, 24, 256))
    d['mask2'] = np.ascontiguousarray(valid.transpose(1, 0, 2).reshape(128, 256).astype(f))
    return d


_NC = {}


def kernel(**inputs):
    if 'nc' not in _NC:
        _NC['nc'] = build()
    nc = _NC['nc']
    in_maps = [host_inputs(inputs, c % 4) for c in range(8)]
    res = run_bass_kernel_spmd(nc, in_maps, core_ids=list(range(8)))
    out = np.stack([np.ascontiguousarray(res.results[b]['outT'].T) for b in range(4)], 0)
    return out.astype(np.float32)


def phase_B0(C):
    nc, S, D = C.nc, C.S, C.dsc
    with ExitStack() as st:
        T = lambda n, s, d=F32: st.enter_context(nc.sbuf_tensor(uniq(n), s, d))
        P = lambda n, s, d=F32: st.enter_context(nc.psum_tensor(uniq(n), s, d))
        dect = T("r_dec", [128, 8, 128])
        gq = T("r_gq", [128, 8, 128])
        gk = T("r_gk", [128, 8, 64])
        cdr = T("r_cd", [128, 4])
        S.dma('sp', dect[:], C.din['dect'], writes=['dect'])
        S.dma('sp', gq[:], C.din['gq'], writes=['gq'])
        S.dma('sp', gk[:], C.din['gk'], writes=['gk'])
        S.dma('sp', cdr[:], C.din['cdr'], writes=['cdr'])
        Sf = T("r_S", [128, 4, 256])
        Sb = [T("r_Sb%d" % i, [128, 4, 256], BF16) for i in range(2)]
        S.op('dve', I('memset', Sf[:], 0.0), writes=['Sf'])
        S.op('dve', I('memset', Sb[0][:], 0.0), writes=[('Sb', 0)])
        qT = [T("r_q%d" % i, [128, 8, 512], BF16) for i in range(2)]
        for i in range(2):
            S.op("dve", I('memset', qT[i][:], 0.0), writes=[("qT", i)])
        kT = [T("r_k%d" % i, [128, 4, 512], BF16) for i in range(2)]
        va = [T("r_v%d" % i, [128, 4, 1024], BF16) for i in range(2)]
        ga = [T("r_g%d" % i, [128, 4, 1024], BF16) for i in range(2)]
        kout = [T("r_ko%d" % i, [128, 8, 64], BF16) for i in range(2)]
        qin = [T("r_qi%d" % i, [128, 8, 128], BF16) for i in range(2)]
        sc = [T("r_sc%d" % i, [128, 8, 128], BF16) for i in range(2)]
        xs = T("r_xs", [128, 8, 128])
        sqb = T("r_sq", [128, 8, 128])
        xn = T("r_xn", [128, 8, 128])
        yb = T("r_yb", [128, 8, 128], BF16)
        stt = T("r_stat", [128, 8, 8])
        yst = [T("r_yst%d" % i, [128, 8, 512], BF16) for i in range(2)]
        ps_kt = P("r_pkt", [128, 512], BF16)
        ps_s = [P("r_ps%d" % i, [128, 512]) for i in range(2)]
        ps_o = [P("r_po%d" % i, [128, 512]) for i in range(2)]
        ps_inc = [P("r_pi%d" % i, [128, 512]) for i in range(2)]
        ps_yt = P("r_pyt", [128, 1024], BF16)
        qv = D['qaT'].rearrange("(g two d) t -> two d g t", two=2, d=64)
        kv = D['kaT'].rearrange("(g p) t -> p g t", p=128)
        vv = D['va'].rearrange("(c p) e -> p c e", p=128)
        gv = D['ga'].rearrange("(c p) e -> p c e", p=128)
        yv = D['yT'].rearrange("(h p) t -> p h t", p=128)
        sbi = 0
        for sci in range(SEQ // 512):
            b = sci % 2
            tsl = slice(sci * 512, (sci + 1) * 512)
            for half in range(2):
                S.dma('sp', qT[b][64 * half:64 * half + 64, half:8:2, :], qv[half][:, :, tsl], reads=[('qT', b)], writes=[('qT', b, half)])
            S.dma('sp', kT[b][:], kv[:, :, tsl], writes=[('kT', b)])
            S.dma('sp', va[b][:], vv[:, sci * 4:(sci + 1) * 4, :], writes=[('va', b)])
            S.dma('sp', ga[b][:], gv[:, sci * 4:(sci + 1) * 4, :], writes=[('ga', b)])
            for c4 in range(4):
                if B0_LEVEL < 1:
                    continue
                n = sci * 4 + c4
                kb = n % 2
                csl = slice(c4 * 128, (c4 + 1) * 128)
                S.ops('pe', [(I('transpose', ps_kt[:, g * 128:(g + 1) * 128], kT[b][:, g, csl], C.ident[:]))
                             for g in range(4)], reads=[('kT', b)], writes=['pskt'])
                if B0_SUB >= 2:
                  S.op('dve', I('tensor_tensor', kout[kb][:], ps_kt[:].rearrange("p (h d) -> p h d", d=64), gk[:], op=ALU.mult),
                     reads=['pskt', 'gk'], writes=[('kout', kb)])
                if B0_SUB < 3:
                    continue
                S.op('dve', I('tensor_tensor', qin[kb][:], qT[b][:, :, csl], gq[:], op=ALU.mult),
                     reads=[('qT', b, 0), ('qT', b, 1), 'gq'], writes=[('qin', kb)])
                if B0_LEVEL >= 2:
                    fns = []
                    for h in range(8):
                        g, r0 = h // 2, 64 * (h % 2)
                        fns.append(I('matmul',
                            ps_s[h % 2][:, (h // 2) * 128:(h // 2 + 1) * 128], lhsT=kT[b][:, g, csl],
                            rhs=qT[b][:, h, csl], start=True, stop=True))
                    S.ops('pe', fns, reads=[('kT', b), ('qT', b, 0), ('qT', b, 1)], writes=[('pss', 0), ('pss', 1)])
                    for i in range(2):
                        S.op('dve', I('tensor_tensor', sc[kb][:, i:8:2, :],
                                                                   ps_s[i][:].rearrange("p (h t) -> p h t", t=128),
                                                                   dect[:, i:8:2, :], op=ALU.mult),
                             reads=[('pss', i), 'dect'], writes=[('sc', kb, i)])
                if B0_LEVEL >= 3:
                    fns = []
                    for h in range(8):
                        g, r0 = h // 2, 64 * (h % 2)
                        osl = slice((h // 2) * 128, (h // 2 + 1) * 128)
                        fns.append(I('matmul',
                            ps_o[h % 2][:, osl], lhsT=sc[kb][:, h, :], rhs=va[b][:, c4, h * 128:(h + 1) * 128], start=True, stop=False))
                        fns.append(I('matmul',
                            ps_o[h % 2][:, osl], lhsT=qin[kb][:, h, :],
                            rhs=Sb[sbi][:, g, (h % 2) * 128:(h % 2 + 1) * 128], start=False, stop=True))
                    S.ops('pe', fns, reads=[('sc', kb, 0), ('sc', kb, 1), ('va', b), ('qin', kb), ('Sb', sbi)],
                          writes=[('pso', 0), ('pso', 1)])
                if n == TAPN:
                    C.tap('kout', kout[kb][:], [128, 8, 64], BF16, [('kout', kb)])
                    C.tap('qin', qin[kb][:], [128, 8, 128], BF16, [('qin', kb)])
                    C.tap('sc', sc[kb][:], [128, 8, 128], BF16, [('sc', kb, 0), ('sc', kb, 1)])
                    C.tap('qz', qT[b][:], [128, 8, 512], BF16, [('qT', b, 0), ('qT', b, 1)])
                    C.tap('sb', Sb[sbi][:], [128, 4, 256], BF16, [('Sb', sbi)])
                for i in range(2 if B0_LEVEL >= 4 else 0):
                    fns = []
                    for g in range(2 * i, 2 * i + 2):
                        fns.append(I('matmul',
                            ps_inc[i][:, (g % 2) * 256:(g % 2 + 1) * 256],
                            lhsT=kout[kb][:, 2 * g:2 * g + 2, :].rearrange("p h d -> p (h d)"),
                            rhs=va[b][:, c4, g * 256:(g + 1) * 256], start=True, stop=True))
                    S.ops('pe', fns, reads=[('kout', kb), ('va', b)], writes=[('psi', i)])
                    for g in range(2 * i, 2 * i + 2):
                        S.op('dve', I('scalar_tensor_tensor',
                            Sf[:, g, :], in0=Sf[:, g, :], scalar=cdr[:, g:g + 1], in1=ps_inc[i][:, (g % 2) * 256:(g % 2 + 1) * 256],
                            op0=ALU.mult, op1=ALU.add), reads=[('psi', i), 'cdr', ('Sf', g)], writes=[('Sf', g)])
                if B0_SUB >= 4:
                  S.op('pool', I('tensor_copy', Sb[1 - sbi][:], Sf[:]), reads=[('Sf', g) for g in range(4)] + ['Sf'],
                     writes=[('Sb', 1 - sbi)])
                sbi = 1 - sbi
                if B0_LEVEL < 5:
                    continue
                for i in range(2):
                    S.op('act', I('copy', xs[:, i:8:2, :], ps_o[i][:].rearrange("p (h t) -> p h t", t=128)),
                         reads=[('pso', i)], writes=[('xs', i)])
                xk = [('xs', 0), ('xs', 1)]
                if 'dbg_ret' in D:
                    S.dma('sp', D['dbg_ret'][n * 128:(n + 1) * 128, :], xs[:].rearrange("p h t -> p (h t)"), reads=xk)
                S.op('dve', I('tensor_reduce', stt[:, 0, :], xs[:], axis=AX.X, op=ALU.add), reads=xk, writes=['sums'])
                S.op('act', I('activation', sqb[:], xs[:], AF.Square), reads=xk, writes=['sqb'])
                S.op('dve', I('tensor_reduce', stt[:, 1, :], sqb[:], axis=AX.X, op=ALU.add), reads=['sqb'], writes=['sumsq'])
                S.op('dve', I('tensor_scalar', stt[:, 2, :], stt[:, 0, :], 1.0 / 128, None, op0=ALU.mult),
                     reads=['sums'], writes=['mean'])
                S.op('dve', I('tensor_tensor', stt[:, 3, :], stt[:, 2, :], stt[:, 2, :], op=ALU.mult), reads=['mean'], writes=['msq'])
                S.op('dve', I('scalar_tensor_tensor', stt[:, 4, :], in0=stt[:, 1, :], scalar=1.0 / 128, in1=stt[:, 3, :],
                                                             op0=ALU.mult, op1=ALU.subtract), reads=['sumsq', 'msq'], writes=['var'])
                S.op('act', I('activation', stt[:, 5, :], stt[:, 4, :], AF.Sqrt, bias=C.epsc[:, 0:1]), reads=['var'], writes=['sd'])
                S.op('dve', I('reciprocal', stt[:, 6, :], stt[:, 5, :]), reads=['sd'], writes=['rstd'])
                S.op('dve', I('tensor_tensor', xn[:], xs[:], stt[:, 2, :].unsqueeze(2).to_broadcast([128, 8, 128]), op=ALU.subtract),
                     reads=xk + ['mean'], writes=['xn'])
                S.op('dve', I('tensor_tensor', xn[:], xn[:], stt[:, 6, :].unsqueeze(2).to_broadcast([128, 8, 128]), op=ALU.mult),
                     reads=['xn', 'rstd'], writes=['xn'])
                S.op('dve', I('tensor_tensor', yb[:], xn[:], ga[b][:, c4, :].rearrange("p (h t) -> p h t", t=128), op=ALU.mult),
                     reads=['xn', ('ga', b)], writes=['yb'])
                S.ops('pe', [(I('transpose', ps_yt[:, h * 128:(h + 1) * 128], yb[:, h, :], C.ident[:])) for h in range(8)],
                      reads=['yb'], writes=['psyt'])
                S.op('act', I('copy', yst[b][:, :, csl], ps_yt[:].rearrange("p (h t) -> p h t", t=128)),
                     reads=['psyt'], writes=[('yst', b, c4)])
            S.dma('pool', yv[:, 0:8, tsl], yst[b][:], reads=[('yst', b, c4) for c4 in range(4)])
        S.sync()


DIL_R = (1, 4, 16)


def phase_C0(C):
    nc, S, D = C.nc, C.S, C.dsc
    with ExitStack() as st:
        T = lambda n, s, d=F32: st.enter_context(nc.sbuf_tensor(uniq(n), s, d))
        P = lambda n, s, d=F32: st.enter_context(nc.psum_tensor(uniq(n), s, d))
        EBT = T("c_ebt", [128, 24, 256], BF16)
        with ExitStack() as st2:
            bt = st2.enter_context(nc.sbuf_tensor(uniq("c_bt"), [128, 24, 256], F32))
            mk = st2.enter_context(nc.sbuf_tensor(uniq("c_mk"), [128, 256], F32))
            S.dma('sp', bt[:], C.din['biasT'], writes=['bt'])
            S.dma('sp', mk[:], C.din['mask2'], writes=['mk'])
            S.op('act', I('activation', bt[:], bt[:], AF.Exp), reads=['bt'], writes=['bt'])
            S.op('dve', I('tensor_tensor', EBT[:], bt[:], mk[:].unsqueeze(1).to_broadcast([128, 24, 256]), op=ALU.mult),
                 reads=['bt', 'mk'], writes=['ebt'])
            S.sync()
        kT = T("c_k", [128, SEQ], BF16)
        qz = T("c_q", [128, 2, SEQ], BF16)
        vp = [T("c_v%d" % i, [128, 32, 2, 64], BF16) for i in range(2)]
        onesb = T("c_ones", [128, 64], BF16)
        accn = T("c_an", [64, 2, SEQ])
        accd = T("c_ad", [64, 2, SEQ])
        NROT = 4
        pe_ = [T("c_pe%d" % i, [128, 2, 128], BF16) for i in range(NROT)]
        pt_ = [T("c_pt%d" % i, [128, 2, 128], BF16) for i in range(NROT)]
        rden = [T("c_rd%d" % i, [64, 512]) for i in range(2)]
        ystg = [T("c_ys%d" % i, [64, 512], BF16) for i in range(2)]
        ps = [P("c_ps%d" % i, [128, 256]) for i in range(NROT)]
        po = [P("c_po%d" % i, [64, 512]) for i in range(2)]
        pd = [P("c_pd%d" % i, [64, 512]) for i in range(2)]
        S.op('dve', I('memset', qz[:], 0.0), writes=['qz'])
        S.op('dve', I('memset', onesb[:], 1.0), writes=['onesb'])
        qv = D['qbT'].rearrange("(g two d) t -> g two d t", two=2, d=64)
        kv = D['kbT'].rearrange("(g p) t -> g p t", p=128)
        st_ = dict(cnt=0, bcnt=0, vcnt=0)
        DEPTH = 2

        def sl(start, r):
            return slice(start, start + 127 * r + 1, r)

        for g in range(4):
            S.dma('sp', kT[:], kv[g], writes=['kT'])
            for half in range(2):
                S.dma('sp', qz[64 * half:64 * half + 64, half, :], qv[g, half], reads=['qz'], writes=[('qz', half)])
            vinfo = {}

            def issue_v(bi, g=g, vinfo=vinfo):
                r = DIL_R[bi]
                nb = SEQ // (128 * r)
                vb_ = vp[st_['vcnt'] % 2]
                vkey = ('vp', st_['vcnt'] % 2)
                st_['vcnt'] += 1
                vsrc = D['vb'].rearrange("(n a r) (g2 hh d) -> a r n g2 hh d", a=128, r=r, hh=2, d=64)
                vkeys = []
                for rr in range(r):
                    step = 8 if nb > 8 else nb
                    for n0 in range(0, nb, step):
                        S.dma('sp', vb_[:, rr * nb + n0:rr * nb + n0 + step, :, :], vsrc[:, rr, n0:n0 + step, g, :, :],
                              writes=[(vkey, rr, n0)])
                        vkeys.append((vkey, rr, n0))
                vinfo[bi] = (vb_, vkeys)

            tiles = []
            for bi, r in enumerate(DIL_R):
                nb = SEQ // (128 * r)
                first = True
                for hh in range(2):
                    if r == 1:
                        batches = [[(0, n) for n in range(n0, n0 + 4)] for n0 in range(0, nb, 4)]
                    else:
                        batches = [[(rho, n) for rho in range(r0, r0 + 4)] for n in range(nb) for r0 in range(0, r, 4)]
                    for batch in batches:
                        bb = st_['bcnt'] % 2
                        st_['bcnt'] += 1
                        for slot, (rho, n) in enumerate(batch):
                            tiles.append(dict(bi=bi, r=r, nb=nb, hh=hh, bb=bb, slot=slot, rho=rho, n=n, batch=batch,
                                              last=(slot == len(batch) - 1), first_of_branch=first))
                            first = False

            def front(t, g=g):
                r, n, rho, hh = t['r'], t['n'], t['rho'], t['hh']
                h = 2 * g + hh
                pi = ei = st_['cnt'] % NROT
                st_['cnt'] += 1
                t['ei'] = ei
                nk = 1 if n == 0 else 2
                t['nk'] = nk
                qsl = sl(128 * n * r + rho, r)
                fns = [I('matmul', ps[pi][:, 0:128], lhsT=kT[:, qsl], rhs=qz[:, hh, qsl], start=True, stop=True)]
                if nk == 2:
                    ksl = sl(128 * (n - 1) * r + rho, r)
                    fns.append(I('matmul', ps[pi][:, 128:256], lhsT=kT[:, ksl], rhs=qz[:, hh, qsl], start=True, stop=True))
                S.ops('pe', fns, reads=['kT', ('qz', 0), ('qz', 1)], writes=[('ps', pi)])
                S.op('act', I('activation', pe_[ei][:, 0:nk, :], ps[pi][:, 0:nk * 128].rearrange("p (s q) -> p s q", q=128), AF.Exp),
                     reads=[('ps', pi)], writes=[('pe', ei)])
                S.op('dve', I('tensor_tensor', pt_[ei][:, 0:nk, :], pe_[ei][:, 0:nk, :],
                              EBT[:, t['bi'] * 8 + h, 0:nk * 128].rearrange("p (s q) -> p s q", q=128), op=ALU.mult),
                     reads=[('pe', ei), 'ebt'], writes=[('pt', ei)])

            def back(t, g=g, vinfo=vinfo):
                r, n, rho, hh, bb, slot, nb, bi = t['r'], t['n'], t['rho'], t['hh'], t['bb'], t['slot'], t['nb'], t['bi']
                ei, nk = t['ei'], t['nk']
                vb_, vkeys = vinfo[bi]
                ti = rho * nb + n
                osl = slice(slot * 128, (slot + 1) * 128)
                fo = [I('matmul', po[bb][:, osl], lhsT=vb_[:, ti, hh, :], rhs=pt_[ei][:, 0, :], start=True, stop=(nk == 1))]
                fd = [I('matmul', pd[bb][:, osl], lhsT=onesb[:], rhs=pt_[ei][:, 0, :], start=True, stop=(nk == 1))]
                if nk == 2:
                    fo.append(I('matmul', po[bb][:, osl], lhsT=vb_[:, ti - 1, hh, :], rhs=pt_[ei][:, 1, :], start=False, stop=True))
                    fd.append(I('matmul', pd[bb][:, osl], lhsT=onesb[:], rhs=pt_[ei][:, 1, :], start=False, stop=True))
                S.ops('pe', fo + fd, reads=[('pt', ei), 'onesb'] + vkeys, writes=[('po', bb), ('pd', bb)])
                if not t['last']:
                    return
                rho0, n0 = t['batch'][0]
                if r == 1:
                    tok0 = 128 * n0
                    dn = accn[:, hh, tok0:tok0 + 512]
                    dd = accd[:, hh, tok0:tok0 + 512]
                    sn, sd = po[bb][:], pd[bb][:]
                else:
                    base = 128 * n0 * r
                    dn = accn[:, hh, base:base + 128 * r].rearrange("p (a r) -> p a r", r=r)[:, :, rho0:rho0 + 4]
                    dd = accd[:, hh, base:base + 128 * r].rearrange("p (a r) -> p a r", r=r)[:, :, rho0:rho0 + 4]
                    sn = po[bb][:].rearrange("p (s a) -> p a s", a=128)
                    sd = pd[bb][:].rearrange("p (s a) -> p a s", a=128)
                if bi == 0:
                    S.op('dve', I('tensor_copy', dn, sn), reads=[('po', bb)], writes=[('accn', hh)])
                    S.op('dve', I('tensor_copy', dd, sd), reads=[('pd', bb)], writes=[('accd', hh)])
                else:
                    S.op('dve', I('tensor_tensor', dn, sn, dn, op=ALU.add), reads=[('po', bb), ('accn', hh)], writes=[('accn', hh)])
                    S.op('dve', I('tensor_tensor', dd, sd, dd, op=ALU.add), reads=[('pd', bb), ('accd', hh)], writes=[('accd', hh)])

            issue_v(0)
            for i in range(len(tiles) + DEPTH):
                if i < len(tiles):
                    front(tiles[i])
                if i - DEPTH >= 0:
                    tb = tiles[i - DEPTH]
                    back(tb)
                    if tb['first_of_branch'] and tb['bi'] + 1 < len(DIL_R):
                        issue_v(tb['bi'] + 1)
            for hh in range(2):
                h = 2 * g + hh
                for tt in range(8):
                    i = tt % 2
                    tsl = slice(tt * 512, (tt + 1) * 512)
                    S.op('dve', I('reciprocal', rden[i][:], accd[:, hh, tsl]), reads=[('accd', hh)], writes=[('rden', i)])
                    S.op('dve', I('tensor_tensor', ystg[i][:], accn[:, hh, tsl], rden[i][:], op=ALU.mult),
                         reads=[('accn', hh), ('rden', i)], writes=[('ystg', i)])
                    S.dma('pool', D['yT'][1024 + 64 * h:1024 + 64 * h + 64, tsl], ystg[i][:], reads=[('ystg', i)])
        S.sync()


def phase_A1(C, hT):
    nc, S, D = C.nc, C.S, C.dsc
    with ExitStack() as st:
        T = lambda n, s, d=F32: st.enter_context(nc.sbuf_tensor(uniq(n), s, d))
        hnT = T("a1_hn", [128, 8, SEQ], BF16)
        norm_pass(C, hT, 2, hnT, 0, 8)
        lbr = T("a1_lbr", [128, 1024])
        omr = T("a1_omr", [128, 1024])
        hgn = T("a1_hgn", [128, 1024])
        lbf = T("a1_lbf", [128, 8])
        omf = T("a1_omf", [128, 8])
        with ExitStack() as st2:
            raw = st2.enter_context(nc.sbuf_tensor(uniq("a1_raw"), [128, 2, 1024], F32))
            rawf = st2.enter_context(nc.sbuf_tensor(uniq("a1_rawf"), [128, 2, 8], F32))
            S.dma('sp', raw[:], C.din['lbrep'], writes=['raw'])
            S.dma('sp', rawf[:], C.din['lbfm'], writes=['rawf'])
            S.dma('sp', hgn[:], C.din['hgn'], writes=['hgn'])
            S.op('dve', I('tensor_tensor', lbr[:], raw[:, 1, :], raw[:, 0, :], op=ALU.subtract), reads=['raw'], writes=['lbr'])
            S.op('act', I('activation', lbr[:], lbr[:], AF.Sigmoid), reads=['lbr'], writes=['lbr'])
            S.op('dve', I('tensor_scalar', omr[:], lbr[:], -1.0, 1.0, op0=ALU.mult, op1=ALU.add), reads=['lbr'], writes=['omr'])
            S.op('dve', I('tensor_tensor', lbf[:], rawf[:, 1, :], rawf[:, 0, :], op=ALU.subtract), reads=['rawf'], writes=['lbf'])
            S.op('act', I('activation', lbf[:], lbf[:], AF.Sigmoid), reads=['lbf'], writes=['lbf'])
            S.op('dve', I('tensor_scalar', omf[:], lbf[:], -1.0, 1.0, op0=ALU.mult, op1=ALU.add), reads=['lbf'], writes=['omf'])
            S.sync()
        sb = Stager(C, st, "a1_sb", [128, 512], BF16, 6)
        sf = Stager(C, st, "a1_sf", [128, 512], F32, 3)
        t1 = [T("a1_t1%d" % i, [128, 512]) for i in range(2)]
        t2 = [T("a1_t2%d" % i, [128, 512]) for i in range(2)]
        cnt = [0]

        def epi_fm_silu(dst, cb):
            def f(sub, tt, ps, pk):
                buf, bk = sb.next()
                S.op('act', I('activation', buf[:], ps[:], AF.Silu), reads=[pk], writes=[bk])
                sb.store(dst[cb * 512 + sub * 128:cb * 512 + (sub + 1) * 128, tt * 512:(tt + 1) * 512], buf[:], bk)
            return f

        def epi_fm_k(dst, cb):
            def f(sub, tt, ps, pk):
                i = cnt[0] % 2
                cnt[0] += 1
                ci = cb * 4 + sub
                S.op('act', I('activation', t1[i][:], ps[:], AF.Sigmoid, scale=-1.0), reads=[pk], writes=[('t1', i)])
                buf, bk = sb.next()
                S.op('dve', I('tensor_scalar', buf[:], t1[i][:], omf[:, ci:ci + 1], None, op0=ALU.mult), reads=[('t1', i)], writes=[bk])
                sb.store(dst[ci * 128:(ci + 1) * 128, tt * 512:(tt + 1) * 512], buf[:], bk)
            return f

        def epi_tm_f(cb):
            def f(t, ps, pk):
                i = cnt[0] % 2
                cnt[0] += 1
                csl = slice(cb * 512, (cb + 1) * 512)
                rsl = slice(t * 128, (t + 1) * 128)
                S.op('act', I('activation', t1[i][:], ps[:], AF.Sigmoid), reads=[pk], writes=[('t1', i)])
                S.op('dve', I('tensor_tensor', t2[i][:], t1[i][:], omr[:, csl], op=ALU.mult), reads=[('t1', i)], writes=[('t2', i)])
                S.op('dve', I('tensor_tensor', t2[i][:], t2[i][:], lbr[:, csl], op=ALU.add), reads=[('t2', i)], writes=[('t2', i)])
                fb, fk = sf.next()
                S.op('act', I('activation', fb[:], t2[i][:], AF.Ln), reads=[('t2', i)], writes=[fk])
                hb_, hk_ = sb.next()
                S.op('pool', I('tensor_copy', hb_[:], fb[:]), reads=[fk], writes=[hk_])
                sb.store(D['lfh'][rsl, csl], hb_[:], hk_)
                lb_, lk_ = sb.next()
                S.op('dve', I('tensor_tensor', lb_[:], fb[:], hb_[:], op=ALU.subtract), reads=[fk, hk_], writes=[lk_])
                sb.store(D['lfl'][rsl, csl], lb_[:], lk_)
                buf, bk = sb.next()
                S.op('act', I('activation', buf[:], t2[i][:], AF.Identity, scale=-1.0, bias=C.onec[:, 0:1]), reads=[('t2', i)], writes=[bk])
                sb.store(D['k1'][rsl, csl], buf[:], bk)
            return f

        def epi_tm_copy(dst, cb):
            def f(t, ps, pk):
                buf, bk = sb.next()
                S.op('act', I('copy', buf[:], ps[:]), reads=[pk], writes=[bk])
                sb.store(dst[t * 128:(t + 1) * 128, cb * 512:(cb + 1) * 512], buf[:], bk)
            return f

        def epi_tm_gate(dst, cb):
            def f(t, ps, pk):
                i = cnt[0] % 2
                cnt[0] += 1
                S.op('act', I('activation', t1[i][:], ps[:], AF.Silu), reads=[pk], writes=[('t1', i)])
                buf, bk = sb.next()
                S.op('dve', I('tensor_tensor', buf[:], t1[i][:], hgn[:, cb * 512:(cb + 1) * 512], op=ALU.mult),
                     reads=[('t1', i)], writes=[bk])
                sb.store(dst[t * 128:(t + 1) * 128, cb * 512:(cb + 1) * 512], buf[:], bk)
            return f
        jobs = []
        for cb in range(2):
            jobs.append(dict(c0=cb * 512, role='fm', epi_fm=epi_fm_silu(D['q1T'], cb)))
        for cb in range(2):
            jobs.append(dict(c0=1024 + cb * 512, role='both', epi_fm=epi_fm_k(D['k1T'], cb), epi_tm=epi_tm_f(cb)))
        for cb in range(2):
            jobs.append(dict(c0=2048 + cb * 512, role='tm', epi_tm=epi_tm_copy(D['v1'], cb)))
        for cb in range(2):
            jobs.append(dict(c0=3072 + cb * 512, role='tm', epi_tm=epi_tm_gate(D['g1'], cb)))
        proj_phase(C, hnT, C.din['w_in1'], jobs)


def phase_B1(C):
    nc, S, D = C.nc, C.S, C.dsc
    with ExitStack() as st:
        T = lambda n, s, d=F32: st.enter_context(nc.sbuf_tensor(uniq(n), s, d))
        P = lambda n, s, d=F32: st.enter_context(nc.psum_tensor(uniq(n), s, d))
        T1 = T("h_t1", [128, 128], BF16)
        T2 = T("h_t2", [128, 128], BF16)
        S.dma('sp', T1[:], C.din['t1m'], writes=['T1'])
        S.dma('sp', T2[:], C.din['t2m'], writes=['T2'])
        NSB = 256
        qT = [T("h_q%d" % i, [128, 8, NSB], BF16) for i in range(2)]
        kT = [T("h_k%d" % i, [128, 8, NSB], BF16) for i in range(2)]
        lf = [T("h_lf%d" % i, [128, 2, 2, 1024], BF16) for i in range(2)]
        kk = [T("h_kk%d" % i, [128, 2, 1024], BF16) for i in range(2)]
        vv = [T("h_v%d" % i, [128, 2, 1024], BF16) for i in range(2)]
        gg = [T("h_g%d" % i, [128, 2, 1024], BF16) for i in range(2)]
        eB = T("h_eB", [128, 8, 128])
        eNB = T("h_eNB", [128, 8, 128])
        eRB = T("h_eRB", [128, 1024])
        qlo = [T("h_qlo%d" % i, [128, 8, 128], BF16) for i in range(2)]
        qhi = [T("h_qhi%d" % i, [128, 8, 128], BF16) for i in range(2)]
        kt = [T("h_kt%d" % i, [128, 8, 128], BF16) for i in range(2)]
        klo = [T("h_klo%d" % i, [128, 1024], BF16) for i in range(2)]
        khi = [T("h_khi%d" % i, [128, 1024], BF16) for i in range(2)]
        sc = [T("h_sc%d" % i, [128, 8, 128], BF16) for i in range(2)]
        Sf = T("h_S", [128, 8, 128])
        Sbp = [T("h_Sbp%d" % i, [128, 8, 128], BF16) for i in range(2)]
        Sbm = [T("h_Sbm%d" % i, [128, 8, 128], BF16) for i in range(2)]
        sq = T("h_sq", [128, 8, 128])
        xn = T("h_xn", [128, 8, 128])
        yb = T("h_yb", [128, 8, 128], BF16)
        stt = T("h_stt", [128, 3, 8])
        yst = [T("h_yst%d" % i, [128, 8, NSB], BF16) for i in range(2)]
        bA = [P("h_pA%d" % i, [128, 512]) for i in range(2)]
        bB = [P("h_pB%d" % i, [128, 512]) for i in range(2)]
        bC = [P("h_pC%d" % i, [128, 512]) for i in range(2)]
        bD = P("h_pD", [128, 1024], BF16)
        for i in range(2):
            S.op('dve', I('memset', qlo[i][:], 0.0), writes=[('qlo', i)])
            S.op('dve', I('memset', qhi[i][:], 0.0), writes=[('qhi', i)])
            S.op('dve', I('memset', klo[i][:], 0.0), writes=[('klo', i)])
            S.op('dve', I('memset', khi[i][:], 0.0), writes=[('khi', i)])
        S.op('dve', I('memset', Sf[:], 0.0), writes=[('Sf', h) for h in range(8)])
        S.op('dve', I('memset', Sbp[0][:], 0.0), writes=[('Sbp', 0, h) for h in range(8)])
        qv = D['q1T'].rearrange("(h p) t -> p h t", p=128)
        kv = D['k1T'].rearrange("(h p) t -> p h t", p=128)
        lvh = D['lfh'].rearrange("(c p) e -> p c e", p=128)
        lvl = D['lfl'].rearrange("(c p) e -> p c e", p=128)
        k2v = D['k1'].rearrange("(c p) e -> p c e", p=128)
        vv_ = D['v1'].rearrange("(c p) e -> p c e", p=128)
        gv = D['g1'].rearrange("(c p) e -> p c e", p=128)
        yv = D['y1T'].rearrange("(h p) t -> p h t", p=128)
        for sbi_ in range(SEQ // NSB):
            b = sbi_ % 2
            tsl = slice(sbi_ * NSB, (sbi_ + 1) * NSB)
            csl2 = slice(sbi_ * 2, sbi_ * 2 + 2)
            S.dma('sp', qT[b][:], qv[:, :, tsl], writes=[('qT', b)])
            S.dma('sp', kT[b][:], kv[:, :, tsl], writes=[('kT', b)])
            S.dma('sp', lf[b][:, 0], lvh[:, csl2, :], writes=[('lf', b, 0)])
            S.dma('sp', lf[b][:, 1], lvl[:, csl2, :], writes=[('lf', b, 1)])
            S.dma('sp', kk[b][:], k2v[:, csl2, :], writes=[('kk', b)])
            S.dma('sp', vv[b][:], vv_[:, csl2, :], writes=[('vv', b)])
            S.dma('sp', gg[b][:], gv[:, csl2, :], writes=[('gg', b)])
            for blk in range(2):
                n = sbi_ * 2 + blk
                p2 = n % 2
                bsl = slice(blk * 128, (blk + 1) * 128)
                for i in range(2):
                    lfk = [('lf', b, 0), ('lf', b, 1)]
                    S.ops('pe', [I('matmul', bA[i][:, (h % 4) * 128:(h % 4 + 1) * 128], lhsT=lf[b][:, hl, blk, h * 128:(h + 1) * 128], rhs=T1[:],
                                   start=(hl == 0), stop=(hl == 1)) for h in range(4 * i, 4 * i + 4) for hl in range(2)],
                          reads=lfk + ['T1'], writes=[('bA', i)])
                    S.ops('pe', [I('matmul', bB[i][:], lhsT=T2[:], rhs=lf[b][:, hl, blk, i * 512:(i + 1) * 512], start=(hl == 0), stop=(hl == 1))
                                 for hl in range(2)], reads=lfk + ['T2'], writes=[('bB', i)])
                    S.op('act', I('activation', eB[:, 4 * i:4 * i + 4, :], bA[i][:].rearrange("p (h t) -> p h t", t=128), AF.Exp),
                         reads=[('bA', i)], writes=[('eB', i)])
                    S.op('act', I('activation', eNB[:, 4 * i:4 * i + 4, :], bA[i][:].rearrange("p (h t) -> p h t", t=128), AF.Exp, scale=-1.0),
                         reads=[('bA', i)], writes=[('eNB', i)])
                    S.op('act', I('activation', eRB[:, i * 512:(i + 1) * 512], bB[i][:], AF.Exp), reads=[('bB', i)], writes=[('eRB', i)])
                ek = [('eB', 0), ('eB', 1)]
                S.op('dve', I('tensor_tensor', qlo[p2][:, :, 0:64], qT[b][:, :, blk * 128:blk * 128 + 64], eB[:, :, 0:64], op=ALU.mult),
                     reads=ek + [('qT', b), ('qlo', p2)], writes=[('qlo', p2)])
                S.op('dve', I('tensor_tensor', qhi[p2][:, :, 64:128], qT[b][:, :, blk * 128 + 64:blk * 128 + 128], eB[:, :, 64:128], op=ALU.mult),
                     reads=ek + [('qT', b), ('qhi', p2)], writes=[('qhi', p2)])
                S.op('dve', I('tensor_tensor', kt[p2][:], kT[b][:, :, bsl], eNB[:], op=ALU.mult),
                     reads=[('eNB', 0), ('eNB', 1), ('kT', b)], writes=[('kt', p2)])
                S.op('dve', I('tensor_tensor', klo[p2][0:64, :], kk[b][0:64, blk, :], eRB[0:64, :], op=ALU.mult),
                     reads=[('eRB', 0), ('eRB', 1), ('kk', b), ('klo', p2)], writes=[('klo', p2)])
                S.op('dve', I('tensor_tensor', khi[p2][64:128, :], kk[b][64:128, blk, :], eRB[64:128, :], op=ALU.mult),
                     reads=[('eRB', 0), ('eRB', 1), ('kk', b), ('khi', p2)], writes=[('khi', p2)])
                for i in range(2):
                    fns = []
                    for h in range(4 * i, 4 * i + 4):
                        o0 = (h % 4) * 128
                        fns.append(I('matmul', bA[i][:, o0:o0 + 64], lhsT=kt[p2][:, h, :], rhs=qlo[p2][:, h, 0:64], start=True, stop=True))
                        fns.append(I('matmul', bA[i][:, o0 + 64:o0 + 128], lhsT=kt[p2][:, h, :], rhs=qhi[p2][:, h, 64:128], start=True, stop=True))
                    S.ops('pe', fns, reads=[('kt', p2), ('qlo', p2), ('qhi', p2), ('eB', i), ('eNB', i)], writes=[('bA', i)])
                    S.op('dve', I('tensor_tensor', sc[p2][:, 4 * i:4 * i + 4, :], bA[i][:].rearrange("p (h t) -> p h t", t=128),
                                  T1[:].unsqueeze(1).to_broadcast([128, 4, 128]), op=ALU.mult), reads=[('bA', i), 'T1'], writes=[('sc', p2, i)])
                for half, (ksrc, kkey, dst_) in enumerate(((klo[p2], ('klo', p2), Sbm[p2]), (khi[p2], ('khi', p2), Sbp[1 - p2]))):
                    dkey = 'Sbm' if half == 0 else 'Sbp'
                    dpar = p2 if half == 0 else 1 - p2
                    col = 63 if half == 0 else 127
                    for i in range(2):
                        S.ops('pe', [I('matmul', bC[i][:, (h % 4) * 128:(h % 4 + 1) * 128], lhsT=ksrc[:, h * 128:(h + 1) * 128],
                                       rhs=vv[b][:, blk, h * 128:(h + 1) * 128], start=True, stop=True) for h in range(4 * i, 4 * i + 4)],
                              reads=[kkey, ('vv', b)], writes=[('bC', i)])
                        for h in range(4 * i, 4 * i + 4):
                            S.op('dve', I('scalar_tensor_tensor', Sf[:, h, :], in0=Sf[:, h, :], scalar=eB[:, h, col:col + 1],
                                          in1=bC[i][:, (h % 4) * 128:(h % 4 + 1) * 128], op0=ALU.mult, op1=ALU.add),
                                 reads=[('bC', i), ('eB', i), ('Sf', h)], writes=[('Sf', h)])
                        S.op('pool', I('tensor_copy', dst_[:, 4 * i:4 * i + 4, :], Sf[:, 4 * i:4 * i + 4, :]),
                             reads=[('Sf', h) for h in range(4 * i, 4 * i + 4)], writes=[(dkey, dpar, h) for h in range(4 * i, 4 * i + 4)])
                for i in range(2):
                    fns = []
                    for h in range(4 * i, 4 * i + 4):
                        osl = slice((h % 4) * 128, (h % 4 + 1) * 128)
                        fns.append(I('matmul', bB[i][:, osl], lhsT=sc[p2][:, h, :], rhs=vv[b][:, blk, h * 128:(h + 1) * 128], start=True, stop=False))
                        fns.append(I('matmul', bB[i][:, osl], lhsT=qlo[p2][:, h, :], rhs=Sbp[p2][:, h, :], start=False, stop=False))
                        fns.append(I('matmul', bB[i][:, osl], lhsT=qhi[p2][:, h, :], rhs=Sbm[p2][:, h, :], start=False, stop=True))
                    S.ops('pe', fns, reads=[('sc', p2, i), ('vv', b), ('qlo', p2), ('qhi', p2), ('eRB', i)]
                          + [('Sbp', p2, h) for h in range(4 * i, 4 * i + 4)] + [('Sbm', p2, h) for h in range(4 * i, 4 * i + 4)],
                          writes=[('bB', i)])
                    S.op('act', I('activation', sq[:, 4 * i:4 * i + 4, :], bB[i][:].rearrange("p (h t) -> p h t", t=128), AF.Square),
                         reads=[('bB', i)], writes=[('sq', i)])
                S.op('dve', I('tensor_reduce', stt[:, 0, :], sq[:], axis=AX.X, op=ALU.add), reads=[('sq', 0), ('sq', 1)], writes=['ss'])
                S.op('act', I('activation', stt[:, 1, :], stt[:, 0, :], AF.Sqrt, scale=1.0 / 128, bias=C.epsc[:, 0:1]), reads=['ss'], writes=['sd'])
                S.op('dve', I('reciprocal', stt[:, 2, :], stt[:, 1, :]), reads=['sd'], writes=['rstd'])
                for i in range(2):
                    S.op('dve', I('tensor_tensor', xn[:, 4 * i:4 * i + 4, :], bB[i][:].rearrange("p (h t) -> p h t", t=128),
                                  stt[:, 2, 4 * i:4 * i + 4].unsqueeze(2).to_broadcast([128, 4, 128]), op=ALU.mult),
                         reads=[('bB', i), 'rstd'], writes=[('xn', i)])
                S.op('dve', I('tensor_tensor', yb[:], xn[:], gg[b][:, blk, :].rearrange("p (h t) -> p h t", t=128), op=ALU.mult),
                     reads=[('xn', 0), ('xn', 1), ('gg', b)], writes=['yb'])
                S.ops('pe', [I('transpose', bD[:, h * 128:(h + 1) * 128], yb[:, h, :], C.ident[:]) for h in range(8)], reads=['yb'], writes=['bD'])
                S.op('act', I('copy', yst[b][:, :, bsl], bD[:].rearrange("p (h t) -> p h t", t=128)), reads=['bD'], writes=[('yst', b, blk)])
            S.dma('pool', yv[:, :, tsl], yst[b][:], reads=[('yst', b, 0), ('yst', b, 1)])
        S.sync()
```

```python
import math
import numpy as np
import ml_dtypes
from contextlib import ExitStack
import concourse.bass as bass
import concourse.mybir as mybir
from concourse.bass_utils import run_bass_kernel_spmd

F32 = mybir.dt.float32
BF16 = mybir.dt.bfloat16
AF = mybir.ActivationFunctionType
ALU = mybir.AluOpType
AX = mybir.AxisListType

SEQ = 4096
DM = 1024
DFF = 2816
EPS = 1e-6
STOP_AFTER = None


class Sched:
    CE = ('pe', 'act', 'dve', 'pool')
    ALLQ = ('pe', 'act', 'dve', 'pool', 'sp')

    def __init__(self, nc, stack, ndma=24):
        self.nc = nc
        self.sem = {e: stack.enter_context(nc.semaphore('s_' + e)) for e in self.CE}
        self.cnt = {e: 0 for e in self.CE}
        self.dsem = [stack.enter_context(nc.semaphore('d%d' % i)) for i in range(ndma)]
        self.dcnt = [0] * ndma
        self.dnext = 0
        self.waited = {}
        self.lastw = {}
        self.readers = {}
        self.q = {e: [] for e in self.ALLQ}
        self.ninst = {e: 0 for e in self.ALLQ}

    def _semof(self, key):
        return self.sem[key[1]] if key[0] == 'c' else self.dsem[key[1]]

    def _deps(self, eng, reads, writes):
        raw = set()
        toks = set()
        for k in reads:
            t = self.lastw.get(k)
            if t is not None:
                raw.add(t)
        for k in writes:
            t = self.lastw.get(k)
            if t is not None:
                toks.add(t)
            r = self.readers.get(k)
            if r:
                toks.update(r.values())
        waits = {}
        for t in raw | toks:
            kind, who, val = t
            if kind == 'c' and who == eng and t not in raw:
                continue
            key = (kind, who)
            if self.waited.get((eng,) + key, 0) >= val:
                continue
            if waits.get(key, 0) < val:
                waits[key] = val
        for key, val in waits.items():
            self.waited[(eng,) + key] = val
        return waits

    def _record(self, tok, reads, writes):
        for k in writes:
            self.lastw[k] = tok
            self.readers[k] = {}
        for k in reads:
            d = self.readers.setdefault(k, {})
            if tok[0] == 'c':
                d[('c', tok[1])] = tok
            else:
                d[tok] = tok

    def op(self, eng, fn, reads=(), writes=()):
        return self.ops(eng, [fn], reads, writes)

    def ops(self, eng, fns, reads=(), writes=()):
        waits = self._deps(eng, reads, writes)
        self.cnt[eng] += 1
        tok = ('c', eng, self.cnt[eng])
        sem = self.sem[eng]
        wl = [(self._semof(k), v) for k, v in waits.items()]

        def run(e, fns=fns, wl=wl, sem=sem):
            for s, v in wl:
                e.wait_ge(s, v)
            for f in fns[:-1]:
                f(e)
            fns[-1](e).then_inc(sem, 1)
        self.q[eng].append(run)
        self.ninst[eng] += len(fns) + len(wl)
        self._record(tok, reads, writes)
        return tok

    def dma(self, qeng, out, in_, reads=(), writes=()):
        i = self.dnext
        self.dnext = (i + 1) % len(self.dsem)
        waits = self._deps(qeng, reads, writes)
        prev = self.dcnt[i]
        if prev and self.waited.get((qeng, 'd', i), 0) < prev:
            waits[('d', i)] = prev
            self.waited[(qeng, 'd', i)] = prev
        self.dcnt[i] += 16
        tok = ('d', i, self.dcnt[i])
        sem = self.dsem[i]
        wl = [(self._semof(k), v) for k, v in waits.items()]

        def run(e, wl=wl, sem=sem, out=out, in_=in_):
            for s, v in wl:
                e.wait_ge(s, v)
            e.dma_start(out=out, in_=in_).then_inc(sem, 16)
        self.q[qeng].append(run)
        self.ninst[qeng] += 1 + len(wl)
        self._record(tok, reads, writes)
        return tok

    def barrier(self):
        for eng in self.ALLQ:
            wl = []
            for e in self.CE:
                if e != eng and self.cnt[e] > self.waited.get((eng, 'c', e), 0):
                    wl.append((self.sem[e], self.cnt[e]))
                    self.waited[(eng, 'c', e)] = self.cnt[e]
            for i, c in enumerate(self.dcnt):
                if c > self.waited.get((eng, 'd', i), 0):
                    wl.append((self.dsem[i], c))
                    self.waited[(eng, 'd', i)] = c

            def run(e, wl=wl):
                for s, v in wl:
                    e.wait_ge(s, v)
            self.q[eng].append(run)
        self.lastw.clear()
        self.readers.clear()

    def flush(self):
        nc = self.nc
        q = self.q
        with nc.Block() as block:
            @block.tensor
            def _(e):
                for f in q['pe']:
                    f(e)

            @block.scalar
            def _(e):
                for f in q['act']:
                    f(e)

            @block.vector
            def _(e):
                for f in q['dve']:
                    f(e)

            @block.gpsimd
            def _(e):
                for f in q['pool']:
                    f(e)

            @block.sync
            def _(e):
                for f in q['sp']:
                    f(e)
        self.q = {e: [] for e in self.ALLQ}

    def sync(self):
        self.barrier()
        self.flush()


class Ctx:
    debug = False

    def tap(self, name, ap, shape, dt, keys):
        if not self.debug:
            return
        d = self.nc.dram_tensor("tap_" + name, list(shape), dt, kind="ExternalOutput").ap()
        self.S.dma('sp', d, ap, reads=list(keys))


_UID = [0]


def uniq(n):
    _UID[0] += 1
    return '%s_%d' % (n, _UID[0])


def I(name, *args, **kw):
    return lambda e: getattr(e, name)(*args, **kw)


def mm_group(S, ps_ap, pskey, pairs, reads):
    n = len(pairs)
    fns = [(I('matmul', ps_ap, lhsT=l, rhs=r, start=(i == 0), stop=(i == n - 1)))
           for i, (l, r) in enumerate(pairs)]
    S.ops('pe', fns, reads=reads, writes=[pskey])


def norm_pass(C, src, gidx, hnT, tok0, ntile):
    nc, S = C.nc, C.S
    srcv = src.rearrange("(c p) t -> p c t", p=128)
    with ExitStack() as st:
        T = lambda n, s, d=F32: st.enter_context(nc.sbuf_tensor(uniq(n), s, d))
        ht = [T("n_ht%d" % i, [128, 8, 512]) for i in range(3)]
        sq = [T("n_sq%d" % i, [128, 8, 512], BF16) for i in range(3)]
        rs = [T("n_rs%d" % i, [128, 512]) for i in range(3)]
        pss = [st.enter_context(nc.psum_tensor(uniq("n_ps%d" % i), [128, 512], F32)) for i in range(3)]
        for i in range(ntile):
            b = i % 3
            t0 = tok0 + i * 512
            S.dma('sp', ht[b][:], srcv[:, :, t0:t0 + 512], writes=[('ht', b)])
            S.op('act', I('activation', sq[b][:], ht[b][:], AF.Square), reads=[('ht', b)], writes=[('sq', b)])
            mm_group(S, pss[b][:], ('nps', b), [(C.ones[:], sq[b][:, c, :]) for c in range(8)], [('sq', b)])
            S.op('act', I('activation', rs[b][:], pss[b][:], AF.Sqrt, scale=1.0 / DM, bias=C.epsc[:, 0:1]),
                 reads=[('nps', b)], writes=[('rs', b)])
            S.op('dve', I('reciprocal', rs[b][:], rs[b][:]), reads=[('rs', b)], writes=[('rs', b)])
            for c in range(8):
                eng = 'dve'
                S.op(eng, I('scalar_tensor_tensor',
                    hnT[:, c, i * 512:(i + 1) * 512], in0=ht[b][:, c, :], scalar=C.gains[:, gidx, c:c + 1],
                    in1=rs[b][:], op0=ALU.mult, op1=ALU.mult), reads=[('ht', b), ('rs', b)], writes=[('hn', i, c)])
        S.sync()


def load_consts(C, st):
    nc, S = C.nc, C.S
    T = lambda n, s, d=F32: st.enter_context(nc.sbuf_tensor(uniq(n), s, d))
    C.ones = T("c_ones", [128, 128], BF16)
    C.ident = T("c_ident", [128, 128], BF16)
    C.gains = T("c_gains", [128, 5, 8])
    C.epsc = T("c_eps", [128, 1])
    C.onec = T("c_one", [128, 1])
    S.dma('sp', C.ones[:], C.din['ones'], writes=['c1'])
    S.dma('sp', C.ident[:], C.din['ident'], writes=['c2'])
    S.dma('sp', C.gains[:], C.din['gains'], writes=['c3'])
    S.op('dve', I('memset', C.epsc[:], EPS), writes=['c4'])
    S.op('dve', I('memset', C.onec[:], 1.0), writes=['c5'])
    S.sync()


def proj_phase(C, hnT, w_ap, jobs, ntok=SEQ):
    nc, S = C.nc, C.S
    wv = w_ap.rearrange("(c p) n -> p c n", p=128)
    with ExitStack() as st:
        T = lambda n, s, d=F32: st.enter_context(nc.sbuf_tensor(uniq(n), s, d))
        wf = [T("p_wf%d" % i, [128, 8, 512]) for i in range(2)]
        wb = [T("p_wb%d" % i, [128, 8, 512], BF16) for i in range(2)]
        wsw = [T("p_ws%d" % i, [128, 8, 512], BF16) for i in range(2)]
        psum = [st.enter_context(nc.psum_tensor(uniq("p_ps%d" % i), [128, 512], F32)) for i in range(6)]
        C.pp = 0

        def next_ps():
            i = C.pp % 6
            C.pp += 1
            return psum[i], ('pps', i)
        ntt = ntok // 512

        def prep(ji):
            job = jobs[ji]
            b = ji % 2
            c0 = job['c0']
            S.dma('sp', wf[b][:], wv[:, :, c0:c0 + 512], writes=[('wf', b)])
            S.op('pool', I('tensor_copy', wb[b][:], wf[b][:]), reads=[('wf', b)], writes=[('wb', b)])
            if job['role'] == 'rot':
                src = wf[b][:].rearrange("p c (h two j) -> p c h two j", two=2, j=32)
                dst = wsw[b][:].rearrange("p c (h two j) -> p c h two j", two=2, j=32)
                for c in range(8):
                    S.op('act', I('copy', dst[:, c, :, 0, :], src[:, c, :, 1, :]),
                         reads=[('wf', b)], writes=[('wsa', b, c)])
                    S.op('act', I('copy', dst[:, c, :, 1, :], src[:, c, :, 0, :]),
                         reads=[('wf', b)], writes=[('wsb', b, c)])
        if jobs:
            prep(0)
        for ji, job in enumerate(jobs):
            b = ji % 2
            if ji + 1 < len(jobs):
                prep(ji + 1)
            role = job['role']
            if role in ('fm', 'both', 'rot'):
                for sub in range(4):
                    for tt in range(ntt):
                        ps, pk = next_ps()
                        mm_group(S, ps[:], pk, [(wb[b][:, c, sub * 128:(sub + 1) * 128], hnT[:, c, tt * 512:(tt + 1) * 512])
                                                for c in range(8)], [('wb', b)])
                        if role == 'rot':
                            ps2, pk2 = next_ps()
                            mm_group(S, ps2[:], pk2, [(wsw[b][:, c, sub * 128:(sub + 1) * 128], hnT[:, c, tt * 512:(tt + 1) * 512])
                                                      for c in range(8)],
                                     [('wsa', b, c) for c in range(8)] + [('wsb', b, c) for c in range(8)])
                            job['epi_fm'](sub, tt, ps, pk, ps2, pk2)
                        else:
                            job['epi_fm'](sub, tt, ps, pk)
            if role in ('tm', 'both'):
                for t in range(ntok // 128):
                    ps, pk = next_ps()
                    mm_group(S, ps[:], pk, [(hnT[:, c, t * 128:(t + 1) * 128], wb[b][:, c, :]) for c in range(8)], [('wb', b)])
                    job['epi_tm'](t, ps, pk)
        S.sync()


class Stager:
    def __init__(self, C, st, name, shape, dt, n=4):
        self.C = C
        self.bufs = [st.enter_context(C.nc.sbuf_tensor(uniq("%s%d" % (name, i)), shape, dt)) for i in range(n)]
        self.name = name
        self.i = 0

    def next(self):
        i = self.i % len(self.bufs)
        self.i += 1
        return self.bufs[i], (self.name, i)

    def store(self, dst_ap, buf_ap, key, wkeys=()):
        self.C.S.dma('pool', dst_ap, buf_ap, reads=[key], writes=list(wkeys))


def phase_A0(C):
    nc, S, D = C.nc, C.S, C.dsc
    with ExitStack() as st:
        T = lambda n, s, d=F32: st.enter_context(nc.sbuf_tensor(uniq(n), s, d))
        hnT = T("a0_hn", [128, 8, SEQ], BF16)
        norm_pass(C, C.din['xT'], 0, hnT, 0, 8)
        rot = T("a0_rot", [128, 2, SEQ])
        retn = T("a0_retn", [128, 1024])
        S.dma('sp', rot[:], C.din['rot'], writes=['rot'])
        S.dma('sp', retn[:], C.din['retn'], writes=['retn'])
        sb = Stager(C, st, "a0_sb", [128, 512], BF16, 4)
        t1 = [T("a0_t1%d" % i, [128, 512]) for i in range(2)]
        t2 = [T("a0_t2%d" % i, [128, 512]) for i in range(2)]
        cnt = [0]

        def epi_rot(dst, kscale):
            def f(sub, tt, ps, pk, ps2, pk2, dst=dst, kscale=kscale):
                i = cnt[0] % 2
                cnt[0] += 1
                tsl = slice(tt * 512, (tt + 1) * 512)
                S.op('dve', I('scalar_tensor_tensor', t1[i][:], in0=ps[:], scalar=kscale, in1=rot[:, 0, tsl],
                                                             op0=ALU.mult, op1=ALU.mult), reads=[pk, 'rot'], writes=[('t1', i)])
                S.op('dve', I('scalar_tensor_tensor', t2[i][:], in0=ps2[:], scalar=kscale, in1=rot[:, 1, tsl],
                                                             op0=ALU.mult, op1=ALU.mult), reads=[pk2, 'rot'], writes=[('t2', i)])
                buf, bk = sb.next()
                S.op('dve', I('tensor_tensor', buf[:], t1[i][:], t2[i][:], op=ALU.add),
                     reads=[('t1', i), ('t2', i)], writes=[bk])
                sb.store(dst[sub * 128:(sub + 1) * 128, tsl], buf[:], bk)
            return f

        def epi_fm_scale(dst, scale):
            def f(sub, tt, ps, pk, dst=dst, scale=scale):
                buf, bk = sb.next()
                S.op('act', I('activation', buf[:], ps[:], AF.Identity, scale=scale), reads=[pk], writes=[bk])
                sb.store(dst[sub * 128:(sub + 1) * 128, tt * 512:(tt + 1) * 512], buf[:], bk)
            return f

        def epi_tm_copy(dst, cb):
            def f(t, ps, pk, dst=dst, cb=cb):
                buf, bk = sb.next()
                S.op('act', I('copy', buf[:], ps[:]), reads=[pk], writes=[bk])
                sb.store(dst[t * 128:(t + 1) * 128, cb * 512:(cb + 1) * 512], buf[:], bk)
            return f

        def epi_tm_gate(dst, cb):
            def f(t, ps, pk, dst=dst, cb=cb):
                i = cnt[0] % 2
                cnt[0] += 1
                S.op('act', I('activation', t1[i][:], ps[:], AF.Silu), reads=[pk], writes=[('t1', i)])
                buf, bk = sb.next()
                S.op('dve', I('tensor_tensor', buf[:], t1[i][:], retn[:, cb * 512:(cb + 1) * 512], op=ALU.mult),
                     reads=[('t1', i), 'retn'], writes=[bk])
                sb.store(dst[t * 128:(t + 1) * 128, cb * 512:(cb + 1) * 512], buf[:], bk)
            return f
        jobs = [dict(c0=0, role='rot', epi_fm=epi_rot(D['qaT'], 1.0)),
                dict(c0=512, role='rot', epi_fm=epi_rot(D['kaT'], 0.125)),
                dict(c0=1024, role='tm', epi_tm=epi_tm_copy(D['va'], 0)),
                dict(c0=1536, role='tm', epi_tm=epi_tm_copy(D['va'], 1)),
                dict(c0=2048, role='tm', epi_tm=epi_tm_gate(D['ga'], 0)),
                dict(c0=2560, role='tm', epi_tm=epi_tm_gate(D['ga'], 1)),
                dict(c0=3072, role='fm', epi_fm=epi_fm_scale(D['qbT'], 0.125)),
                dict(c0=3584, role='fm', epi_fm=epi_fm_scale(D['kbT'], 1.0)),
                dict(c0=4096, role='tm', epi_tm=epi_tm_copy(D['vb'], 0))]
        if JOBSEL is not None:
            jobs = [jobs[i] for i in JOBSEL]
        proj_phase(C, hnT, C.din['w_in0'], jobs)


def phase_outproj(C, yT_d, KC, w_ap, h_src, h_dst):
    nc, S = C.nc, C.S
    wv = w_ap.rearrange("(c p) n -> p c n", p=128)
    with ExitStack() as st:
        T = lambda n, s, d=F32: st.enter_context(nc.sbuf_tensor(uniq(n), s, d))
        yT = T("o_y", [128, KC, SEQ], BF16)
        yv = yT_d.rearrange("(c p) t -> p c t", p=128)
        for c in range(KC):
            S.dma('sp', yT[:, c, :], yv[:, c, :], writes=[('y', c)])
        wf = [T("o_wf%d" % i, [128, KC, 128]) for i in range(2)]
        wb = [T("o_wb%d" % i, [128, KC, 128], BF16) for i in range(2)]
        hb = [T("o_h%d" % i, [128, 512]) for i in range(3)]
        psum = [st.enter_context(nc.psum_tensor(uniq("o_ps%d" % i), [128, 512], F32)) for i in range(4)]
        k = 0

        def prep(dmb):
            b = dmb % 2
            S.dma('sp', wf[b][:], wv[:, :, dmb * 128:(dmb + 1) * 128], writes=[('wf', b)])
            S.op('pool', I('tensor_copy', wb[b][:], wf[b][:]), reads=[('wf', b)], writes=[('wb', b)])
        prep(0)
        for dmb in range(8):
            b = dmb % 2
            if dmb + 1 < 8:
                prep(dmb + 1)
            for tt in range(8):
                pi = k % 4
                hi = k % 3
                k += 1
                tsl = slice(tt * 512, (tt + 1) * 512)
                rsl = slice(dmb * 128, (dmb + 1) * 128)
                S.dma('sp', hb[hi][:], h_src[rsl, tsl], reads=[('hd', dmb, tt)], writes=[('hb', hi)])
                mm_group(S, psum[pi][:], ('ops', pi), [(wb[b][:, c, :], yT[:, c, tsl]) for c in range(KC)],
                         [('wb', b)] + [('y', c) for c in range(KC)])
                S.op('dve', I('tensor_tensor', hb[hi][:], psum[pi][:], hb[hi][:], op=ALU.add),
                     reads=[('ops', pi), ('hb', hi)], writes=[('hb', hi)])
                S.dma('pool', h_dst[rsl, tsl], hb[hi][:], reads=[('hb', hi)], writes=[('hd', dmb, tt)])
        S.sync()


def phase_ffn(C, layer, hT):
    nc, S = C.nc, C.S
    ST = 2048
    NJ = DFF // 128
    wup = C.din['w_up'][layer].rearrange("(c p) n -> p c n", p=128)
    wdn = C.din['w_down'][layer].rearrange("(j p) n -> p j n", p=128)
    with ExitStack() as st0:
        T0 = lambda n, s, d=F32: st0.enter_context(nc.sbuf_tensor(uniq(n), s, d))
        m = T0("f_m", [128, NJ, ST], BF16)
        halo = T0("f_halo", [128, NJ, 2, 2])
        cp = T0("f_cp", [128, 4, 44])
        S.dma('sp', cp[:], C.din['convp'][:, layer], writes=['cp'])
        S.op('dve', I('memset', halo[:], 0.0), writes=['halo'])
        S.sync()
        for sti in range(SEQ // ST):
            tok0 = sti * ST
            with ExitStack() as st:
                T = lambda n, s, d=F32: st.enter_context(nc.sbuf_tensor(uniq(n), s, d))
                hnT = T("f_hn", [128, 8, ST], BF16)
                norm_pass(C, hT, 1 + 2 * layer, hnT, tok0, ST // 512)
                u = [T("f_u%d" % i, [128, ST + 2]) for i in range(2)]
                ccd = [[T("f_c%d_%d" % (par, i), [128, ST]) for i in range(2)] for par in range(2)]
                wf = [T("f_wf%d" % i, [128, 8, 2, 128]) for i in range(2)]
                wb = [T("f_wb%d" % i, [128, 8, 2, 128], BF16) for i in range(2)]
                psum = [st.enter_context(nc.psum_tensor(uniq("f_ps%d" % i), [128, 512], F32)) for i in range(8)]
                def prep_up(j):
                    b = j % 2
                    for ab in range(2):
                        c0 = ab * DFF + j * 128
                        S.dma('sp', wf[b][:, :, ab, :], wup[:, :, c0:c0 + 128], writes=[('wf', b, ab)])
                    S.op('pool', I('tensor_copy', wb[b][:], wf[b][:]), reads=[('wf', b, 0), ('wf', b, 1)],
                         writes=[('wb', b)])
                prep_up(0)
                for j in range(NJ):
                    b = j % 2
                    cc = ccd[j % 2]
                    cp_ = j % 2
                    if j + 1 < NJ:
                        prep_up(j + 1)
                    for ab in range(2):
                        S.op('pool', I('tensor_copy', u[ab][:, 0:2], halo[:, j, ab, :]),
                             reads=['halo', ('hl', j, ab)], writes=[('u', ab)])
                        for tt in range(ST // 512):
                            pi = ab * 4 + tt
                            mm_group(S, psum[pi][:], ('fps', pi),
                                     [(wb[b][:, c, ab, :], hnT[:, c, tt * 512:(tt + 1) * 512]) for c in range(8)], [('wb', b)])
                            S.op('act', I('copy', u[ab][:, 2 + tt * 512:2 + (tt + 1) * 512], psum[pi][:]),
                                 reads=[('fps', pi)], writes=[('u', ab, tt)])
                        ukeys = [('u', ab)] + [('u', ab, tt) for tt in range(ST // 512)]
                        jb = ab * NJ + j
                        S.op('pool', I('tensor_copy', halo[:, j, ab, :], u[ab][:, ST:ST + 2]),
                             reads=ukeys, writes=[('hl', j, ab)])
                        S.op('act', I('activation', cc[ab][:], u[ab][:, 2:ST + 2], AF.Identity,
                                                                         scale=cp[:, 2, jb:jb + 1], bias=cp[:, 3, jb:jb + 1]),
                             reads=ukeys + ['cp'], writes=[('cc', cp_, ab)])
                        eng = 'dve'
                        S.op(eng, I('scalar_tensor_tensor', cc[ab][:], in0=u[ab][:, 1:ST + 1], scalar=cp[:, 1, jb:jb + 1],
                                                                                 in1=cc[ab][:], op0=ALU.mult, op1=ALU.add),
                             reads=ukeys + [('cc', cp_, ab), 'cp'], writes=[('cc', cp_, ab)])
                        S.op(eng, I('scalar_tensor_tensor', cc[ab][:], in0=u[ab][:, 0:ST], scalar=cp[:, 0, jb:jb + 1],
                                                                                 in1=cc[ab][:], op0=ALU.mult, op1=ALU.add),
                             reads=ukeys + [('cc', cp_, ab), 'cp'], writes=[('cc', cp_, ab)])
                    S.op('act', I('activation', cc[0][:], cc[0][:], AF.Silu), reads=[('cc', cp_, 0)], writes=[('cc', cp_, 0)])
                    S.op('dve', I('tensor_tensor', m[:, j, :], cc[0][:], cc[1][:], op=ALU.mult),
                         reads=[('cc', cp_, 0), ('cc', cp_, 1)], writes=[('m', j)])
                S.sync()
            with ExitStack() as st:
                T = lambda n, s, d=F32: st.enter_context(nc.sbuf_tensor(uniq(n), s, d))
                wf = [T("g_wf%d" % i, [128, NJ, 128]) for i in range(2)]
                wb = [T("g_wb%d" % i, [128, NJ, 128], BF16) for i in range(2)]
                hb = [T("g_h%d" % i, [128, 512]) for i in range(3)]
                psum = [st.enter_context(nc.psum_tensor(uniq("g_ps%d" % i), [128, 512], F32)) for i in range(4)]
                k = 0

                def prep_dn(dmb):
                    b = dmb % 2
                    S.dma('sp', wf[b][:, 0:11, :], wdn[:, 0:11, dmb * 128:(dmb + 1) * 128], writes=[('wf', b, 0)])
                    S.dma('sp', wf[b][:, 11:22, :], wdn[:, 11:22, dmb * 128:(dmb + 1) * 128], writes=[('wf', b, 1)])
                    S.op('pool', I('tensor_copy', wb[b][:], wf[b][:]), reads=[('wf', b, 0), ('wf', b, 1)], writes=[('wb', b)])
                prep_dn(0)
                for dmb in range(8):
                    b = dmb % 2
                    if dmb + 1 < 8:
                        prep_dn(dmb + 1)
                    for tt in range(ST // 512):
                        pi = k % 4
                        hi = k % 3
                        k += 1
                        tsl = slice(tok0 + tt * 512, tok0 + (tt + 1) * 512)
                        rsl = slice(dmb * 128, (dmb + 1) * 128)
                        S.dma('sp', hb[hi][:], hT[rsl, tsl], writes=[('hb', hi)])
                        mm_group(S, psum[pi][:], ('gps', pi), [(wb[b][:, j, :], m[:, j, tt * 512:(tt + 1) * 512]) for j in range(NJ)],
                                 [('wb', b)])
                        S.op('dve', I('tensor_tensor', hb[hi][:], psum[pi][:], hb[hi][:], op=ALU.add),
                             reads=[('gps', pi), ('hb', hi)], writes=[('hb', hi)])
                        S.dma('pool', hT[rsl, tsl], hb[hi][:], reads=[('hb', hi)])
                S.sync()


def phase_final(C, hT, outT):
    nc, S = C.nc, C.S
    with ExitStack() as st:
        T = lambda n, s, d=F32: st.enter_context(nc.sbuf_tensor(uniq(n), s, d))
        ht = [T("z_ht%d" % i, [128, 8, 512]) for i in range(3)]
        sq = [T("z_sq%d" % i, [128, 8, 512], BF16) for i in range(3)]
        rs = [T("z_rs%d" % i, [128, 512]) for i in range(3)]
        pss = [st.enter_context(nc.psum_tensor(uniq("z_ps%d" % i), [128, 512], F32)) for i in range(3)]
        srcv = hT.rearrange("(c p) t -> p c t", p=128)
        dstv = outT.rearrange("(c p) t -> p c t", p=128)
        for i in range(8):
            b = i % 3
            tsl = slice(i * 512, (i + 1) * 512)
            hk = [('ht', b, c) for c in range(8)]
            S.dma('sp', ht[b][:], srcv[:, :, tsl], writes=hk)
            S.op('act', I('activation', sq[b][:], ht[b][:], AF.Square), reads=hk, writes=[('sq', b)])
            mm_group(S, pss[b][:], ('nps', b), [(C.ones[:], sq[b][:, c, :]) for c in range(8)], [('sq', b)])
            S.op('act', I('activation', rs[b][:], pss[b][:], AF.Sqrt, scale=1.0 / DM, bias=C.epsc[:, 0:1]),
                 reads=[('nps', b)], writes=[('rs', b)])
            S.op('dve', I('reciprocal', rs[b][:], rs[b][:]), reads=[('rs', b)], writes=[('rs', b)])
            for c in range(8):
                eng = 'dve'
                S.op(eng, I('scalar_tensor_tensor',
                    ht[b][:, c, :], in0=ht[b][:, c, :], scalar=C.gains[:, 4, c:c + 1],
                    in1=rs[b][:], op0=ALU.mult, op1=ALU.mult), reads=[('ht', b, c), ('rs', b), ('sq', b)], writes=[('ht', b, c)])
            S.dma('pool', dstv[:, :, tsl], ht[b][:], reads=hk)
        S.sync()


PHASES = []


def build(debug=False, stop_after=None, feed=(), skip=()):
    nc = bass.Bass("TRN2", target_bir_lowering=False)
    C = Ctx()
    C.nc = nc
    C.debug = debug
    kindS = "ExternalOutput" if debug else "Internal"
    C.din = {}
    C.dsc = {}

    def din(name, shape, dt=F32):
        C.din[name] = nc.dram_tensor(name, shape, dt, kind="ExternalInput").ap()

    def dsc(name, shape, dt=BF16):
        C.dsc[name] = nc.dram_tensor(name, shape, dt, kind=("ExternalInput" if name in feed else kindS)).ap()
    din('xT', [DM, SEQ])
    din('w_in0', [DM, 4608])
    din('w_out0', [1536, DM])
    din('w_in1', [DM, 4096])
    din('w_out1', [DM, DM])
    din('w_up', [2, DM, 2 * DFF])
    din('w_down', [2, DFF, DM])
    din('gains', [128, 5, 8])
    din('convp', [128, 2, 4, 44])
    din('retn', [128, 1024])
    din('rot', [128, 2, SEQ])
    din('ones', [128, 128], BF16)
    din('ident', [128, 128], BF16)
    din('dect', [128, 8, 128])
    din('gq', [128, 8, 128])
    din('gk', [128, 8, 64])
    din('cdr', [128, 4])
    din('biasT', [128, 24, 256])
    din('mask2', [128, 256])
    din('lbrep', [128, 2, 1024])
    din('lbfm', [128, 2, 8])
    din('hgn', [128, 1024])
    din('t1m', [128, 128], BF16)
    din('t2m', [128, 128], BF16)
    for n in ('qaT', 'kaT', 'qbT', 'kbT'):
        dsc(n, [512, SEQ])
    dsc('va', [SEQ, 1024])
    dsc('ga', [SEQ, 1024])
    dsc('vb', [SEQ, 512])
    dsc('yT', [1536, SEQ])
    dsc('hT', [DM, SEQ], F32)
    dsc('q1T', [1024, SEQ])
    dsc('k1T', [1024, SEQ])
    dsc('lfh', [SEQ, 1024])
    dsc('lfl', [SEQ, 1024])
    dsc('k1', [SEQ, 1024])
    dsc('v1', [SEQ, 1024])
    dsc('g1', [SEQ, 1024])
    dsc('y1T', [1024, SEQ])
    if debug:
        dsc('dbg_ret', [SEQ, 1024], F32)
    outT = nc.dram_tensor('outT', [DM, SEQ], F32, kind="ExternalOutput").ap()
    with ExitStack() as st:
        C.S = Sched(nc, st)
        load_consts(C, st)
        plan = [
            ('A0', lambda: phase_A0(C)),
            ('B0', lambda: phase_B0(C)),
            ('C0', lambda: phase_C0(C)),
            ('D0', lambda: phase_outproj(C, C.dsc['yT'], 12, C.din['w_out0'], C.din['xT'], C.dsc['hT'])),
            ('E0', lambda: phase_ffn(C, 0, C.dsc['hT'])),
            ('A1', lambda: phase_A1(C, C.dsc['hT'])),
            ('B1', lambda: phase_B1(C)),
            ('D1', lambda: phase_outproj(C, C.dsc['y1T'], 8, C.din['w_out1'], C.dsc['hT'], C.dsc['hT'])),
            ('E1', lambda: phase_ffn(C, 1, C.dsc['hT'])),
            ('Z', lambda: phase_final(C, C.dsc['hT'], outT)),
        ]
        for name, fn in plan:
            if name in skip:
                continue
            fn()
            if stop_after == name:
                break
    print("ninst", C.S.ninst, "cnt", C.S.cnt, "dcnt max", max(C.S.dcnt))
    return nc


C_SKIP = set()
JOBSEL = None
B0_LEVEL = 9
TAPN = 0
B0_SUB = 9


def host_inputs(inputs, b):
    f = np.float32
    x = np.asarray(inputs['x'], f)
    d = {}
    d['xT'] = np.ascontiguousarray(x[b].T)
    d['w_in0'] = np.ascontiguousarray(inputs['even_w_in'][0], f)
    d['w_out0'] = np.ascontiguousarray(inputs['even_w_out'][0], f)
    d['w_in1'] = np.ascontiguousarray(inputs['odd_w_in'][0], f)
    d['w_out1'] = np.ascontiguousarray(inputs['odd_w_out'][0], f)
    d['w_up'] = np.ascontiguousarray(inputs['ffn_w_up'], f)
    d['w_down'] = np.ascontiguousarray(inputs['ffn_w_down'], f)
    g = np.stack([inputs['mix_norm'][0], inputs['ffn_norm'][0], inputs['mix_norm'][1], inputs['ffn_norm'][1],
                  inputs['final_norm']], 0).astype(f)
    d['gains'] = np.ascontiguousarray(g.reshape(5, 8, 128).transpose(2, 0, 1))
    cw = np.asarray(inputs['ffn_conv_w'], f)
    cb = np.asarray(inputs['ffn_conv_b'], f)
    cp = np.concatenate([cw, cb[:, None, :]], 1)
    d['convp'] = np.ascontiguousarray(cp.reshape(2, 4, 44, 128).transpose(3, 0, 1, 2))
    d['retn'] = np.ascontiguousarray(np.broadcast_to(np.asarray(inputs['ret_norm'], f)[0][None, :], (128, 1024)))
    j = np.arange(128) % 64
    inv = (10000.0 ** (-np.arange(0, 64, 2, dtype=np.float32) / 64)).astype(f)
    ang = np.arange(SEQ, dtype=f)[None, :] * inv[j % 32][:, None]
    cos = np.cos(ang).astype(f)
    sin = np.sin(ang).astype(f)
    sgn = np.where(j < 32, -1.0, 1.0).astype(f)[:, None]
    d['rot'] = np.ascontiguousarray(np.stack([cos, sin * sgn], 1))
    d['ones'] = np.ones((128, 128), ml_dtypes.bfloat16)
    gam = 1.0 - 2.0 ** (-5.0 - np.arange(8, dtype=np.float64))
    ii = np.arange(128)
    diff = ii[None, :] - ii[:, None]
    dec = np.where(diff[:, None, :] >= 0, gam[None, :, None] ** np.maximum(diff, 0)[:, None, :], 0.0)
    d['dect'] = np.ascontiguousarray(dec.astype(f))
    hp = 2 * np.arange(4)[None, :] + (np.arange(128) // 64)[:, None]
    d['gq'] = np.ascontiguousarray(np.broadcast_to((gam[:, None] ** (ii[None, :] + 1.0))[None], (128, 8, 128)).astype(f))
    d['gk'] = np.ascontiguousarray(np.broadcast_to((gam[None, :] ** (127.0 - ii[:, None]))[:, :, None], (128, 8, 64)).astype(f))
    d['cdr'] = np.ascontiguousarray((gam[hp] ** 128.0).astype(f))
    d['ident'] = np.eye(128).astype(ml_dtypes.bfloat16)
    lb = np.asarray(inputs['hgrn_lb'], f)
    d['lbrep'] = np.ascontiguousarray(np.broadcast_to(lb[None], (128, 2, 1024)))
    d['lbfm'] = np.ascontiguousarray(lb.reshape(2, 8, 128).transpose(2, 0, 1))
    d['hgn'] = np.ascontiguousarray(np.broadcast_to(np.asarray(inputs['hgrn_norm'], f)[0][None, :], (128, 1024)))
    jj = np.arange(128)
    same = (jj[:, None] // 64) == (jj[None, :] // 64)
    d['t1m'] = np.ascontiguousarray((same & (jj[:, None] <= jj[None, :])).astype(ml_dtypes.bfloat16))
    d['t2m'] = np.ascontiguousarray((same & (jj[:, None] > jj[None, :])).astype(ml_dtypes.bfloat16))
    rb = np.asarray(inputs['rel_bias'], f)
    cc_ = np.arange(128)[:, None]
    aa_ = np.arange(128)[None, :]
    dist2 = np.stack([aa_ - cc_, 128 + aa_ - cc_], 0)
    valid = np.stack([aa_ >= cc_, aa_ <= cc_], 0)
    bt = np.zeros((128, 3, 8, 2, 128), f)
    for bi, r in enumerate(DIL_R):
        dd_ = (np.maximum(dist2, 0) * r).astype(np.int64)
        df = dd_.astype(np.float32)
        large = 16 + (np.log(np.maximum(df, np.float32(1.0)) / np.float32(16)) / np.float32(math.log(2048 / 16)) * np.float32(16)).astype(np.int32)
        large = np.minimum(large, 31)
        bucket = np.where(dd_ < 16, dd_, large)
        gb = rb[bucket]
        gb = np.where(valid[..., None], gb, 0.0)
        bt[:, bi] = gb.transpose(1, 3, 0, 2)
    d['biasT'] = np.ascontiguousarray(bt.reshape(128, 24, 256))
    d['mask2'] = np.ascontiguousarray(valid.transpose(1, 0, 2).reshape(128, 256).astype(f))
    return d


_NC = {}


def kernel(**inputs):
    if 'nc' not in _NC:
        _NC['nc'] = build()
    nc = _NC['nc']
    in_maps = [host_inputs(inputs, c % 4) for c in range(8)]
    res = run_bass_kernel_spmd(nc, in_maps, core_ids=list(range(8)))
    out = np.stack([np.ascontiguousarray(res.results[b]['outT'].T) for b in range(4)], 0)
    return out.astype(np.float32)


def phase_B0(C):
    nc, S, D = C.nc, C.S, C.dsc
    with ExitStack() as st:
        T = lambda n, s, d=F32: st.enter_context(nc.sbuf_tensor(uniq(n), s, d))
        P = lambda n, s, d=F32: st.enter_context(nc.psum_tensor(uniq(n), s, d))
        dect = T("r_dec", [128, 8, 128])
        gq = T("r_gq", [128, 8, 128])
        gk = T("r_gk", [128, 8, 64])
        cdr = T("r_cd", [128, 4])
        S.dma('sp', dect[:], C.din['dect'], writes=['dect'])
        S.dma('sp', gq[:], C.din['gq'], writes=['gq'])
        S.dma('sp', gk[:], C.din['gk'], writes=['gk'])
        S.dma('sp', cdr[:], C.din['cdr'], writes=['cdr'])
        Sf = T("r_S", [128, 4, 256])
        Sb = [T("r_Sb%d" % i, [128, 4, 256], BF16) for i in range(2)]
        S.op('dve', I('memset', Sf[:], 0.0), writes=['Sf'])
        S.op('dve', I('memset', Sb[0][:], 0.0), writes=[('Sb', 0)])
        qT = [T("r_q%d" % i, [128, 8, 512], BF16) for i in range(2)]
        for i in range(2):
            S.op("dve", I('memset', qT[i][:], 0.0), writes=[("qT", i)])
        kT = [T("r_k%d" % i, [128, 4, 512], BF16) for i in range(2)]
        va = [T("r_v%d" % i, [128, 4, 1024], BF16) for i in range(2)]
        ga = [T("r_g%d" % i, [128, 4, 1024], BF16) for i in range(2)]
        kout = [T("r_ko%d" % i, [128, 8, 64], BF16) for i in range(2)]
        qin = [T("r_qi%d" % i, [128, 8, 128], BF16) for i in range(2)]
        sc = [T("r_sc%d" % i, [128, 8, 128], BF16) for i in range(2)]
        xs = T("r_xs", [128, 8, 128])
        sqb = T("r_sq", [128, 8, 128])
        xn = T("r_xn", [128, 8, 128])
        ybs = [T("r_yb%d" % i, [128, 8, 128], BF16) for i in range(2)]
        stt = T("r_stat", [128, 8, 8])
        yst = [T("r_yst%d" % i, [128, 8, 512], BF16) for i in range(2)]
        ps_kt = P("r_pkt", [128, 512], BF16)
        ps_s = [P("r_ps%d" % i, [128, 512]) for i in range(2)]
        ps_o = [P("r_po%d" % i, [128, 512]) for i in range(2)]
        ps_inc = [P("r_pi%d" % i, [128, 512]) for i in range(2)]
        ps_yt = P("r_pyt", [128, 1024], BF16)
        qv = D['qaT'].rearrange("(g two d) t -> two d g t", two=2, d=64)
        kv = D['kaT'].rearrange("(g p) t -> p g t", p=128)
        vv = D['va'].rearrange("(c p) e -> p c e", p=128)
        gv = D['ga'].rearrange("(c p) e -> p c e", p=128)
        yv = D['yT'].rearrange("(h p) t -> p h t", p=128)
        state = dict(sbi=0)
        NCH = SEQ // 128

        def loads(sci):
            b = sci % 2
            tsl = slice(sci * 512, (sci + 1) * 512)
            for half in range(2):
                S.dma('sp', qT[b][64 * half:64 * half + 64, half:8:2, :], qv[half][:, :, tsl], reads=[('qT', b)], writes=[('qT', b, half)])
            S.dma('sp', kT[b][:], kv[:, :, tsl], writes=[('kT', b)])
            S.dma('sp', va[b][:], vv[:, sci * 4:(sci + 1) * 4, :], writes=[('va', b)])
            S.dma('sp', ga[b][:], gv[:, sci * 4:(sci + 1) * 4, :], writes=[('ga', b)])

        def front(n):
            sci, c4 = divmod(n, 4)
            b, kb = sci % 2, n % 2
            csl = slice(c4 * 128, (c4 + 1) * 128)
            S.ops('pe', [I('transpose', ps_kt[:, g * 128:(g + 1) * 128], kT[b][:, g, csl], C.ident[:]) for g in range(4)],
                  reads=[('kT', b)], writes=['pskt'])
            S.op('dve', I('tensor_tensor', kout[kb][:], ps_kt[:].rearrange("p (h d) -> p h d", d=64), gk[:], op=ALU.mult),
                 reads=['pskt', 'gk'], writes=[('kout', kb)])
            S.op('dve', I('tensor_tensor', qin[kb][:], qT[b][:, :, csl], gq[:], op=ALU.mult),
                 reads=[('qT', b, 0), ('qT', b, 1), 'gq'], writes=[('qin', kb)])
            fns = []
            for h in range(8):
                g = h // 2
                fns.append(I('matmul', ps_s[h % 2][:, (h // 2) * 128:(h // 2 + 1) * 128], lhsT=kT[b][:, g, csl],
                             rhs=qT[b][:, h, csl], start=True, stop=True))
            S.ops('pe', fns, reads=[('kT', b), ('qT', b, 0), ('qT', b, 1)], writes=[('pss', 0), ('pss', 1)])
            for i in range(2):
                S.op('dve', I('tensor_tensor', sc[kb][:, i:8:2, :], ps_s[i][:].rearrange("p (h t) -> p h t", t=128),
                              dect[:, i:8:2, :], op=ALU.mult), reads=[('pss', i), 'dect'], writes=[('sc', kb, i)])

        def back(n):
            sci, c4 = divmod(n, 4)
            b, kb = sci % 2, n % 2
            sbi = state['sbi']
            fns = []
            for h in range(8):
                g = h // 2
                osl = slice((h // 2) * 128, (h // 2 + 1) * 128)
                fns.append(I('matmul', ps_o[h % 2][:, osl], lhsT=sc[kb][:, h, :], rhs=va[b][:, c4, h * 128:(h + 1) * 128], start=True, stop=False))
                fns.append(I('matmul', ps_o[h % 2][:, osl], lhsT=qin[kb][:, h, :],
                             rhs=Sb[sbi][:, g, (h % 2) * 128:(h % 2 + 1) * 128], start=False, stop=True))
            S.ops('pe', fns, reads=[('sc', kb, 0), ('sc', kb, 1), ('va', b), ('qin', kb), ('Sb', sbi)], writes=[('pso', 0), ('pso', 1)])
            for i in range(2):
                fns = []
                for g in range(2 * i, 2 * i + 2):
                    fns.append(I('matmul', ps_inc[i][:, (g % 2) * 256:(g % 2 + 1) * 256],
                                 lhsT=kout[kb][:, 2 * g:2 * g + 2, :].rearrange("p h d -> p (h d)"),
                                 rhs=va[b][:, c4, g * 256:(g + 1) * 256], start=True, stop=True))
                S.ops('pe', fns, reads=[('kout', kb), ('va', b)], writes=[('psi', i)])
                for g in range(2 * i, 2 * i + 2):
                    S.op('dve', I('scalar_tensor_tensor', Sf[:, g, :], in0=Sf[:, g, :], scalar=cdr[:, g:g + 1],
                                  in1=ps_inc[i][:, (g % 2) * 256:(g % 2 + 1) * 256], op0=ALU.mult, op1=ALU.add),
                         reads=[('psi', i), 'cdr', ('Sf', g)], writes=[('Sf', g)])
            S.op('pool', I('tensor_copy', Sb[1 - sbi][:], Sf[:]), reads=[('Sf', g) for g in range(4)] + ['Sf'], writes=[('Sb', 1 - sbi)])
            state['sbi'] = 1 - sbi
            yb = ybs[n % 2]
            for i in range(2):
                S.op('act', I('copy', xs[:, i:8:2, :], ps_o[i][:].rearrange("p (h t) -> p h t", t=128)), reads=[('pso', i)], writes=[('xs', i)])
            xk = [('xs', 0), ('xs', 1)]
            if 'dbg_ret' in D:
                S.dma('sp', D['dbg_ret'][n * 128:(n + 1) * 128, :], xs[:].rearrange("p h t -> p (h t)"), reads=xk)
            S.op('dve', I('tensor_reduce', stt[:, 0, :], xs[:], axis=AX.X, op=ALU.add), reads=xk, writes=['sums'])
            S.op('act', I('activation', sqb[:], xs[:], AF.Square), reads=xk, writes=['sqb'])
            S.op('dve', I('tensor_reduce', stt[:, 1, :], sqb[:], axis=AX.X, op=ALU.add), reads=['sqb'], writes=['sumsq'])
            S.op('dve', I('tensor_scalar', stt[:, 2, :], stt[:, 0, :], 1.0 / 128, None, op0=ALU.mult), reads=['sums'], writes=['mean'])
            S.op('dve', I('tensor_tensor', stt[:, 3, :], stt[:, 2, :], stt[:, 2, :], op=ALU.mult), reads=['mean'], writes=['msq'])
            S.op('dve', I('scalar_tensor_tensor', stt[:, 4, :], in0=stt[:, 1, :], scalar=1.0 / 128, in1=stt[:, 3, :],
                          op0=ALU.mult, op1=ALU.subtract), reads=['sumsq', 'msq'], writes=['var'])
            S.op('act', I('activation', stt[:, 5, :], stt[:, 4, :], AF.Sqrt, bias=C.epsc[:, 0:1]), reads=['var'], writes=['sd'])
            S.op('dve', I('reciprocal', stt[:, 6, :], stt[:, 5, :]), reads=['sd'], writes=['rstd'])
            S.op('dve', I('tensor_tensor', xn[:], xs[:], stt[:, 2, :].unsqueeze(2).to_broadcast([128, 8, 128]), op=ALU.subtract),
                 reads=xk + ['mean'], writes=['xn'])
            S.op('dve', I('tensor_tensor', xn[:], xn[:], stt[:, 6, :].unsqueeze(2).to_broadcast([128, 8, 128]), op=ALU.mult),
                 reads=['xn', 'rstd'], writes=['xn'])
            S.op('dve', I('tensor_tensor', yb[:], xn[:], ga[b][:, c4, :].rearrange("p (h t) -> p h t", t=128), op=ALU.mult),
                 reads=['xn', ('ga', b)], writes=[('yb', n % 2)])

        def ytrans(n):
            sci, c4 = divmod(n, 4)
            b = sci % 2
            csl = slice(c4 * 128, (c4 + 1) * 128)
            yb = ybs[n % 2]
            S.ops('pe', [I('transpose', ps_yt[:, h * 128:(h + 1) * 128], yb[:, h, :], C.ident[:]) for h in range(8)],
                  reads=[('yb', n % 2)], writes=['psyt'])
            S.op('act', I('copy', yst[b][:, :, csl], ps_yt[:].rearrange("p (h t) -> p h t", t=128)), reads=['psyt'], writes=[('yst', b, c4)])
            if c4 == 3:
                S.dma('pool', yv[:, 0:8, sci * 512:(sci + 1) * 512], yst[b][:], reads=[('yst', b, k) for k in range(4)])

        loads(0)
        front(0)
        for n in range(NCH):
            if n % 4 == 0 and n // 4 + 1 < SEQ // 512:
                loads(n // 4 + 1)
            if n + 1 < NCH:
                front(n + 1)
            back(n)
            if n >= 1:
                ytrans(n - 1)
        ytrans(NCH - 1)
        S.sync()


DIL_R = (1, 4, 16)


def phase_C0(C):
    nc, S, D = C.nc, C.S, C.dsc
    with ExitStack() as st:
        T = lambda n, s, d=F32: st.enter_context(nc.sbuf_tensor(uniq(n), s, d))
        P = lambda n, s, d=F32: st.enter_context(nc.psum_tensor(uniq(n), s, d))
        EBT = T("c_ebt", [128, 24, 256], BF16)
        with ExitStack() as st2:
            bt = st2.enter_context(nc.sbuf_tensor(uniq("c_bt"), [128, 24, 256], F32))
            mk = st2.enter_context(nc.sbuf_tensor(uniq("c_mk"), [128, 256], F32))
            S.dma('sp', bt[:], C.din['biasT'], writes=['bt'])
            S.dma('sp', mk[:], C.din['mask2'], writes=['mk'])
            S.op('act', I('activation', bt[:], bt[:], AF.Exp), reads=['bt'], writes=['bt'])
            S.op('dve', I('tensor_tensor', EBT[:], bt[:], mk[:].unsqueeze(1).to_broadcast([128, 24, 256]), op=ALU.mult),
                 reads=['bt', 'mk'], writes=['ebt'])
            S.sync()
        kT = T("c_k", [128, SEQ], BF16)
        qz = T("c_q", [128, 2, SEQ], BF16)
        vp = [T("c_v%d" % i, [128, 32, 2, 64], BF16) for i in range(2)]
        onesb = T("c_ones", [128, 64], BF16)
        accn = T("c_an", [64, 2, SEQ])
        accd = T("c_ad", [64, 2, SEQ])
        NROT = 4
        pe_ = [T("c_pe%d" % i, [128, 2, 128], BF16) for i in range(NROT)]
        pt_ = [T("c_pt%d" % i, [128, 2, 128], BF16) for i in range(NROT)]
        rden = [T("c_rd%d" % i, [64, 512]) for i in range(2)]
        ystg = [T("c_ys%d" % i, [64, 512], BF16) for i in range(2)]
        ps = [P("c_ps%d" % i, [128, 256]) for i in range(NROT)]
        po = [P("c_po%d" % i, [64, 512]) for i in range(2)]
        pd = [P("c_pd%d" % i, [64, 512]) for i in range(2)]
        S.op('dve', I('memset', qz[:], 0.0), writes=['qz'])
        S.op('dve', I('memset', onesb[:], 1.0), writes=['onesb'])
        qv = D['qbT'].rearrange("(g two d) t -> g two d t", two=2, d=64)
        kv = D['kbT'].rearrange("(g p) t -> g p t", p=128)
        st_ = dict(cnt=0, bcnt=0, vcnt=0)
        DEPTH = 2

        def sl(start, r):
            return slice(start, start + 127 * r + 1, r)

        for g in range(4):
            S.dma('sp', kT[:], kv[g], writes=['kT'])
            for half in range(2):
                S.dma('sp', qz[64 * half:64 * half + 64, half, :], qv[g, half], reads=['qz'], writes=[('qz', half)])
            vinfo = {}

            def issue_v(bi, g=g, vinfo=vinfo):
                r = DIL_R[bi]
                nb = SEQ // (128 * r)
                vb_ = vp[st_['vcnt'] % 2]
                vkey = ('vp', st_['vcnt'] % 2)
                st_['vcnt'] += 1
                vsrc = D['vb'].rearrange("(n a r) (g2 hh d) -> a r n g2 hh d", a=128, r=r, hh=2, d=64)
                vkeys = []
                for rr in range(r):
                    step = 8 if nb > 8 else nb
                    for n0 in range(0, nb, step):
                        S.dma('sp', vb_[:, rr * nb + n0:rr * nb + n0 + step, :, :], vsrc[:, rr, n0:n0 + step, g, :, :],
                              writes=[(vkey, rr, n0)])
                        vkeys.append((vkey, rr, n0))
                vinfo[bi] = (vb_, vkeys)

            tiles = []
            for bi, r in enumerate(DIL_R):
                nb = SEQ // (128 * r)
                first = True
                for hh in range(2):
                    if r == 1:
                        batches = [[(0, n) for n in range(n0, n0 + 4)] for n0 in range(0, nb, 4)]
                    else:
                        batches = [[(rho, n) for rho in range(r0, r0 + 4)] for n in range(nb) for r0 in range(0, r, 4)]
                    for batch in batches:
                        bb = st_['bcnt'] % 2
                        st_['bcnt'] += 1
                        for slot, (rho, n) in enumerate(batch):
                            tiles.append(dict(bi=bi, r=r, nb=nb, hh=hh, bb=bb, slot=slot, rho=rho, n=n, batch=batch,
                                              last=(slot == len(batch) - 1), first_of_branch=first))
                            first = False

            def front(t, g=g):
                r, n, rho, hh = t['r'], t['n'], t['rho'], t['hh']
                h = 2 * g + hh
                pi = ei = st_['cnt'] % NROT
                st_['cnt'] += 1
                t['ei'] = ei
                nk = 1 if n == 0 else 2
                t['nk'] = nk
                qsl = sl(128 * n * r + rho, r)
                fns = [I('matmul', ps[pi][:, 0:128], lhsT=kT[:, qsl], rhs=qz[:, hh, qsl], start=True, stop=True)]
                if nk == 2:
                    ksl = sl(128 * (n - 1) * r + rho, r)
                    fns.append(I('matmul', ps[pi][:, 128:256], lhsT=kT[:, ksl], rhs=qz[:, hh, qsl], start=True, stop=True))
                S.ops('pe', fns, reads=['kT', ('qz', 0), ('qz', 1)], writes=[('ps', pi)])
                S.op('act', I('activation', pe_[ei][:, 0:nk, :], ps[pi][:, 0:nk * 128].rearrange("p (s q) -> p s q", q=128), AF.Exp),
                     reads=[('ps', pi)], writes=[('pe', ei)])
                S.op('dve', I('tensor_tensor', pt_[ei][:, 0:nk, :], pe_[ei][:, 0:nk, :],
                              EBT[:, t['bi'] * 8 + h, 0:nk * 128].rearrange("p (s q) -> p s q", q=128), op=ALU.mult),
                     reads=[('pe', ei), 'ebt'], writes=[('pt', ei)])

            def back(t, g=g, vinfo=vinfo):
                r, n, rho, hh, bb, slot, nb, bi = t['r'], t['n'], t['rho'], t['hh'], t['bb'], t['slot'], t['nb'], t['bi']
                ei, nk = t['ei'], t['nk']
                vb_, vkeys = vinfo[bi]
                ti = rho * nb + n
                osl = slice(slot * 128, (slot + 1) * 128)
                fo = [I('matmul', po[bb][:, osl], lhsT=vb_[:, ti, hh, :], rhs=pt_[ei][:, 0, :], start=True, stop=(nk == 1))]
                fd = [I('matmul', pd[bb][:, osl], lhsT=onesb[:], rhs=pt_[ei][:, 0, :], start=True, stop=(nk == 1))]
                if nk == 2:
                    fo.append(I('matmul', po[bb][:, osl], lhsT=vb_[:, ti - 1, hh, :], rhs=pt_[ei][:, 1, :], start=False, stop=True))
                    fd.append(I('matmul', pd[bb][:, osl], lhsT=onesb[:], rhs=pt_[ei][:, 1, :], start=False, stop=True))
                S.ops('pe', fo + fd, reads=[('pt', ei), 'onesb'] + vkeys, writes=[('po', bb), ('pd', bb)])
                if not t['last']:
                    return
                rho0, n0 = t['batch'][0]
                if r == 1:
                    tok0 = 128 * n0
                    dn = accn[:, hh, tok0:tok0 + 512]
                    dd = accd[:, hh, tok0:tok0 + 512]
                    sn, sd = po[bb][:], pd[bb][:]
                else:
                    base = 128 * n0 * r
                    dn = accn[:, hh, base:base + 128 * r].rearrange("p (a r) -> p a r", r=r)[:, :, rho0:rho0 + 4]
                    dd = accd[:, hh, base:base + 128 * r].rearrange("p (a r) -> p a r", r=r)[:, :, rho0:rho0 + 4]
                    sn = po[bb][:].rearrange("p (s a) -> p a s", a=128)
                    sd = pd[bb][:].rearrange("p (s a) -> p a s", a=128)
                if bi == 0:
                    S.op('dve', I('tensor_copy', dn, sn), reads=[('po', bb)], writes=[('accn', hh)])
                    S.op('dve', I('tensor_copy', dd, sd), reads=[('pd', bb)], writes=[('accd', hh)])
                else:
                    S.op('dve', I('tensor_tensor', dn, sn, dn, op=ALU.add), reads=[('po', bb), ('accn', hh)], writes=[('accn', hh)])
                    S.op('dve', I('tensor_tensor', dd, sd, dd, op=ALU.add), reads=[('pd', bb), ('accd', hh)], writes=[('accd', hh)])

            issue_v(0)
            for i in range(len(tiles) + DEPTH):
                if i < len(tiles):
                    front(tiles[i])
                if i - DEPTH >= 0:
                    tb = tiles[i - DEPTH]
                    back(tb)
                    if tb['first_of_branch'] and tb['bi'] + 1 < len(DIL_R):
                        issue_v(tb['bi'] + 1)
            for hh in range(2):
                h = 2 * g + hh
                for tt in range(8):
                    i = tt % 2
                    tsl = slice(tt * 512, (tt + 1) * 512)
                    S.op('dve', I('reciprocal', rden[i][:], accd[:, hh, tsl]), reads=[('accd', hh)], writes=[('rden', i)])
                    S.op('dve', I('tensor_tensor', ystg[i][:], accn[:, hh, tsl], rden[i][:], op=ALU.mult),
                         reads=[('accn', hh), ('rden', i)], writes=[('ystg', i)])
                    S.dma('pool', D['yT'][1024 + 64 * h:1024 + 64 * h + 64, tsl], ystg[i][:], reads=[('ystg', i)])
        S.sync()


def phase_A1(C, hT):
    nc, S, D = C.nc, C.S, C.dsc
    with ExitStack() as st:
        T = lambda n, s, d=F32: st.enter_context(nc.sbuf_tensor(uniq(n), s, d))
        hnT = T("a1_hn", [128, 8, SEQ], BF16)
        norm_pass(C, hT, 2, hnT, 0, 8)
        lbr = T("a1_lbr", [128, 1024])
        omr = T("a1_omr", [128, 1024])
        hgn = T("a1_hgn", [128, 1024])
        lbf = T("a1_lbf", [128, 8])
        omf = T("a1_omf", [128, 8])
        with ExitStack() as st2:
            raw = st2.enter_context(nc.sbuf_tensor(uniq("a1_raw"), [128, 2, 1024], F32))
            rawf = st2.enter_context(nc.sbuf_tensor(uniq("a1_rawf"), [128, 2, 8], F32))
            S.dma('sp', raw[:], C.din['lbrep'], writes=['raw'])
            S.dma('sp', rawf[:], C.din['lbfm'], writes=['rawf'])
            S.dma('sp', hgn[:], C.din['hgn'], writes=['hgn'])
            S.op('dve', I('tensor_tensor', lbr[:], raw[:, 1, :], raw[:, 0, :], op=ALU.subtract), reads=['raw'], writes=['lbr'])
            S.op('act', I('activation', lbr[:], lbr[:], AF.Sigmoid), reads=['lbr'], writes=['lbr'])
            S.op('dve', I('tensor_scalar', omr[:], lbr[:], -1.0, 1.0, op0=ALU.mult, op1=ALU.add), reads=['lbr'], writes=['omr'])
            S.op('dve', I('tensor_tensor', lbf[:], rawf[:, 1, :], rawf[:, 0, :], op=ALU.subtract), reads=['rawf'], writes=['lbf'])
            S.op('act', I('activation', lbf[:], lbf[:], AF.Sigmoid), reads=['lbf'], writes=['lbf'])
            S.op('dve', I('tensor_scalar', omf[:], lbf[:], -1.0, 1.0, op0=ALU.mult, op1=ALU.add), reads=['lbf'], writes=['omf'])
            S.sync()
        sb = Stager(C, st, "a1_sb", [128, 512], BF16, 6)
        sf = Stager(C, st, "a1_sf", [128, 512], F32, 3)
        t1 = [T("a1_t1%d" % i, [128, 512]) for i in range(2)]
        t2 = [T("a1_t2%d" % i, [128, 512]) for i in range(2)]
        cnt = [0]

        def epi_fm_silu(dst, cb):
            def f(sub, tt, ps, pk):
                buf, bk = sb.next()
                S.op('act', I('activation', buf[:], ps[:], AF.Silu), reads=[pk], writes=[bk])
                sb.store(dst[cb * 512 + sub * 128:cb * 512 + (sub + 1) * 128, tt * 512:(tt + 1) * 512], buf[:], bk)
            return f

        def epi_fm_k(dst, cb):
            def f(sub, tt, ps, pk):
                i = cnt[0] % 2
                cnt[0] += 1
                ci = cb * 4 + sub
                S.op('act', I('activation', t1[i][:], ps[:], AF.Sigmoid, scale=-1.0), reads=[pk], writes=[('t1', i)])
                buf, bk = sb.next()
                S.op('dve', I('tensor_scalar', buf[:], t1[i][:], omf[:, ci:ci + 1], None, op0=ALU.mult), reads=[('t1', i)], writes=[bk])
                sb.store(dst[ci * 128:(ci + 1) * 128, tt * 512:(tt + 1) * 512], buf[:], bk)
            return f

        def epi_tm_f(cb):
            def f(t, ps, pk):
                i = cnt[0] % 2
                cnt[0] += 1
                csl = slice(cb * 512, (cb + 1) * 512)
                rsl = slice(t * 128, (t + 1) * 128)
                S.op('act', I('activation', t1[i][:], ps[:], AF.Sigmoid), reads=[pk], writes=[('t1', i)])
                S.op('dve', I('tensor_tensor', t2[i][:], t1[i][:], omr[:, csl], op=ALU.mult), reads=[('t1', i)], writes=[('t2', i)])
                S.op('dve', I('tensor_tensor', t2[i][:], t2[i][:], lbr[:, csl], op=ALU.add), reads=[('t2', i)], writes=[('t2', i)])
                fb, fk = sf.next()
                S.op('act', I('activation', fb[:], t2[i][:], AF.Ln), reads=[('t2', i)], writes=[fk])
                hb_, hk_ = sb.next()
                S.op('dve', I('tensor_copy', hb_[:], fb[:]), reads=[fk], writes=[hk_])
                sb.store(D['lfh'][rsl, csl], hb_[:], hk_)
                lb_, lk_ = sb.next()
                S.op('dve', I('tensor_tensor', lb_[:], fb[:], hb_[:], op=ALU.subtract), reads=[fk, hk_], writes=[lk_])
                sb.store(D['lfl'][rsl, csl], lb_[:], lk_)
                buf, bk = sb.next()
                S.op('dve', I('tensor_scalar', buf[:], t2[i][:], -1.0, 1.0, op0=ALU.mult, op1=ALU.add), reads=[('t2', i)], writes=[bk])
                sb.store(D['k1'][rsl, csl], buf[:], bk)
            return f

        def epi_tm_copy(dst, cb):
            def f(t, ps, pk):
                buf, bk = sb.next()
                S.op('act', I('copy', buf[:], ps[:]), reads=[pk], writes=[bk])
                sb.store(dst[t * 128:(t + 1) * 128, cb * 512:(cb + 1) * 512], buf[:], bk)
            return f

        def epi_tm_gate(dst, cb):
            def f(t, ps, pk):
                i = cnt[0] % 2
                cnt[0] += 1
                S.op('act', I('activation', t1[i][:], ps[:], AF.Silu), reads=[pk], writes=[('t1', i)])
                buf, bk = sb.next()
                S.op('dve', I('tensor_tensor', buf[:], t1[i][:], hgn[:, cb * 512:(cb + 1) * 512], op=ALU.mult),
                     reads=[('t1', i)], writes=[bk])
                sb.store(dst[t * 128:(t + 1) * 128, cb * 512:(cb + 1) * 512], buf[:], bk)
            return f
        jobs = []
        for cb in range(2):
            jobs.append(dict(c0=cb * 512, role='fm', epi_fm=epi_fm_silu(D['q1T'], cb)))
        for cb in range(2):
            jobs.append(dict(c0=1024 + cb * 512, role='both', epi_fm=epi_fm_k(D['k1T'], cb), epi_tm=epi_tm_f(cb)))
        for cb in range(2):
            jobs.append(dict(c0=2048 + cb * 512, role='tm', epi_tm=epi_tm_copy(D['v1'], cb)))
        for cb in range(2):
            jobs.append(dict(c0=3072 + cb * 512, role='tm', epi_tm=epi_tm_gate(D['g1'], cb)))
        proj_phase(C, hnT, C.din['w_in1'], jobs)


def phase_B1(C):
    nc, S, D = C.nc, C.S, C.dsc
    with ExitStack() as st:
        T = lambda n, s, d=F32: st.enter_context(nc.sbuf_tensor(uniq(n), s, d))
        P = lambda n, s, d=F32: st.enter_context(nc.psum_tensor(uniq(n), s, d))
        T1 = T("h_t1", [128, 128], BF16)
        T2 = T("h_t2", [128, 128], BF16)
        S.dma('sp', T1[:], C.din['t1m'], writes=['T1'])
        S.dma('sp', T2[:], C.din['t2m'], writes=['T2'])
        NSB = 256
        qT = [T("h_q%d" % i, [128, 8, NSB], BF16) for i in range(2)]
        kT = [T("h_k%d" % i, [128, 8, NSB], BF16) for i in range(2)]
        lf = [T("h_lf%d" % i, [128, 2, 2, 1024], BF16) for i in range(2)]
        kk = [T("h_kk%d" % i, [128, 2, 1024], BF16) for i in range(2)]
        vv = [T("h_v%d" % i, [128, 2, 1024], BF16) for i in range(2)]
        gg = [T("h_g%d" % i, [128, 2, 1024], BF16) for i in range(2)]
        eB = T("h_eB", [128, 8, 128])
        eNB = T("h_eNB", [128, 8, 128])
        eRB = T("h_eRB", [128, 1024])
        qlo = [T("h_qlo%d" % i, [128, 8, 128], BF16) for i in range(2)]
        qhi = [T("h_qhi%d" % i, [128, 8, 128], BF16) for i in range(2)]
        kt = [T("h_kt%d" % i, [128, 8, 128], BF16) for i in range(2)]
        klo = [T("h_klo%d" % i, [128, 1024], BF16) for i in range(2)]
        khi = [T("h_khi%d" % i, [128, 1024], BF16) for i in range(2)]
        sc = [T("h_sc%d" % i, [128, 8, 128], BF16) for i in range(2)]
        Sf = T("h_S", [128, 8, 128])
        Sbp = [T("h_Sbp%d" % i, [128, 8, 128], BF16) for i in range(2)]
        Sbm = [T("h_Sbm%d" % i, [128, 8, 128], BF16) for i in range(2)]
        sq = T("h_sq", [128, 8, 128])
        xn = T("h_xn", [128, 8, 128])
        yb = T("h_yb", [128, 8, 128], BF16)
        stt = T("h_stt", [128, 3, 8])
        yst = [T("h_yst%d" % i, [128, 8, NSB], BF16) for i in range(2)]
        bA = [P("h_pA%d" % i, [128, 512]) for i in range(2)]
        bB = [P("h_pB%d" % i, [128, 512]) for i in range(2)]
        bC = [P("h_pC%d" % i, [128, 512]) for i in range(2)]
        bD = P("h_pD", [128, 1024], BF16)
        for i in range(2):
            S.op('dve', I('memset', qlo[i][:], 0.0), writes=[('qlo', i)])
            S.op('dve', I('memset', qhi[i][:], 0.0), writes=[('qhi', i)])
            S.op('dve', I('memset', klo[i][:], 0.0), writes=[('klo', i)])
            S.op('dve', I('memset', khi[i][:], 0.0), writes=[('khi', i)])
        S.op('dve', I('memset', Sf[:], 0.0), writes=[('Sf', h) for h in range(8)])
        S.op('dve', I('memset', Sbp[0][:], 0.0), writes=[('Sbp', 0, h) for h in range(8)])
        qv = D['q1T'].rearrange("(h p) t -> p h t", p=128)
        kv = D['k1T'].rearrange("(h p) t -> p h t", p=128)
        lvh = D['lfh'].rearrange("(c p) e -> p c e", p=128)
        lvl = D['lfl'].rearrange("(c p) e -> p c e", p=128)
        k2v = D['k1'].rearrange("(c p) e -> p c e", p=128)
        vv_ = D['v1'].rearrange("(c p) e -> p c e", p=128)
        gv = D['g1'].rearrange("(c p) e -> p c e", p=128)
        yv = D['y1T'].rearrange("(h p) t -> p h t", p=128)
        for sbi_ in range(SEQ // NSB):
            b = sbi_ % 2
            tsl = slice(sbi_ * NSB, (sbi_ + 1) * NSB)
            csl2 = slice(sbi_ * 2, sbi_ * 2 + 2)
            S.dma('sp', qT[b][:], qv[:, :, tsl], writes=[('qT', b)])
            S.dma('sp', kT[b][:], kv[:, :, tsl], writes=[('kT', b)])
            S.dma('sp', lf[b][:, 0], lvh[:, csl2, :], writes=[('lf', b, 0)])
            S.dma('sp', lf[b][:, 1], lvl[:, csl2, :], writes=[('lf', b, 1)])
            S.dma('sp', kk[b][:], k2v[:, csl2, :], writes=[('kk', b)])
            S.dma('sp', vv[b][:], vv_[:, csl2, :], writes=[('vv', b)])
            S.dma('sp', gg[b][:], gv[:, csl2, :], writes=[('gg', b)])
            for blk in range(2):
                n = sbi_ * 2 + blk
                p2 = n % 2
                bsl = slice(blk * 128, (blk + 1) * 128)
                for i in range(2):
                    lfk = [('lf', b, 0), ('lf', b, 1)]
                    S.ops('pe', [I('matmul', bA[i][:, (h % 4) * 128:(h % 4 + 1) * 128], lhsT=lf[b][:, hl, blk, h * 128:(h + 1) * 128], rhs=T1[:],
                                   start=(hl == 0), stop=(hl == 1)) for h in range(4 * i, 4 * i + 4) for hl in range(2)],
                          reads=lfk + ['T1'], writes=[('bA', i)])
                    S.ops('pe', [I('matmul', bB[i][:], lhsT=T2[:], rhs=lf[b][:, hl, blk, i * 512:(i + 1) * 512], start=(hl == 0), stop=(hl == 1))
                                 for hl in range(2)], reads=lfk + ['T2'], writes=[('bB', i)])
                    S.op('act', I('activation', eB[:, 4 * i:4 * i + 4, :], bA[i][:].rearrange("p (h t) -> p h t", t=128), AF.Exp),
                         reads=[('bA', i)], writes=[('eB', i)])
                    S.op('act', I('activation', eNB[:, 4 * i:4 * i + 4, :], bA[i][:].rearrange("p (h t) -> p h t", t=128), AF.Exp, scale=-1.0),
                         reads=[('bA', i)], writes=[('eNB', i)])
                    S.op('act', I('activation', eRB[:, i * 512:(i + 1) * 512], bB[i][:], AF.Exp), reads=[('bB', i)], writes=[('eRB', i)])
                ek = [('eB', 0), ('eB', 1)]
                S.op('dve', I('tensor_tensor', qlo[p2][:, :, 0:64], qT[b][:, :, blk * 128:blk * 128 + 64], eB[:, :, 0:64], op=ALU.mult),
                     reads=ek + [('qT', b), ('qlo', p2)], writes=[('qlo', p2)])
                S.op('dve', I('tensor_tensor', qhi[p2][:, :, 64:128], qT[b][:, :, blk * 128 + 64:blk * 128 + 128], eB[:, :, 64:128], op=ALU.mult),
                     reads=ek + [('qT', b), ('qhi', p2)], writes=[('qhi', p2)])
                S.op('dve', I('tensor_tensor', kt[p2][:], kT[b][:, :, bsl], eNB[:], op=ALU.mult),
                     reads=[('eNB', 0), ('eNB', 1), ('kT', b)], writes=[('kt', p2)])
                S.op('dve', I('tensor_tensor', klo[p2][0:64, :], kk[b][0:64, blk, :], eRB[0:64, :], op=ALU.mult),
                     reads=[('eRB', 0), ('eRB', 1), ('kk', b), ('klo', p2)], writes=[('klo', p2)])
                S.op('dve', I('tensor_tensor', khi[p2][64:128, :], kk[b][64:128, blk, :], eRB[64:128, :], op=ALU.mult),
                     reads=[('eRB', 0), ('eRB', 1), ('kk', b), ('khi', p2)], writes=[('khi', p2)])
                for i in range(2):
                    fns = []
                    for h in range(4 * i, 4 * i + 4):
                        o0 = (h % 4) * 128
                        fns.append(I('matmul', bA[i][:, o0:o0 + 64], lhsT=kt[p2][:, h, :], rhs=qlo[p2][:, h, 0:64], start=True, stop=True))
                        fns.append(I('matmul', bA[i][:, o0 + 64:o0 + 128], lhsT=kt[p2][:, h, :], rhs=qhi[p2][:, h, 64:128], start=True, stop=True))
                    S.ops('pe', fns, reads=[('kt', p2), ('qlo', p2), ('qhi', p2), ('eB', i), ('eNB', i)], writes=[('bA', i)])
                    S.op('dve', I('tensor_tensor', sc[p2][:, 4 * i:4 * i + 4, :], bA[i][:].rearrange("p (h t) -> p h t", t=128),
                                  T1[:].unsqueeze(1).to_broadcast([128, 4, 128]), op=ALU.mult), reads=[('bA', i), 'T1'], writes=[('sc', p2, i)])
                for half, (ksrc, kkey, dst_) in enumerate(((klo[p2], ('klo', p2), Sbm[p2]), (khi[p2], ('khi', p2), Sbp[1 - p2]))):
                    dkey = 'Sbm' if half == 0 else 'Sbp'
                    dpar = p2 if half == 0 else 1 - p2
                    col = 63 if half == 0 else 127
                    for i in range(2):
                        S.ops('pe', [I('matmul', bC[i][:, (h % 4) * 128:(h % 4 + 1) * 128], lhsT=ksrc[:, h * 128:(h + 1) * 128],
                                       rhs=vv[b][:, blk, h * 128:(h + 1) * 128], start=True, stop=True) for h in range(4 * i, 4 * i + 4)],
                              reads=[kkey, ('vv', b)], writes=[('bC', i)])
                        for h in range(4 * i, 4 * i + 4):
                            S.op('dve', I('scalar_tensor_tensor', Sf[:, h, :], in0=Sf[:, h, :], scalar=eB[:, h, col:col + 1],
                                          in1=bC[i][:, (h % 4) * 128:(h % 4 + 1) * 128], op0=ALU.mult, op1=ALU.add),
                                 reads=[('bC', i), ('eB', i), ('Sf', h)], writes=[('Sf', h)])
                        S.op('pool', I('tensor_copy', dst_[:, 4 * i:4 * i + 4, :], Sf[:, 4 * i:4 * i + 4, :]),
                             reads=[('Sf', h) for h in range(4 * i, 4 * i + 4)], writes=[(dkey, dpar, h) for h in range(4 * i, 4 * i + 4)])
                for i in range(2):
                    fns = []
                    for h in range(4 * i, 4 * i + 4):
                        osl = slice((h % 4) * 128, (h % 4 + 1) * 128)
                        fns.append(I('matmul', bB[i][:, osl], lhsT=sc[p2][:, h, :], rhs=vv[b][:, blk, h * 128:(h + 1) * 128], start=True, stop=False))
                        fns.append(I('matmul', bB[i][:, osl], lhsT=qlo[p2][:, h, :], rhs=Sbp[p2][:, h, :], start=False, stop=False))
                        fns.append(I('matmul', bB[i][:, osl], lhsT=qhi[p2][:, h, :], rhs=Sbm[p2][:, h, :], start=False, stop=True))
                    S.ops('pe', fns, reads=[('sc', p2, i), ('vv', b), ('qlo', p2), ('qhi', p2), ('eRB', i)]
                          + [('Sbp', p2, h) for h in range(4 * i, 4 * i + 4)] + [('Sbm', p2, h) for h in range(4 * i, 4 * i + 4)],
                          writes=[('bB', i)])
                    S.op('act', I('activation', sq[:, 4 * i:4 * i + 4, :], bB[i][:].rearrange("p (h t) -> p h t", t=128), AF.Square),
                         reads=[('bB', i)], writes=[('sq', i)])
                S.op('dve', I('tensor_reduce', stt[:, 0, :], sq[:], axis=AX.X, op=ALU.add), reads=[('sq', 0), ('sq', 1)], writes=['ss'])
                S.op('act', I('activation', stt[:, 1, :], stt[:, 0, :], AF.Sqrt, scale=1.0 / 128, bias=C.epsc[:, 0:1]), reads=['ss'], writes=['sd'])
                S.op('dve', I('reciprocal', stt[:, 2, :], stt[:, 1, :]), reads=['sd'], writes=['rstd'])
                for i in range(2):
                    S.op('dve', I('tensor_tensor', xn[:, 4 * i:4 * i + 4, :], bB[i][:].rearrange("p (h t) -> p h t", t=128),
                                  stt[:, 2, 4 * i:4 * i + 4].unsqueeze(2).to_broadcast([128, 4, 128]), op=ALU.mult),
                         reads=[('bB', i), 'rstd'], writes=[('xn', i)])
                S.op('dve', I('tensor_tensor', yb[:], xn[:], gg[b][:, blk, :].rearrange("p (h t) -> p h t", t=128), op=ALU.mult),
                     reads=[('xn', 0), ('xn', 1), ('gg', b)], writes=['yb'])
                S.ops('pe', [I('transpose', bD[:, h * 128:(h + 1) * 128], yb[:, h, :], C.ident[:]) for h in range(8)], reads=['yb'], writes=['bD'])
                S.op('act', I('copy', yst[b][:, :, bsl], bD[:].rearrange("p (h t) -> p h t", t=128)), reads=['bD'], writes=[('yst', b, blk)])
            S.dma('pool', yv[:, :, tsl], yst[b][:], reads=[('yst', b, 0), ('yst', b, 1)])
        S.sync()
```
